# Optimizing a Trainium2 kernel written in Bass

```python
import math
import jax, jax.numpy as jnp
from jax import lax
import numpy as np

D_MODEL = 2048
BATCH = 4
SEQ = 2048
DEPTH = 4
DEC_BATCH = 32
DEC_SEQ = 32
PAST_LEN = 4096

CHUNK = 64
D_MIX = D_MODEL
W_GROUP = D_MIX // 4
W_A = W_GROUP
P_A = 64
H_A = W_A // P_A
N_A = 128
G_A = 2
K_A = 4
CONV_DIM_A = W_A + 2 * G_A * N_A
W_B = W_GROUP
D_B = 64
H_B = W_B // D_B
Q_BLOCK = 128
W_C = W_GROUP
H_C = 4
DH_C = W_C // H_C
K_C = 4
W_D = D_MIX - W_A - W_B - W_C
K_D = 31
SPLIT_SIZES = (W_A, CONV_DIM_A, H_A, W_B, W_B, W_B, W_B, W_C, W_C, H_C, H_C, W_D, W_D, W_D)
IN_COLS = sum(SPLIT_SIZES)
ALPHA = (2 * DEPTH) ** 0.25
BETA = (8 * DEPTH) ** -0.25
EPS = 1e-5

kernel_name = "hybrid_streaming_encoder_step"


def layer_norm(x):
    x32 = x.astype(jnp.float32)
    mu = jnp.mean(x32, -1, keepdims=True)
    var = jnp.mean(jnp.square(x32 - mu), -1, keepdims=True)
    return (x32 - mu) * lax.rsqrt(var + EPS)


def rms_norm(x, w):
    x32 = x.astype(jnp.float32)
    return x32 * lax.rsqrt(jnp.mean(x32 * x32, -1, keepdims=True) + EPS) * w


def chunk_len(L):
    return CHUNK if L % CHUNK == 0 else L


def split_cols(t):
    parts = []
    start = 0
    for s in SPLIT_SIZES:
        parts.append(t[..., start:start + s])
        start += s
    return parts


def causal_dwconv(x, buf, w, b):
    xp = jnp.concatenate([buf.astype(x.dtype), x], axis=1)
    y = lax.conv_general_dilated(xp, w[:, None, :].astype(x.dtype), window_strides=(1,), padding='VALID',
                                 dimension_numbers=('NWC', 'WIO', 'NWC'), feature_group_count=x.shape[-1])
    return y + b, xp[:, xp.shape[1] - (w.shape[0] - 1):]


def to_chunks(t, nc, q):
    return jnp.moveaxis(t.reshape((t.shape[0], nc, q) + t.shape[2:]), 1, 0)


def ssd_scan(x, dt, a, bm, cm, h0, q):
    bsz, L, H, P = x.shape
    nc = L // q
    causal = jnp.tril(jnp.ones((q, q), bool))[None, :, :, None]

    def step(h, inp):
        xc, dtc, bc, cc = inp
        acum = jnp.cumsum(dtc * a, axis=1)
        seg = acum[:, :, None, :] - acum[:, None, :, :]
        decay = jnp.exp(jnp.where(causal, seg, -jnp.inf))
        xdt = xc * dtc[..., None]
        y = jnp.einsum('bthn,bshn,btsh,bshp->bthp', cc, bc, decay, xdt)
        y = y + jnp.einsum('bthn,bhpn,bth->bthp', cc, h, jnp.exp(acum))
        last = acum[:, -1]
        w_s = jnp.exp(last[:, None] - acum)
        h_new = jnp.exp(last)[..., None, None] * h + jnp.einsum('bshn,bsh,bshp->bhpn', bc, w_s, xdt)
        return h_new, y

    xs = (to_chunks(x, nc, q), to_chunks(dt, nc, q), to_chunks(bm, nc, q), to_chunks(cm, nc, q))
    h_last, ys = lax.scan(step, h0.astype(jnp.float32), xs)
    return jnp.moveaxis(ys, 0, 1).reshape(bsz, L, H, P), h_last


def stick_breaking_attention(q, k, v, q_pos, k_pos):
    bsz, Lq, H, D = q.shape
    qb = Q_BLOCK if Lq % Q_BLOCK == 0 else Lq
    nb = Lq // qb
    qs = jnp.moveaxis(q.reshape(bsz, nb, qb, H, D), 1, 0)
    ps = q_pos.reshape(nb, qb)
    scale = D ** -0.5
    v32 = v.astype(jnp.float32)

    def block(args):
        qblk, pblk = args
        z = jnp.einsum('bqhd,bkhd->bhqk', qblk, k).astype(jnp.float32) * scale
        mask = (k_pos[None, :] < pblk[:, None])[None, None]
        log_keep = jnp.where(mask, jax.nn.log_sigmoid(-z), 0.0)
        suffix = lax.cumsum(log_keep, axis=3, reverse=True) - log_keep
        w = jnp.where(mask, jnp.exp(jax.nn.log_sigmoid(z) + suffix), 0.0)
        return jnp.einsum('bhqk,bkhd->bqhd', w, v32)

    out = lax.map(block, (qs, ps))
    return jnp.moveaxis(out, 0, 1).reshape(bsz, Lq, H, D)


def mlstm_scan(q, k, v, ipre, logf, c0, n0, m0, chunk):
    bsz, L, H, Dk = q.shape
    nc = L // chunk
    causal = jnp.tril(jnp.ones((chunk, chunk), bool))[None, :, :, None]

    def step(carry, inp):
        cm, nm, mm = carry
        qc, kc, vc, ic, fc = inp
        b = jnp.cumsum(fc, axis=1)
        d = jnp.where(causal, b[:, :, None] - b[:, None] + ic[:, None], -jnp.inf)
        inter = b + mm[:, None]
        m_t = jnp.maximum(inter, jnp.max(d, axis=2))
        w = jnp.exp(d - m_t[:, :, None]) * jnp.einsum('bthd,bshd->btsh', qc, kc)
        g = jnp.exp(inter - m_t)
        num = jnp.einsum('btsh,bshv->bthv', w, vc) + g[..., None] * jnp.einsum('bhvd,bthd->bthv', cm, qc)
        nq = jnp.sum(w, axis=2) + g * jnp.einsum('bhd,bthd->bth', nm, qc)
        h = num / jnp.maximum(jnp.abs(nq), jnp.exp(-m_t))[..., None]
        m_new = m_t[:, -1]
        ws = jnp.exp(b[:, -1:] - b + ic - m_new[:, None])
        g_last = jnp.exp(b[:, -1] + mm - m_new)
        c_new = g_last[..., None, None] * cm + jnp.einsum('bsh,bshv,bshd->bhvd', ws, vc, kc)
        n_new = g_last[..., None] * nm + jnp.einsum('bsh,bshd->bhd', ws, kc)
        return (c_new, n_new, m_new), h

    xs = (to_chunks(q, nc, chunk), to_chunks(k, nc, chunk), to_chunks(v, nc, chunk),
          to_chunks(ipre, nc, chunk), to_chunks(logf, nc, chunk))
    carry0 = (c0.astype(jnp.float32), n0.astype(jnp.float32), m0.astype(jnp.float32))
    (c_f, n_f, m_f), hs = lax.scan(step, carry0, xs)
    return jnp.moveaxis(hs, 0, 1).reshape(bsz, L, H, -1), c_f, n_f, m_f


def mixer_layer(x, c, kv_past, conv_a_buf, ssm0, conv_c_buf, mc0, mn0, mm0, conv_d_buf,
                w_mod, b_mod, w_in, conv_a_w, conv_a_b, dt_bias, a_log, d_skip, norm_a_w,
                conv_c_w, conv_c_b, wq_c, wk_c, wv_c, ig_bias, fg_bias, norm_c_w, skip_c,
                conv_d_w, conv_d_b, ln_d_g, ln_d_b, w_out, ln_g, ln_b):
    bsz, L, _ = x.shape
    ql = chunk_len(L)
    shift, scale, gate = jnp.split(c.astype(jnp.float32) @ w_mod + b_mod, 3, axis=-1)
    u = layer_norm(x) * (1.0 + scale[:, None]) + shift[:, None]
    (z_a, xbc_a, dt_a, q_b, k_b, v_b, g_b, x_c, z_c, i_c, f_c, a_d, b_d, g_d) = split_cols(u @ w_in)

    xbc_a, conv_a_new = causal_dwconv(xbc_a, conv_a_buf, conv_a_w, conv_a_b)
    xbc_a = jax.nn.silu(xbc_a)
    xs_a = xbc_a[..., :W_A].reshape(bsz, L, H_A, P_A)
    bm_a = jnp.repeat(xbc_a[..., W_A:W_A + G_A * N_A].reshape(bsz, L, G_A, N_A), H_A // G_A, axis=2)
    cm_a = jnp.repeat(xbc_a[..., W_A + G_A * N_A:].reshape(bsz, L, G_A, N_A), H_A // G_A, axis=2)
    dt = jax.nn.softplus(dt_a + dt_bias)
    y_a, ssm_new = ssd_scan(xs_a, dt, -jnp.exp(a_log), bm_a, cm_a, ssm0, ql)
    y_a = (y_a + d_skip[:, None] * xs_a).reshape(bsz, L, W_A)
    y_a = rms_norm(y_a * jax.nn.silu(z_a), norm_a_w)

    q_b = q_b.reshape(bsz, L, H_B, D_B)
    k_b = k_b.reshape(bsz, L, H_B, D_B)
    v_b = v_b.reshape(bsz, L, H_B, D_B)
    if kv_past is None:
        past = 0
        k_all, v_all = k_b, v_b
    else:
        past = kv_past[0].shape[1]
        k_all = jnp.concatenate([kv_past[0], k_b], axis=1)
        v_all = jnp.concatenate([kv_past[1], v_b], axis=1)
    y_b = stick_breaking_attention(q_b, k_all, v_all, past + jnp.arange(L), jnp.arange(past + L))
    y_b = y_b.reshape(bsz, L, W_B) * jax.nn.silu(g_b)

    xconv, conv_c_new = causal_dwconv(x_c, conv_c_buf, conv_c_w, conv_c_b)
    xconv = jax.nn.silu(xconv)
    xh = xconv.reshape(bsz, L, H_C, DH_C)
    q_c = jnp.einsum('blhd,hde->blhe', xh, wq_c)
    k_c = jnp.einsum('blhd,hde->blhe', xh, wk_c) * DH_C ** -0.5
    v_c = jnp.einsum('blhd,hde->blhe', x_c.reshape(bsz, L, H_C, DH_C), wv_c)
    h_c, mc_new, mn_new, mm_new = mlstm_scan(q_c, k_c, v_c, i_c + ig_bias, jax.nn.log_sigmoid(f_c + fg_bias),
                                             mc0, mn0, mm0, ql)
    h_c = layer_norm(h_c) * norm_c_w.reshape(H_C, DH_C)
    y_c = (h_c.reshape(bsz, L, W_C) + skip_c * xconv) * jax.nn.silu(z_c)

    glu = a_d * jax.nn.sigmoid(b_d)
    y_d, conv_d_new = causal_dwconv(glu, conv_d_buf, conv_d_w, conv_d_b)
    y_d = jax.nn.silu(layer_norm(y_d) * ln_d_g + ln_d_b) * jax.nn.silu(g_d)

    o = jnp.concatenate([y_a, y_b, y_c, y_d], axis=-1) @ w_out
    x_new = layer_norm(ALPHA * x + (1.0 + gate[:, None]) * o) * ln_g + ln_b
    return x_new, (k_b, v_b, conv_a_new, ssm_new, conv_c_new, mc_new, mn_new, mm_new, conv_d_new)


def run_trunk(x, c, cache_k, cache_v, st_conv_a, st_ssm, st_conv_c, st_mc, st_mn, st_mm, st_conv_d, weights):
    outs = [[] for _ in range(9)]
    for l in range(DEPTH):
        kv = None if cache_k is None else (cache_k[l], cache_v[l])
        x, new = mixer_layer(x, c, kv, st_conv_a[l], st_ssm[l], st_conv_c[l], st_mc[l], st_mn[l], st_mm[l],
                             st_conv_d[l], *[w[l] for w in weights])
        for o, t in zip(outs, new):
            o.append(t)
    return x, [jnp.stack(o) for o in outs]


def setup_inputs(seed: int = 0) -> dict:
    key = jax.random.key(seed)
    ks = iter(jax.random.split(key, 48))

    def nrm(shape, s=1.0):
        return s * jax.random.normal(next(ks), shape, jnp.float32)

    L = DEPTH
    x_prompt = nrm((BATCH, SEQ, D_MODEL))
    x_sample = nrm((DEC_BATCH, DEC_SEQ, D_MODEL))
    cache_k = nrm((L, DEC_BATCH, PAST_LEN, H_B, D_B))
    cache_v = nrm((L, DEC_BATCH, PAST_LEN, H_B, D_B))
    state_conv_a = nrm((L, DEC_BATCH, K_A - 1, CONV_DIM_A))
    state_ssm = nrm((L, DEC_BATCH, H_A, P_A, N_A), 0.1)
    state_conv_c = nrm((L, DEC_BATCH, K_C - 1, W_C))
    state_mlstm_c = nrm((L, DEC_BATCH, H_C, DH_C, DH_C), 0.1)
    state_mlstm_n = nrm((L, DEC_BATCH, H_C, DH_C), 0.1)
    state_mlstm_m = nrm((L, DEC_BATCH, H_C))
    state_conv_d = nrm((L, DEC_BATCH, K_D - 1, W_D), 0.5)
    c_prompt = nrm((BATCH, D_MODEL))
    c_sample = nrm((DEC_BATCH, D_MODEL))
    w_mod = nrm((L, D_MODEL, 3 * D_MODEL), 0.2 * D_MODEL ** -0.5)
    b_mod = nrm((L, 3 * D_MODEL), 0.01)
    w_in = nrm((L, D_MODEL, IN_COLS), D_MODEL ** -0.5)
    conv_a_w = nrm((L, K_A, CONV_DIM_A), K_A ** -0.5)
    conv_a_b = nrm((L, CONV_DIM_A), 0.01)
    dt0 = jnp.exp(jax.random.uniform(next(ks), (L, H_A), jnp.float32, math.log(1e-3), math.log(1e-1)))
    dt_bias = dt0 + jnp.log(-jnp.expm1(-dt0))
    a_log = jnp.log(jax.random.uniform(next(ks), (L, H_A), jnp.float32, 1.0, 16.0))
    d_skip = 1.0 + nrm((L, H_A), 0.1)
    norm_a_w = 1.0 + nrm((L, W_A), 0.1)
    conv_c_w = nrm((L, K_C, W_C), K_C ** -0.5)
    conv_c_b = nrm((L, W_C), 0.01)
    wq_c = nrm((L, H_C, DH_C, DH_C), DH_C ** -0.5)
    wk_c = nrm((L, H_C, DH_C, DH_C), DH_C ** -0.5)
    wv_c = nrm((L, H_C, DH_C, DH_C), DH_C ** -0.5)
    ig_bias = nrm((L, H_C), 0.1)
    fg_bias = jnp.linspace(3.0, 6.0, H_C, dtype=jnp.float32) + nrm((L, H_C), 0.1)
    norm_c_w = 1.0 + nrm((L, W_C), 0.1)
    skip_c = 1.0 + nrm((L, W_C), 0.1)
    conv_d_w = nrm((L, K_D, W_D), K_D ** -0.5)
    conv_d_b = nrm((L, W_D), 0.01)
    ln_d_g = 1.0 + nrm((L, W_D), 0.1)
    ln_d_b = nrm((L, W_D), 0.01)
    w_out = nrm((L, D_MIX, D_MODEL), BETA * D_MIX ** -0.5)
    ln_g = 1.0 + nrm((L, D_MODEL), 0.1)
    ln_b = nrm((L, D_MODEL), 0.01)
    return {"x_prompt": x_prompt, "x_sample": x_sample, "cache_k": cache_k, "cache_v": cache_v,
            "state_conv_a": state_conv_a, "state_ssm": state_ssm, "state_conv_c": state_conv_c,
            "state_mlstm_c": state_mlstm_c, "state_mlstm_n": state_mlstm_n, "state_mlstm_m": state_mlstm_m,
            "state_conv_d": state_conv_d, "c_prompt": c_prompt, "c_sample": c_sample,
            "w_mod": w_mod, "b_mod": b_mod, "w_in": w_in, "conv_a_w": conv_a_w, "conv_a_b": conv_a_b,
            "dt_bias": dt_bias, "a_log": a_log, "d_skip": d_skip, "norm_a_w": norm_a_w,
            "conv_c_w": conv_c_w, "conv_c_b": conv_c_b, "wq_c": wq_c, "wk_c": wk_c, "wv_c": wv_c,
            "ig_bias": ig_bias, "fg_bias": fg_bias, "norm_c_w": norm_c_w, "skip_c": skip_c,
            "conv_d_w": conv_d_w, "conv_d_b": conv_d_b, "ln_d_g": ln_d_g, "ln_d_b": ln_d_b,
            "w_out": w_out, "ln_g": ln_g, "ln_b": ln_b}


def reference(x_prompt, x_sample, cache_k, cache_v, state_conv_a, state_ssm, state_conv_c,
              state_mlstm_c, state_mlstm_n, state_mlstm_m, state_conv_d, c_prompt, c_sample,
              w_mod, b_mod, w_in, conv_a_w, conv_a_b, dt_bias, a_log, d_skip, norm_a_w,
              conv_c_w, conv_c_b, wq_c, wk_c, wv_c, ig_bias, fg_bias, norm_c_w, skip_c,
              conv_d_w, conv_d_b, ln_d_g, ln_d_b, w_out, ln_g, ln_b):
    weights = (w_mod, b_mod, w_in, conv_a_w, conv_a_b, dt_bias, a_log, d_skip, norm_a_w,
               conv_c_w, conv_c_b, wq_c, wk_c, wv_c, ig_bias, fg_bias, norm_c_w, skip_c,
               conv_d_w, conv_d_b, ln_d_g, ln_d_b, w_out, ln_g, ln_b)

    def zeros(*shape):
        return jnp.zeros((DEPTH, BATCH) + shape, jnp.float32)

    y_prompt, sp = run_trunk(x_prompt, c_prompt, None, None,
                             zeros(K_A - 1, CONV_DIM_A), zeros(H_A, P_A, N_A), zeros(K_C - 1, W_C),
                             zeros(H_C, DH_C, DH_C), zeros(H_C, DH_C), zeros(H_C), zeros(K_D - 1, W_D), weights)
    y_sample, ss = run_trunk(x_sample, c_sample, cache_k, cache_v, state_conv_a, state_ssm, state_conv_c,
                             state_mlstm_c, state_mlstm_n, state_mlstm_m, state_conv_d, weights)
    return (y_prompt, y_sample,
            sp[0], sp[1], sp[2], sp[3], sp[4], sp[5], sp[6], sp[7], sp[8],
            ss[0], ss[1], ss[2], ss[3], ss[4], ss[5], ss[6], ss[7], ss[8])
```

```python
import numpy as np
import os
from contextlib import ExitStack
import concourse.bass as bass
import concourse.mybir as mybir
from concourse.bass_utils import run_bass_kernel_spmd

F32 = mybir.dt.float32
BF16 = mybir.dt.bfloat16
AF = mybir.ActivationFunctionType
ALU = mybir.AluOpType
AX = mybir.AxisListType

ENGS = ("pe", "act", "dve", "pool", "sp")
SEM_LIMIT = 30000
N_DMA_SEMS = 6
SB_BYTES = 212800


class V:
    __slots__ = ("t", "ap")

    def __init__(self, t, ap):
        self.t = t
        self.ap = ap

    def __getitem__(self, k):
        return V(self.t, self.ap[k])

    def re(self, pattern_, **kw):
        return V(self.t, self.ap.rearrange(pattern_, **kw))

    def un(self, axis):
        return V(self.t, self.ap.unsqueeze(axis))

    def bc(self, shape):
        return V(self.t, self.ap.to_broadcast(list(shape)))

    def bitcast(self, dt):
        return V(self.t, self.ap.bitcast(dt))


class Tile:
    __slots__ = ("ap", "ws", "r", "rd", "name")

    def __init__(self, ap, name=""):
        self.ap = ap
        self.ws = []
        self.r = {}
        self.rd = []
        self.name = name

    def __getitem__(self, k):
        return V(self, self.ap[k])

    @property
    def v(self):
        return V(self, self.ap)


class Ins:
    __slots__ = ("eng", "fn", "deps", "inc", "seq", "dma", "sem", "val")

    def __init__(self, eng, fn, dma):
        self.eng = eng
        self.fn = fn
        self.dma = dma
        self.deps = []
        self.inc = False
        self.seq = 0
        self.sem = None
        self.val = 0


class Prog:
    def __init__(self, nc):
        self.nc = nc
        self.q = {e: [] for e in ENGS}
        self.extra = {e: [] for e in ENGS}
        self.dmas_since_barrier = []

    def _add(self, eng, fn, reads, writes, dma=False):
        ins = Ins(eng, fn, dma)
        ins.seq = len(self.q[eng])
        raw = set()
        deps = []
        for t in reads:
            for w in t.ws:
                deps.append(w)
                raw.add(id(w))
        for t in writes:
            deps.extend(t.ws)
            deps.extend(t.r.values())
            deps.extend(t.rd)
        deps.extend(self.extra[eng])
        self.extra[eng] = []
        for t in writes:
            if t.r or t.rd:
                t.ws = [ins]
            else:
                t.ws = [w for w in t.ws if w.dma or w.eng != eng or dma] + [ins]
            t.r = {}
            t.rd = []
        for t in reads:
            if dma:
                t.rd.append(ins)
            else:
                t.r[eng] = ins
        best = {}
        seen = set()
        out = []
        for d in deps:
            if d is ins or id(d) in seen:
                continue
            seen.add(id(d))
            if d.dma:
                out.append(d)
                continue
            if d.eng == eng and not dma:
                if eng == "pe" or id(d) not in raw:
                    continue
            b = best.get(d.eng)
            if b is None or d.seq > b.seq:
                best[d.eng] = d
        out.extend(best.values())
        for d in out:
            d.inc = True
        ins.deps = out
        self.q[eng].append(ins)
        if dma:
            ins.inc = True
            self.dmas_since_barrier.append(ins)
        return ins

    def op(self, eng, fn, reads=(), writes=()):
        return self._add(eng, fn, reads, writes, False)

    def dma(self, eng, out_ap, in_ap, reads=(), writes=(), **kw):
        return self._add(eng, lambda e: e.dma_start(out=out_ap, in_=in_ap, **kw), reads, writes, True)

    def barrier(self):
        lasts = []
        for e in ENGS:
            for ins in reversed(self.q[e]):
                if not ins.dma:
                    lasts.append(ins)
                    break
        alld = list(self.dmas_since_barrier)
        self.dmas_since_barrier = []
        for e in ENGS:
            self.extra[e] = self.extra[e] + lasts + alld

    def emit(self, stack):
        nc = self.nc
        self.barrier()
        fin = {}
        for e in ENGS:
            deps = self.extra[e]
            self.extra[e] = []
            seen = set()
            out = []
            for d in deps:
                if id(d) in seen:
                    continue
                seen.add(id(d))
                if (not d.dma) and d.eng == e:
                    continue
                d.inc = True
                out.append(d)
            fin[e] = out

        def newsem(name):
            return stack.enter_context(nc.semaphore(name))
        nsem = 0
        for e in ENGS:
            cur = None
            cnt = 0
            k = 0
            dsems = []
            dcnt = []
            di = 0
            for ins in self.q[e]:
                if ins.dma:
                    if not dsems:
                        dsems = [newsem(f"d_{e}_{k}_{j}") for j in range(N_DMA_SEMS)]
                        nsem += N_DMA_SEMS
                        dcnt = [0] * N_DMA_SEMS
                        k += 1
                    j = di % N_DMA_SEMS
                    di += 1
                    dcnt[j] += 16
                    ins.sem = dsems[j]
                    ins.val = dcnt[j]
                    if dcnt[j] >= SEM_LIMIT and j == N_DMA_SEMS - 1:
                        dsems = []
                elif ins.inc:
                    if cur is None or cnt >= SEM_LIMIT:
                        cur = newsem(f"c_{e}_{k}")
                        nsem += 1
                        k += 1
                        cnt = 0
                    cnt += 1
                    ins.sem = cur
                    ins.val = cnt
        waited = {}
        nwaits = [0]

        def do_waits(eobj, e, deps):
            for d in deps:
                key = (e, id(d.sem))
                if waited.get(key, 0) >= d.val:
                    continue
                waited[key] = d.val
                eobj.wait_ge(d.sem, d.val)
                nwaits[0] += 1

        def run(e, eobj):
            for ins in self.q[e]:
                do_waits(eobj, e, ins.deps)
                r = ins.fn(eobj)
                if ins.inc:
                    r.then_inc(ins.sem, 16 if ins.dma else 1)
            do_waits(eobj, e, fin[e])

        block = stack.enter_context(nc.Block())

        @block.tensor
        def _(x):
            run("pe", x)

        @block.scalar
        def _(x):
            run("act", x)

        @block.vector
        def _(x):
            run("dve", x)

        @block.gpsimd
        def _(x):
            run("pool", x)

        @block.sync
        def _(x):
            run("sp", x)
        self.stats = {e: len(self.q[e]) for e in ENGS}
        self.stats["waits"] = nwaits[0]
        self.stats["sems"] = nsem


DSIZE = {F32: 4, BF16: 2}


class Mem:
    def __init__(self, nc, stack, nbytes):
        self.t = stack.enter_context(nc.sbuf_tensor("sbpool", [128, nbytes // 2], BF16))
        self.nbytes = nbytes
        self.top = 0
        self.marks = []
        self.peak = 0

    def push(self):
        self.marks.append(self.top)

    def pop(self):
        self.top = self.marks.pop()

    def alloc(self, shape, dtype, name=""):
        P = shape[0]
        free = list(shape[1:])
        cnt = int(np.prod(free))
        n = cnt * DSIZE[dtype]
        n = (n + 63) // 64 * 64
        off = self.top
        self.top += n
        self.peak = max(self.peak, self.top)
        assert self.top <= self.nbytes, f"SBUF overflow {self.top} > {self.nbytes} at {name}"
        ap = self.t[0:P, off // 2:(off + n) // 2]
        if dtype != BF16:
            ap = ap.bitcast(dtype)
        ap = ap[:, 0:cnt]
        if len(free) == 2:
            ap = ap.rearrange("p (a b) -> p a b", a=free[0], b=free[1])
        elif len(free) == 3:
            ap = ap.rearrange("p (a b c) -> p a b c", a=free[0], b=free[1], c=free[2])
        return Tile(ap, name)


DEPTH = 4
D = 2048
KC = 16
LP = 2048
NS = 4
LS = 32
PAST = 4096
T = LP + NS * LS
ALPHA = (2 * DEPTH) ** 0.25
EPS = 1e-5
GROUPS = [(0, 512), (512, 512), (1024, 512), (1536, 512), (2048, 128)]
NEG = -30000.0

CF_ID, CF_ONE, CF_NEG1, CF_TRI, CF_SEL128, CF_SEL32, CF_SELP5, CF_SELS5 = [i * 128 for i in range(8)]
NCF = 1024
CB_ID, CB_NEG1, CB_LATT = 0, 128, 256
CB_NM4 = 384
CB_NMT4 = 896
CB_AM = 1408
CB_SM = 3456
NCB = 3712
PP_CAW, PP_CAB, PP_CCW, PP_CCB, PP_CDW, PP_CDB, PP_LDG, PP_LDB, PP_NAW, PP_NCW, PP_SKC = 0, 32, 40, 56, 60, 184, 188, 192, 196, 200, 204
NPP = 208
RP_LNG, RP_LNB, RP_DTB, RP_ALOG, RP_DSK, RP_IGB, RP_FGB = 0, 2048, 4096, 4104, 4112, 4120, 4124
NRP = 4128
FB_XBC, FB_Q, FB_K, FB_G, FB_XC, FB_ZC, FB_AD, FB_BD, FB_GD = 0, 8, 12, 16, 20, 24, 28, 32, 36
FM_COLS = [512 + 128 * b for b in range(8)] + [1544 + 128 * b for b in range(4)] + [2056 + 128 * b for b in range(4)] + \
    [3080 + 128 * b for b in range(4)] + [3592 + 128 * b for b in range(4)] + [4104 + 128 * b for b in range(4)] + \
    [4624 + 128 * b for b in range(4)] + [5136 + 128 * b for b in range(4)] + [5648 + 128 * b for b in range(4)]
TM_COLS = [0, 2568, 2056]


def make_consts():
    cf = np.zeros((128, NCF), np.float32)
    k = np.arange(128)[:, None]
    t = np.arange(128)[None, :]
    cf[:, CF_ID:CF_ID + 128] = (k == t)
    cf[:, CF_ONE:CF_ONE + 128] = 1.0
    cf[:, CF_NEG1:CF_NEG1 + 128] = -1.0
    cf[:, CF_TRI:CF_TRI + 128] = (k <= t)
    cf[127, CF_SEL128:CF_SEL128 + 128] = 1.0
    cf[31, CF_SEL32:CF_SEL32 + 128] = 1.0
    cf[0, CF_SELP5:CF_SELP5 + 128] = 1.0
    for j in range(4):
        cf[1 + j, CF_SELS5 + 32 * j:CF_SELS5 + 32 * j + 32] = 1.0
    cb = np.zeros((128, NCB), np.float32)
    cb[:, CB_ID:CB_ID + 128] = (k == t)
    cb[:, CB_NEG1:CB_NEG1 + 128] = -1.0
    cb[:, CB_LATT:CB_LATT + 128] = np.where(k >= t, -1.0, 0.0)
    nm = np.where(t < k, NEG, 0.0)
    nmT = np.where(t > k, NEG, 0.0)
    for h in range(4):
        cb[:, CB_NM4 + 128 * h:CB_NM4 + 128 * h + 128] = nm
        cb[:, CB_NMT4 + 128 * h:CB_NMT4 + 128 * h + 128] = nmT
    tl = np.arange(512)[None, :]
    for dl in range(4):
        cb[:, CB_AM + 512 * dl:CB_AM + 512 * dl + 512] = ((128 * dl + k) < tl)
    q = np.arange(32)[None, :]
    for h in range(8):
        cb[:, CB_SM + 32 * h:CB_SM + 32 * h + 32] = (k < q)
    return cf, cb


class StopBuild(Exception):
    pass


class Builder:
    def chk(self, name):
        if self.stop == name:
            raise StopBuild()

    def __init__(self, nlayers=DEPTH, mixers="ABCD", stop=None):
        self.nlayers = nlayers
        self.mixers = mixers
        self.stop = stop
        self.nc = bass.Bass("TRN2", target_bir_lowering=False)
        self.stack = ExitStack()

    def din(self, name, shape):
        return Tile(self.nc.dram_tensor(name, list(shape), F32, kind="ExternalInput").ap(), name)

    def dout(self, name, shape):
        return self.nc.dram_tensor(name, list(shape), F32, kind="ExternalOutput").ap()

    def mm(self, out, lhsT, rhs, start=True, stop=True):
        self.P.op("pe", lambda e: e.matmul(out.ap, lhsT=lhsT.ap, rhs=rhs.ap, start=start, stop=stop),
                  reads=[lhsT.t, rhs.t], writes=[out.t])

    def tr(self, out, in_, ident):
        self.P.op("pe", lambda e: e.transpose(out=out.ap, in_=in_.ap, identity=ident.ap),
                  reads=[in_.t, ident.t], writes=[out.t])

    def act(self, out, in_, func, scale=None, bias=None, accum=None):
        kw = {}
        reads = [in_.t]
        writes = [out.t]
        if scale is not None:
            if isinstance(scale, V):
                kw["scale"] = scale.ap
                reads.append(scale.t)
            else:
                kw["scale"] = float(scale)
        if bias is not None:
            if isinstance(bias, V):
                kw["bias"] = bias.ap
                reads.append(bias.t)
            else:
                kw["bias"] = self.cbias(float(bias), in_)
                reads.append(self.cbias_t)
        if accum is not None:
            kw["accum_out"] = accum.ap
            writes.append(accum.t)
        self.P.op("act", lambda e: e.activation(out=out.ap, in_=in_.ap, func=func, **kw), reads=reads, writes=writes)

    def cbias(self, val, in_):
        idx = self.cbias_vals.index(val)
        p = in_.ap.shape[0]
        return self.cbias_t.ap[0:p, idx:idx + 1]

    def tt(self, eng, out, a, b, op):
        self.P.op(eng, lambda e: e.tensor_tensor(out=out.ap, in0=a.ap, in1=b.ap, op=op),
                  reads=[a.t, b.t], writes=[out.t])

    def ts(self, eng, out, a, s1, op0, s2=None, op1=None, accum=None):
        reads = [a.t]
        writes = [out.t]
        s1v = s1.ap if isinstance(s1, V) else float(s1)
        if isinstance(s1, V):
            reads.append(s1.t)
        s2v = None
        if s2 is not None:
            s2v = s2.ap if isinstance(s2, V) else float(s2)
            if isinstance(s2, V):
                reads.append(s2.t)
        kw = {}
        if op1 is not None:
            kw["op1"] = op1
        if accum is not None:
            kw["accum_out"] = accum.ap
            writes.append(accum.t)
        self.P.op(eng, lambda e: e.tensor_scalar(out=out.ap, in0=a.ap, scalar1=s1v, scalar2=s2v, op0=op0, **kw),
                  reads=reads, writes=writes)

    def stt(self, out, a, s, b, op0, op1, accum=None):
        reads = [a.t, b.t]
        writes = [out.t]
        sv = s.ap if isinstance(s, V) else float(s)
        if isinstance(s, V):
            reads.append(s.t)
        kw = {}
        if accum is not None:
            kw["accum_out"] = accum.ap
            writes.append(accum.t)
        self.P.op("dve", lambda e: e.scalar_tensor_tensor(out=out.ap, in0=a.ap, scalar=sv, in1=b.ap, op0=op0, op1=op1, **kw),
                  reads=reads, writes=writes)

    def cp(self, eng, out, in_):
        if eng == "act":
            self.P.op("act", lambda e: e.activation(out=out.ap, in_=in_.ap, func=AF.Copy), reads=[in_.t], writes=[out.t])
        else:
            self.P.op(eng, lambda e: e.tensor_copy(out=out.ap, in_=in_.ap), reads=[in_.t], writes=[out.t])

    def red(self, out, in_, op):
        self.P.op("dve", lambda e: e.tensor_reduce(out=out.ap, in_=in_.ap, axis=AX.X, op=op), reads=[in_.t], writes=[out.t])

    def memset(self, eng, out, val):
        self.P.op(eng, lambda e: e.memset(out.ap, val), writes=[out.t])

    def dma(self, q, out, in_, **kw):
        self.P.dma(q, out.ap, in_.ap, reads=[in_.t], writes=[out.t], **kw)

    def bnstats(self, out, in_):
        self.P.op("dve", lambda e: e.bn_stats(out=out.ap, in_=in_.ap), reads=[in_.t], writes=[out.t])

    def bnaggr(self, out, in_):
        self.P.op("dve", lambda e: e.bn_aggr(out=out.ap, in_=in_.ap), reads=[in_.t], writes=[out.t])

    def recip(self, out, in_):
        self.P.op("dve", lambda e: e.reciprocal(out=out.ap, in_=in_.ap), reads=[in_.t], writes=[out.t])

    def bankt(self, i, bf=False):
        ap = self.banks[i][:, :]
        if bf:
            ap = ap.bitcast(BF16)
        return Tile(ap, f"bank{i}")

    def phase_begin(self):
        self.P.barrier()
        self.mem.push()

    def phase_end(self):
        self.P.barrier()
        self.mem.pop()

    def rstd(self, out, var, tmp, scale=1.0):
        self.act(tmp, var, AF.Ln, scale=scale, bias=EPS)
        self.act(out, tmp, AF.Exp, scale=-0.5)

    def build(self):
        nc = self.nc
        st = self.stack
        L = self.nlayers
        I = {}
        I["xp"] = self.din("xp", [LP, D])
        I["xs"] = self.din("xs", [NS * LS, D])
        I["ck"] = self.din("ck", [L, NS, PAST, 512])
        I["cv"] = self.din("cv", [L, NS, PAST, 512])
        I["sca"] = self.din("sca", [L, NS * 3, 1024])
        I["sssm"] = self.din("sssm", [L, NS, 512, 128])
        I["scc"] = self.din("scc", [L, NS * 3, 512])
        I["smc"] = self.din("smc", [L, NS, 4, 128, 128])
        I["smn"] = self.din("smn", [L, NS, 4, 128])
        I["smm"] = self.din("smm", [L, NS, 4])
        I["scd"] = self.din("scd", [L, NS, 30, 512])
        I["cT"] = self.din("cT", [128, KC * 5])
        I["wmod"] = self.din("wmod", [L, 12, 128, KC * 512])
        I["bmod"] = self.din("bmod", [L, 5, 6144])
        I["wfm"] = self.din("wfm", [L, 40, 128, KC * 128])
        I["wtm"] = self.din("wtm", [L, 3, 128, KC * 512])
        I["wsm"] = self.din("wsm", [L, 128, KC * 16])
        I["wout"] = self.din("wout", [L, 128, KC * D])
        I["wqkv"] = self.din("wqkv", [L, 3, 128, 4 * 128])
        I["ppack"] = self.din("ppack", [L, 128, NPP])
        I["rpack"] = self.din("rpack", [L, 128, NRP])
        I["cf"] = self.din("cf", [128, NCF])
        I["cb"] = self.din("cb", [128, NCB])
        self.I = I
        O = {}

        def o(name, shape):
            O[name] = Tile(self.dout(name, shape), name)
        o("yp", [LP, D]); o("ys", [NS * LS, D])
        o("nkp", [L, LP, 512]); o("nvp", [L, LP, 512])
        o("cap", [L, 3, 1024]); o("ssmp", [L, 512, 128]); o("ccp", [L, 3, 512])
        o("mcp", [L, 4, 128, 128]); o("mnp", [L, 4, 128]); o("mmp", [L, 4]); o("cdp", [L, 30, 512])
        o("nks", [L, NS * LS, 512]); o("nvs", [L, NS * LS, 512])
        o("cas", [L, NS, 3, 1024]); o("ssms", [L, NS, 512, 128]); o("ccs", [L, NS, 3, 512])
        o("mcs", [L, NS, 4, 128, 128]); o("mns", [L, NS, 4, 128]); o("mms", [L, NS, 4]); o("cds", [L, NS, 30, 512])
        self.O = O
        self.xres = [Tile(nc.dram_tensor(f"xres{i}", [128, D], F32, kind="Internal").ap(), f"xres{i}") for i in range(17)]
        ytd = nc.dram_tensor("ytd", [KC, 128, T], BF16, kind="Internal").ap()
        self.ytd = ytd
        self.ytt = [[Tile(ytd[4 * m:4 * m + 4, :, (i * 128):(i * 128 + 128)], f"yt{m}_{i}") for i in range(17)] for m in range(4)]
        self.modg = Tile(nc.dram_tensor("modg", [5, D], F32, kind="Internal").ap(), "modg")

        self.mem = Mem(nc, st, SB_BYTES)
        self.banks = [st.enter_context(nc.psum_tensor(f"bank{i}", [128, 512], F32)) for i in range(8)]
        self.P = Prog(nc)
        mem = self.mem

        self.cf = mem.alloc([128, NCF], F32, "cf")
        self.cb = mem.alloc([128, NCB], BF16, "cb")
        self.dma("sp", self.cf.v, I["cf"].v)
        self.dma("pool", self.cb.v, I["cb"].v)
        self.cbias_vals = [EPS, 1.0]
        self.cbias_t = mem.alloc([128, 2], F32, "cbias")
        self.memset("dve", self.cbias_t[:, 0:1], EPS)
        self.memset("dve", self.cbias_t[:, 1:2], 1.0)
        cT = mem.alloc([128, KC, 5], BF16, "cT")
        self.dma("pool", cT.v, I["cT"].v.re("p (k s) -> p k s", k=KC, s=5))
        self.cT = cT
        self.pp = mem.alloc([128, NPP], F32, "pp")
        self.rs = mem.alloc([128, 32], F32, "rs")
        self.modT = mem.alloc([128, 32, 5], F32, "modT")
        self.small = mem.alloc([128, 20, 16], F32, "small")

        try:
            for l in range(self.nlayers):
                self.layer(l)
        except StopBuild:
            pass
        self.P.emit(st)
        return nc

    def cfv(self, off, Q=128, M=128):
        return self.cf[0:Q, off:off + M]

    def cbv(self, off, Q=128, M=128):
        return self.cb[0:Q, off:off + M]

    def layer(self, l):
        I = self.I
        self.P.barrier()
        self.dma("sp", self.pp.v, I["ppack"][l])
        self.dma("sp", self.rs.v, I["rpack"][l, :, 4096:4128])
        if self.stop == "const":
            return
        self.phase_mod(l)
        if self.stop == "mod":
            return
        self.P.barrier()
        self.mem.push()
        self.uT_off = self.mem.top
        self.uT = self.mem.alloc([128, KC, T], BF16, "uT")
        self.phase_ln(l)
        if self.stop == "ln":
            return
        if "A" in self.mixers:
            self.mixer_A(l)
        if "B" in self.mixers:
            self.mixer_B(l)
        if "C" in self.mixers:
            self.mixer_C(l)
        if "D" in self.mixers:
            self.mixer_D(l)
        self.P.barrier()
        self.mem.pop()
        self.phase_out(l)

    def phase_mod(self, l):
        I = self.I
        mem = self.mem
        self.phase_begin()
        modsb = mem.alloc([5, 6144], F32, "modsb")
        bm = mem.alloc([5, 6144], F32, "bm")
        self.dma("sp", bm.v, I["bmod"][l])
        wm = [mem.alloc([128, KC, 512], BF16, f"wm{i}") for i in range(2)]
        bk = [self.bankt(0), self.bankt(1)]
        bT = self.bankt(2)
        for blk in range(12):
            w = wm[blk % 2]
            self.dma("pool", w.v, I["wmod"][l, blk].re("p (k c) -> p k c", k=KC, c=512))
            b = bk[blk % 2]
            for kc in range(KC):
                self.mm(b[0:5, :], self.cT[:, kc, :], w[:, kc, :], start=(kc == 0), stop=(kc == KC - 1))
            self.tt("dve", modsb[:, blk * 512:(blk + 1) * 512], b[0:5, :], bm[:, blk * 512:(blk + 1) * 512], ALU.add)
        for j in range(32):
            self.tr(bT[:, j * 8:j * 8 + 5], modsb[0:5, j * 128:(j + 1) * 128], self.cfv(CF_ID, 5, 5))
        self.cp("dve", self.modT[:, 0:16, :], bT[:, 0:128].re("p (a b) -> p a b", a=16, b=8)[:, :, 0:5])
        self.ts("dve", self.modT[:, 16:32, :], bT[:, 128:256].re("p (a b) -> p a b", a=16, b=8)[:, :, 0:5], 1.0, ALU.add)
        self.dma("sp", self.modg.v, modsb[0:5, 4096:6144])
        self.phase_end()

    def xsrc(self, l, i):
        if l == 0:
            if i < 16:
                return self.I["xp"][i * 128:(i + 1) * 128, :]
            return self.I["xs"].v
        return self.xres[i].v

    def phase_ln(self, l):
        mem = self.mem
        self.phase_begin()
        xt = [mem.alloc([128, D], F32, f"xt{i}") for i in range(2)]
        xn = [mem.alloc([128, D], F32, f"xn{i}") for i in range(2)]
        bs = [mem.alloc([128, 24], F32, f"bs{i}") for i in range(2)]
        mv = [mem.alloc([128, 4], F32, f"mv{i}") for i in range(2)]
        bk = [[self.bankt(4 * s + g) for g in range(4)] for s in range(2)]
        ident = self.cfv(CF_ID)
        for i in range(17):
            s = i % 2
            x, n, b, m = xt[s], xn[s], bs[s], mv[s]
            self.dma("sp", x.v, self.xsrc(l, i))
            for j in range(4):
                self.bnstats(b[:, j * 6:(j + 1) * 6], x[:, j * 512:(j + 1) * 512])
            self.bnaggr(m[:, 0:2], b[:, 0:24])
            self.rstd(m[:, 2:3], m[:, 1:2], m[:, 3:4])
            self.ts("dve", n.v, x.v, m[:, 0:1], ALU.subtract, m[:, 2:3], ALU.mult)
            for kc in range(KC):
                self.tr(bk[s][kc // 4][:, (kc % 4) * 128:(kc % 4) * 128 + 128], n[:, kc * 128:(kc + 1) * 128], ident)
            for kc in range(KC):
                src = bk[s][kc // 4][:, (kc % 4) * 128:(kc % 4) * 128 + 128]
                if i < 16:
                    dst = self.uT[:, kc, i * 128:(i + 1) * 128]
                    if kc % 2 == 0:
                        self.act(dst, src, AF.Identity, scale=self.modT[:, 16 + kc, 0:1], bias=self.modT[:, kc, 0:1])
                    else:
                        self.ts("dve", dst, src, self.modT[:, 16 + kc, 0:1], ALU.mult, self.modT[:, kc, 0:1], ALU.add)
                else:
                    for j in range(NS):
                        dst = self.uT[:, kc, LP + 32 * j:LP + 32 * j + 32]
                        sj = src[:, 32 * j:32 * j + 32]
                        if (kc + j) % 2 == 0:
                            self.act(dst, sj, AF.Identity, scale=self.modT[:, 16 + kc, 1 + j:2 + j], bias=self.modT[:, kc, 1 + j:2 + j])
                        else:
                            self.ts("dve", dst, sj, self.modT[:, 16 + kc, 1 + j:2 + j], ALU.mult, self.modT[:, kc, 1 + j:2 + j], ALU.add)
        self.phase_end()

    def phase_out(self, l):
        I, O = self.I, self.O
        mem = self.mem
        self.phase_begin()
        last = (l == self.nlayers - 1)
        pre = getattr(self, "wout_pre", None)
        if pre is not None:
            assert mem.top == self.uT_off
        wout = mem.alloc([128, KC, D], BF16, "wout")
        if pre is not None:
            wout = pre
            self.wout_pre = None
        else:
            for q in range(4):
                self.dma("pool", wout[:, 4 * q:4 * q + 4, :], I["wout"][l, :, 4 * q * D:(4 * q + 4) * D].re("p (k c) -> p k c", k=4, c=D))
        lng = mem.alloc([128, D], F32, "lng")
        lnb = mem.alloc([128, D], F32, "lnb")
        self.dma("sp", lng.v, I["rpack"][l, :, 0:2048])
        self.dma("sp", lnb.v, I["rpack"][l, :, 2048:4096])
        mg = mem.alloc([5, D], F32, "mg")
        self.dma("sp", mg.v, self.modg.v)
        gP = mem.alloc([128, D], F32, "gP")
        gS = mem.alloc([128, D], F32, "gS")
        bk = [self.bankt(i) for i in range(8)]
        for c in range(4):
            self.mm(bk[c].v, self.cf[0:5, CF_SELP5:CF_SELP5 + 128], mg[0:5, c * 512:(c + 1) * 512])
            self.ts("dve", gP[:, c * 512:(c + 1) * 512], bk[c].v, 1.0, ALU.add)
            self.mm(bk[4 + c].v, self.cf[0:5, CF_SELS5:CF_SELS5 + 128], mg[0:5, c * 512:(c + 1) * 512])
            self.ts("dve", gS[:, c * 512:(c + 1) * 512], bk[4 + c].v, 1.0, ALU.add)
        yt = [mem.alloc([128, KC, 128], BF16, f"yt{i}") for i in range(2)]
        xt = [mem.alloc([128, D], F32, f"xt{i}") for i in range(2)]
        vt = [mem.alloc([128, D], F32, f"vt{i}") for i in range(2)]
        bs = [mem.alloc([128, 24], F32, f"bs{i}") for i in range(2)]
        mv = [mem.alloc([128, 4], F32, f"mv{i}") for i in range(2)]
        for i in range(17):
            s = i % 2
            y, x, v, xx, b, m = yt[s], xt[s], vt[s], xt[s], bs[s], mv[s]
            for mx in range(4):
                self.P.dma("sp", y.ap[:, 4 * mx:4 * mx + 4, :],
                           self.ytd[4 * mx:4 * mx + 4, :, i * 128:(i + 1) * 128].rearrange("c p t -> p c t"),
                           reads=[self.ytt[mx][i]], writes=[y])
            self.dma("sp", x.v, self.xsrc(l, i))
            if os.environ.get("DBGY") and i == 16 and l == 0:
                self.cp("dve", gP.v, y.v.re("p a b -> p (a b)"))
                self.dma("sp", O["yp"][0:128, :], gP.v)
            g = gP if i < 16 else gS
            for c in range(4):
                b4 = bk[4 * s + c]
                for kc in range(KC):
                    self.mm(b4.v, y[:, kc, :], wout[:, kc, c * 512:(c + 1) * 512], start=(kc == 0), stop=(kc == KC - 1))
                self.tt("dve", v[:, c * 512:(c + 1) * 512], b4.v, g[:, c * 512:(c + 1) * 512], ALU.mult)
            self.stt(v.v, x.v, ALPHA, v.v, ALU.mult, ALU.add)
            for j in range(4):
                self.bnstats(b[:, j * 6:(j + 1) * 6], v[:, j * 512:(j + 1) * 512])
            self.bnaggr(m[:, 0:2], b[:, 0:24])
            self.rstd(m[:, 2:3], m[:, 1:2], m[:, 3:4])
            self.ts("dve", m[:, 3:4], m[:, 0:1], m[:, 2:3], ALU.mult, -1.0, ALU.mult)
            self.act(xx.v, v.v, AF.Identity, scale=m[:, 2:3], bias=m[:, 3:4])
            self.tt("pool", xx.v, xx.v, lng.v, ALU.mult)
            self.tt("pool", xx.v, xx.v, lnb.v, ALU.add)
            if last:
                if i < 16:
                    dst = O["yp"][i * 128:(i + 1) * 128, :]
                else:
                    dst = O["ys"].v
            else:
                dst = self.xres[i].v
            self.dma("pool", dst, xx.v)
        self.phase_end()

    def fm_stream(self, l, blocks, depth=3):
        mem = self.mem
        ring = [mem.alloc([128, KC, 128], BF16, f"wfm{i}") for i in range(depth)]
        state = {"next": 0}
        I = self.I

        def ensure(upto):
            while state["next"] <= upto and state["next"] < len(blocks):
                k = state["next"]
                self.dma("pool", ring[k % depth].v, I["wfm"][l, blocks[k]].re("p (k c) -> p k c", k=KC, c=128))
                state["next"] += 1

        def get(k):
            ensure(k + depth - 1)
            return ring[k % depth]
        return get

    def fm_block(self, w, banks, evac, groups=GROUPS):
        for g, (c0, n) in enumerate(groups):
            b = banks[g % len(banks)]
            for kc in range(KC):
                self.mm(b[:, 0:n], w[:, kc, :], self.uT[:, kc, c0:c0 + n], start=(kc == 0), stop=(kc == KC - 1))
            evac(g, c0, n, b[:, 0:n])

    def seqs(self):
        out = [("P", 0, 128, list(range(16)), 0)]
        for j in range(NS):
            out.append(("S", j, 32, [16 + j], LP + 32 * j))
        return out

    @staticmethod
    def ccol(c):
        return c * 128 if c < 16 else LP + 32 * (c - 16)

    def small_proj(self, l):
        mem = self.mem
        I = self.I
        mem.push()
        wsm = mem.alloc([128, KC, 16], BF16, "wsm")
        self.dma("pool", wsm.v, I["wsm"][l].re("p (k c) -> p k c", k=KC, c=16))
        b = self.bankt(7)
        for c in range(20):
            Q = 128 if c < 16 else 32
            c0 = self.ccol(c)
            for kc in range(KC):
                self.mm(b[0:Q, c * 16:(c + 1) * 16], self.uT[:, kc, c0:c0 + Q], wsm[:, kc, :], start=(kc == 0), stop=(kc == KC - 1))
        self.cp("dve", self.small[:, 0:16, :], b[:, 0:256].re("p (a b) -> p a b", a=16, b=16))
        self.cp("dve", self.small[0:32, 16:20, :], b[0:32, 256:320].re("p (a b) -> p a b", a=4, b=16))
        self.P.barrier()
        mem.pop()

    def conv4(self, xin, xin_s, acc, acc_s, wcol, bcol):
        pp = self.pp
        self.ts("dve", acc.v, xin[:, 0:LP], pp[:, wcol:wcol + 1], ALU.mult, pp[:, bcol:bcol + 1], ALU.add)
        self.ts("pool", acc_s.v, xin_s[:, :, 0:LS], pp[:, wcol:wcol + 1], ALU.mult, pp[:, bcol:bcol + 1], ALU.add)
        for k in range(1, 4):
            self.stt(acc.v, xin[:, k:k + LP], pp[:, wcol + k:wcol + k + 1], acc.v, ALU.mult, ALU.add)
            self.stt(acc_s.v, xin_s[:, :, k:k + LS], pp[:, wcol + k:wcol + k + 1], acc_s.v, ALU.mult, ALU.add)

    def mixer_A(self, l):
        I, O = self.I, self.O
        mem = self.mem
        self.phase_begin()
        self.small_proj(l)
        self.chk("A_small")
        xtok = mem.alloc([128, 20, 512], BF16, "xtok")
        btok = mem.alloc([128, 20, 256], BF16, "btok")
        BT = mem.alloc([128, 2, T], BF16, "BT")
        CT = mem.alloc([128, 2, T], BF16, "CT")
        identb = self.cbv(CB_ID)
        mem.push()
        xins = [mem.alloc([128, 3 + LP], F32, f"xin{i}") for i in range(2)]
        xin_ss = [mem.alloc([128, NS, 3 + LS], F32, f"xin_s{i}") for i in range(2)]
        accs = [mem.alloc([128, LP], F32, f"acc{i}") for i in range(2)]
        acc_ss = [mem.alloc([128, NS, LS], F32, f"acc_s{i}") for i in range(2)]
        so = mem.alloc([128, T], BF16, "so")
        cst = mem.alloc([12, 1024], F32, "cst")
        self.dma("sp", cst.v, I["sca"][l])
        for x_ in xins:
            self.memset("dve", x_[:, 0:3], 0.0)
        get = self.fm_stream(l, list(range(FB_XBC, FB_XBC + 8)))
        pb = [self.bankt(i) for i in range(4)]
        tb = [self.bankt(4, bf=True), self.bankt(5, bf=True)]
        sb_ = self.bankt(6)
        for b in range(8):
            w = get(b)
            xin, xin_s, acc, acc_s = xins[b % 2], xin_ss[b % 2], accs[b % 2], acc_ss[b % 2]
            self.tr(sb_[:, 0:12], cst[0:12, b * 128:(b + 1) * 128], self.cfv(CF_ID, 12, 12))
            self.cp("dve", xin_s[:, :, 0:3], sb_[:, 0:12].re("p (a b) -> p a b", a=NS, b=3))

            def evac(g, c0, n, ps, b=b):
                if g < 4:
                    self.cp("act", xin[:, 3 + c0:3 + c0 + n], ps)
                else:
                    self.cp("act", xin_s[:, :, 3:3 + LS], ps.re("p (a b) -> p a b", a=NS, b=LS))
            self.fm_block(w, pb, evac)
            self.dma("sp", O["cap"][l, :, b * 128:(b + 1) * 128].re("k f -> f k"), xin[:, LP:LP + 3], allow_slow_non_contiguous=True)
            for j in range(NS):
                self.dma("sp", O["cas"][l, j, :, b * 128:(b + 1) * 128].re("k f -> f k"), xin_s[:, j, LS:LS + 3], allow_slow_non_contiguous=True)
            self.conv4(xin, xin_s, acc, acc_s, PP_CAW + 4 * b, PP_CAB + b)
            if b < 4:
                dst = so.v
            elif b < 6:
                dst = BT[:, b - 4, :]
            else:
                dst = CT[:, b - 6, :]
            self.act(dst[:, 0:LP], acc.v, AF.Silu)
            self.act(dst[:, LP:T].re("p (a b) -> p a b", a=NS, b=LS), acc_s.v, AF.Silu)
            if b < 6:
                tok = xtok if b < 4 else btok
                co = b * 128 if b < 4 else (b - 4) * 128
                for q in range(4):
                    t_ = tb[q % 2]
                    for ii in range(4):
                        i = 4 * q + ii
                        self.tr(t_[:, ii * 128:(ii + 1) * 128], dst[:, i * 128:(i + 1) * 128], identb)
                    self.cp("act" if q % 2 else "dve", tok[:, 4 * q:4 * q + 4, co:co + 128], t_[:, 0:512].re("p (a b) -> p a b", a=4, b=128))
                t_ = tb[0]
                for j in range(NS):
                    self.tr(t_[0:32, j * 128:(j + 1) * 128], dst[:, LP + 32 * j:LP + 32 * j + 32], identb)
                self.cp("dve", tok[0:32, 16:20, co:co + 128], t_[0:32, 0:512].re("p (a b) -> p a b", a=4, b=128))
        self.P.barrier()
        mem.pop()
        self.chk("A_a")
        mem.push()
        wz = mem.alloc([128, KC, 512], BF16, "wz")
        self.dma("pool", wz.v, I["wtm"][l, 0].re("p (k c) -> p k c", k=KC, c=512))
        dt = mem.alloc([128, 20, 8], F32, "dt")
        dta = mem.alloc([128, 20, 8], F32, "dta")
        ab = mem.alloc([128, 8], F32, "ab")
        rs = self.rs
        self.tt("dve", dt.v, self.small[:, :, 0:8], rs[:, 0:8].un(1).bc([128, 20, 8]), ALU.add)
        self.act(dt.v, dt.v, AF.Exp)
        self.act(dt.v, dt.v, AF.Ln, bias=1.0)
        self.act(ab.v, rs[:, 8:16], AF.Exp)
        self.ts("dve", ab.v, ab.v, -1.0, ALU.mult)
        self.tt("dve", dta.v, dt.v, ab.v.un(1).bc([128, 20, 8]), ALU.mult)
        self.chk("A_dt")
        W = {}
        for nm, shp, dty in [("X", [128, 8, 128], F32), ("dec", [128, 8, 128], F32), ("MT", [128, 8, 128], BF16),
                             ("xdt", [128, 8, 64], BF16), ("xdtw", [128, 8, 64], BF16), ("y1", [128, 8, 64], F32),
                             ("y2", [128, 8, 64], F32), ("y3", [128, 8, 64], F32), ("sz", [128, 512], F32),
                             ("yg", [128, 512], F32), ("junk", [128, 512], F32), ("yn", [128, 512], F32),
                             ("yT", [128, 4, 128], BF16), ("acum", [128, 8], F32), ("nacum", [128, 8], F32),
                             ("eacum", [128, 8], F32), ("tmp8", [128, 8], F32), ("wS", [128, 8], F32),
                             ("elast", [128, 8], F32), ("ss", [128, 4], F32), ("hT", [128, 8, 64], F32),
                             ("hTb", [128, 8, 64], BF16), ("hin", [128, 4, 128], F32), ("hout", [128, 4, 128], F32)]:
            W[nm] = mem.alloc(shp, dty, nm)
        bk = {"P1a": self.bankt(0), "P1b": self.bankt(1), "intra": self.bankt(3), "inter": self.bankt(4),
              "z": self.bankt(5), "yT": self.bankt(6), "su": self.bankt(7)}
        b2 = self.banks[2]
        bk["GT"] = Tile(b2[:, 0:256], "GT")
        bk["ac"] = Tile(b2[:, 256:264], "ac")
        bk["al"] = Tile(b2[:, 264:272], "al")
        for (kind, j, Q, chunks, c00) in self.seqs():
            hT, hTb = W["hT"], W["hTb"]
            hTf = hT.v.re("p a b -> p (a b)")
            if kind == "P":
                self.memset("dve", hT.v, 0.0)
                self.memset("pool", hTb.v, 0.0)
            else:
                self.dma("sp", W["hin"].v, I["sssm"][l, j].re("(a p) n -> p a n", a=4, p=128))
                for a in range(4):
                    self.tr(bk["su"][:, a * 128:(a + 1) * 128], W["hin"][:, a, :], self.cfv(CF_ID))
                self.cp("dve", hTf, bk["su"].v)
                self.cp("act", hTb.v, hT.v)
            if kind == "S":
                self.chk("A_S0in")
            for c in chunks:
                self.ssd_chunk(l, c, Q, W, bk, xtok, btok, BT, CT, dt, dta, wz)
            for a in range(4):
                self.tr(bk["su"][:, a * 128:(a + 1) * 128], hTf[:, a * 128:(a + 1) * 128], self.cfv(CF_ID))
            self.cp("dve", W["hout"].v.re("p a b -> p (a b)"), bk["su"].v)
            dst = O["ssmp"][l] if kind == "P" else O["ssms"][l, j]
            self.dma("sp", dst.re("(a p) n -> p a n", a=4, p=128), W["hout"].v)
            self.chk("A_Pout")
        self.P.barrier()
        mem.pop()
        self.phase_end()

    def ssd_chunk(self, l, c, Q, W, bk, xtok, btok, BT, CT, dt, dta, wz):
        c0 = self.ccol(c)
        X, dec, MT, xdt, xdtw = W["X"], W["dec"], W["MT"], W["xdt"], W["xdtw"]
        tri = self.cfv(CF_TRI, Q, Q)
        ones = self.cfv(CF_ONE, Q, Q)
        identb = self.cbv(CB_ID, Q, Q)
        nm4 = self.cb[0:Q, CB_NM4:CB_NM4 + 512].re("p (a b) -> p a b", a=4, b=128)[:, :, 0:Q]
        sel = self.cf[0:Q, (CF_SEL128 if Q == 128 else CF_SEL32):(CF_SEL128 if Q == 128 else CF_SEL32) + 128]
        self.tt("dve", X[0:Q, :, 0:Q], tri.un(1).bc([Q, 8, Q]), dta[0:Q, c, :].un(2).bc([Q, 8, Q]), ALU.mult)
        for hf in range(2):
            p1 = bk["P1a" if hf == 0 else "P1b"]
            o_ = p1[0:Q, 0:4 * Q].re("p (a b) -> p a b", a=4, b=Q)
            self.mm(o_, ones, X[0:Q, 4 * hf:4 * hf + 4, 0:Q], start=True, stop=False)
            self.mm(o_, identb, nm4, start=False, stop=True)
        self.chk("A_c1")
        self.mm(bk["ac"][0:Q, :], tri, dta[0:Q, c, :])
        self.cp("dve", W["acum"][0:Q, :], bk["ac"][0:Q, :])
        self.ts("dve", W["nacum"][0:Q, :], bk["ac"][0:Q, :], -1.0, ALU.mult)
        self.mm(bk["al"].v, sel, W["acum"][0:Q, :])
        self.act(W["eacum"][0:Q, :], W["acum"][0:Q, :], AF.Exp)
        self.tt("dve", W["tmp8"][0:Q, :], bk["al"][0:Q, :], W["acum"][0:Q, :], ALU.subtract)
        self.act(W["wS"][0:Q, :], W["tmp8"][0:Q, :], AF.Exp)
        self.act(W["elast"].v, bk["al"].v, AF.Exp)
        for h in range(8):
            p1 = bk["P1a" if h < 4 else "P1b"]
            self.act(dec[0:Q, h, 0:Q], p1[0:Q, (h % 4) * Q:(h % 4) * Q + Q], AF.Exp, bias=W["nacum"][0:Q, h:h + 1])
        self.chk("A_c2")
        for g in range(2):
            self.mm(bk["GT"][0:Q, g * Q:(g + 1) * Q], BT[:, g, c0:c0 + Q], CT[:, g, c0:c0 + Q])
        for g in range(2):
            self.tt("dve", MT[0:Q, 4 * g:4 * g + 4, 0:Q], dec[0:Q, 4 * g:4 * g + 4, 0:Q],
                    bk["GT"][0:Q, g * Q:(g + 1) * Q].un(1).bc([Q, 4, Q]), ALU.mult)
        xt_ = xtok[0:Q, c, :].re("p (a b) -> p a b", a=8, b=64)
        self.tt("pool", xdt[0:Q], xt_, dt[0:Q, c, :].un(2).bc([Q, 8, 64]), ALU.mult)
        for h in range(8):
            self.mm(bk["intra"][0:Q, h * 64:(h + 1) * 64], MT[0:Q, h, 0:Q], xdt[0:Q, h, :])
        hTbf = W["hTb"].v.re("p a b -> p (a b)")
        for g in range(2):
            self.mm(bk["inter"][0:Q, g * 256:(g + 1) * 256], CT[:, g, c0:c0 + Q], hTbf[:, g * 256:(g + 1) * 256])
        self.tt("dve", W["y1"][0:Q], bk["inter"][0:Q, :].re("p (a b) -> p a b", a=8, b=64),
                W["eacum"][0:Q, :].un(2).bc([Q, 8, 64]), ALU.mult)
        self.tt("dve", W["y2"][0:Q], bk["intra"][0:Q, :].re("p (a b) -> p a b", a=8, b=64), W["y1"][0:Q], ALU.add)
        self.tt("pool", W["y3"][0:Q], xt_, self.rs[0:Q, 16:24].un(2).bc([Q, 8, 64]), ALU.mult)
        self.tt("pool", W["y2"][0:Q], W["y2"][0:Q], W["y3"][0:Q], ALU.add)
        self.chk("A_c3")
        for kc in range(KC):
            self.mm(bk["z"][0:Q, :], self.uT[:, kc, c0:c0 + Q], wz[:, kc, :], start=(kc == 0), stop=(kc == KC - 1))
        self.act(W["sz"][0:Q, :], bk["z"][0:Q, :], AF.Silu)
        self.tt("dve", W["yg"][0:Q, :], W["y2"][0:Q].re("p a b -> p (a b)"), W["sz"][0:Q, :], ALU.mult)
        self.act(W["junk"][0:Q, :], W["yg"][0:Q, :], AF.Square, accum=W["ss"][0:Q, 0:1])
        self.rstd(W["ss"][0:Q, 1:2], W["ss"][0:Q, 0:1], W["ss"][0:Q, 2:3], scale=1.0 / 512)
        self.ts("dve", W["yn"][0:Q, :], W["yg"][0:Q, :], W["ss"][0:Q, 1:2], ALU.mult)
        for a in range(4):
            self.tr(bk["yT"][:, a * Q:(a + 1) * Q], W["yn"][0:Q, a * 128:(a + 1) * 128], self.cfv(CF_ID, Q, Q))
        for a in range(4):
            self.act(W["yT"][:, a, 0:Q], bk["yT"][:, a * Q:(a + 1) * Q], AF.Identity, scale=self.pp[:, PP_NAW + a:PP_NAW + a + 1])
        ti = c if c < 16 else 16
        self.P.dma("sp", self.ytd[0:4, :, c0:c0 + Q].rearrange("c p t -> p c t"), W["yT"].ap[:, :, 0:Q],
                   reads=[W["yT"]], writes=[self.ytt[0][ti]])
        self.chk("A_c4")
        self.tt("pool", xdtw[0:Q], xdt[0:Q], W["wS"][0:Q, :].un(2).bc([Q, 8, 64]), ALU.mult)
        for g in range(2):
            self.mm(bk["su"][:, g * 256:(g + 1) * 256], btok[0:Q, c, g * 128:(g + 1) * 128],
                    xdtw[0:Q, 4 * g:4 * g + 4, :])
        self.tt("dve", W["hT"].v, W["hT"].v, W["elast"].v.un(2).bc([128, 8, 64]), ALU.mult)
        self.tt("dve", W["hT"].v, W["hT"].v, bk["su"].v.re("p (a b) -> p a b", a=8, b=64), ALU.add)
        self.cp("act", W["hTb"].v, W["hT"].v)
        self.chk("A_c5")
        if c == 15:
            self.chk("A_P")

    def mixer_B(self, l):
        I, O = self.I, self.O
        mem = self.mem
        self.phase_begin()
        identb = self.cbv(CB_ID)
        vtok = mem.alloc([128, 16, 512], BF16, "vtok")
        vs = mem.alloc([32, NS, 512], BF16, "vs")
        mem.push()
        wv = mem.alloc([128, KC, 512], BF16, "wv")
        wk = mem.alloc([128, KC, 512], BF16, "wk")
        self.dma("pool", wv.v, I["wtm"][l, 1].re("p (k c) -> p k c", k=KC, c=512))
        self.dma("pool", wk.v, I["wtm"][l, 2].re("p (k c) -> p k c", k=KC, c=512))
        ko = [mem.alloc([128, 512], F32, f"ko{i}") for i in range(2)]
        vo = [mem.alloc([128, 512], F32, f"vo{i}") for i in range(2)]
        bkk = [self.bankt(0), self.bankt(1)]
        bkv = [self.bankt(2), self.bankt(3)]
        bvs = self.bankt(4)
        for i in range(17):
            s = i % 2
            c0 = i * 128
            for kc in range(KC):
                self.mm(bkv[s].v, self.uT[:, kc, c0:c0 + 128], wv[:, kc, :], start=(kc == 0), stop=(kc == KC - 1))
            for kc in range(KC):
                self.mm(bkk[s].v, self.uT[:, kc, c0:c0 + 128], wk[:, kc, :], start=(kc == 0), stop=(kc == KC - 1))
            self.cp("act", vo[s].v, bkv[s].v)
            self.cp("dve", ko[s].v, bkk[s].v)
            if i < 16:
                self.cp("pool", vtok[:, i, :], vo[s].v)
                self.dma("sp", O["nvp"][l, c0:c0 + 128, :], vo[s].v)
                self.dma("sp", O["nkp"][l, c0:c0 + 128, :], ko[s].v)
            else:
                self.dma("sp", O["nvs"][l], vo[s].v)
                self.dma("sp", O["nks"][l], ko[s].v)
        for j in range(NS):
            c0 = LP + 32 * j
            for kc in range(KC):
                self.mm(bvs[0:32, :], self.uT[:, kc, c0:c0 + 32], wv[:, kc, :], start=(kc == 0), stop=(kc == KC - 1))
            self.cp("act", vs[:, j, :], bvs[0:32, :])
        self.P.barrier()
        mem.pop()
        self.chk("B_tm")
        mem.push()
        qT = mem.alloc([128, T], BF16, "qT")
        kT = mem.alloc([128, T], BF16, "kT")
        sg = mem.alloc([128, T], BF16, "sg")
        e_sb = [mem.alloc([128, 512], F32, f"e{i}") for i in range(2)]
        sp_ = [mem.alloc([128, 512], BF16, f"sp{i}") for i in range(2)]
        Wt = [mem.alloc([128, 512], BF16, f"W{i}") for i in range(2)]
        Sl = mem.alloc([128, 512], BF16, "Sl")
        yTt = [mem.alloc([128, 512], BF16, f"yTb{i}") for i in range(2)]
        kst = [mem.alloc([128, 4, 512], BF16, f"kst{i}") for i in range(2)]
        vst = [mem.alloc([128, 4, 512], BF16, f"vst{i}") for i in range(2)]
        kTp = [mem.alloc([128, 128], BF16, f"kTp{i}") for i in range(2)]
        qzs = mem.alloc([128, 4, NS, 64], BF16, "qzs")
        self.memset("dve", qzs.v.re("p a b c -> p (a b c)"), 0.0)
        kTs = mem.alloc([128, 4, 128], BF16, "kTs")
        sgs = mem.alloc([128, 4, 128], BF16, "sgs")
        kp4 = [mem.alloc([128, 512], BF16, f"kp4{i}") for i in range(2)]
        zt = mem.alloc([128, 256], BF16, "zt")
        self.memset("dve", zt.v, 0.0)
        get = self.fm_stream(l, [[FB_Q, FB_K, FB_G][k % 3] + k // 3 for k in range(12)], depth=3)
        pb = [self.bankt(0), self.bankt(1)]
        zA = [self.bankt(2), self.bankt(3)]
        zB = [self.bankt(4), self.bankt(5)]
        bO = self.bankt(6)
        bT = self.bankt(7, bf=True)
        am = self.cb[:, CB_AM:CB_AM + 2048].re("p (a b) -> p a b", a=4, b=512)
        sm = self.cb[0:32, CB_SM:CB_SM + 256]
        latt = self.cbv(CB_LATT)
        neg1 = self.cbv(CB_NEG1)
        cnt = [0]
        for hp in range(4):

            def ev_q(g, c0, n, ps):
                self.act(qT[:, c0:c0 + n], ps, AF.Copy, scale=0.125)

            def ev_k(g, c0, n, ps):
                self.cp("dve", kT[:, c0:c0 + n], ps)

            def ev_g(g, c0, n, ps):
                self.act(sg[:, c0:c0 + n], ps, AF.Silu)
            self.fm_block(get(3 * hp), pb, ev_q)
            self.fm_block(get(3 * hp + 1), pb, ev_k)
            self.fm_block(get(3 * hp + 2), pb, ev_g)
            self.chk("B_fm")
            its = []
            for QS in range(4):
                for hh in range(2):
                    kbs = list(range(4 * QS + 3, -1, -1))
                    for n_, kb in enumerate(kbs):
                        its.append((QS, hh, kb, n_ == 0, hh == 1 and kb == 0))

            def p_stage1(it, s):
                QS, hh, kb, first, lastq = it
                q0, po = QS * 512, 64 * hh
                lk = kT[po:po + 64, kb * 128:(kb + 1) * 128]
                rq = qT[po:po + 64, q0:q0 + 512]
                self.mm(zA[s].v, lk, rq)
                self.act(e_sb[s].v, zA[s].v, AF.Exp)
                self.act(sp_[s].v, e_sb[s].v, AF.Ln, bias=1.0)
                dl = kb - 4 * QS
                if dl >= 0:
                    self.tt("pool", sp_[s].v, sp_[s].v, am[:, dl, :], ALU.mult)

            def p_stage2(it, s, hp=hp):
                QS, hh, kb, first, lastq = it
                q0, po = QS * 512, 64 * hh
                head = 2 * hp + hh
                lk = kT[po:po + 64, kb * 128:(kb + 1) * 128]
                rq = qT[po:po + 64, q0:q0 + 512]
                dl = kb - 4 * QS
                self.mm(zB[s].v, lk, rq, start=True, stop=False)
                self.mm(zB[s].v, latt, sp_[s].v, start=False, stop=first)
                if not first:
                    self.mm(zB[s].v, neg1, Sl.v, start=False, stop=True)
                self.act(Wt[s].v, zB[s].v, AF.Exp)
                if dl >= 0:
                    self.tt("pool", Wt[s].v, Wt[s].v, am[:, dl, :], ALU.mult)
                self.mm(bO[po:po + 64, :], vtok[:, kb, head * 64:(head + 1) * 64], Wt[s].v, start=first, stop=(kb == 0))
                if kb > 0:
                    if first:
                        self.cp("dve", Sl.v, sp_[s].v)
                    else:
                        self.tt("dve", Sl.v, Sl.v, sp_[s].v, ALU.add)
                if lastq:
                    yb = yTt[QS % 2]
                    self.tt("dve", yb.v, bO.v, sg[:, q0:q0 + 512], ALU.mult)
                    for ii in range(4):
                        i = 4 * QS + ii
                        self.P.dma("sp", self.ytd[4 + hp, :, i * 128:(i + 1) * 128], yb.ap[:, ii * 128:(ii + 1) * 128],
                                   reads=[yb], writes=[self.ytt[1][i]])
            base = cnt[0]
            for idx in range(len(its) + 1):
                if idx < len(its):
                    p_stage1(its[idx], (base + idx) % 2)
                if idx >= 1:
                    p_stage2(its[idx - 1], (base + idx - 1) % 2)
            cnt[0] += len(its)
            self.chk("B_P")
            for j in range(NS):
                cq = LP + 32 * j
                self.cp("dve", qzs[0:64, hp, j, 0:32], qT[0:64, cq:cq + 32])
                self.cp("dve", qzs[64:128, hp, j, 32:64], qT[64:128, cq:cq + 32])
            self.cp("dve", kTs[:, hp, :], kT[:, LP:T])
            self.cp("dve", sgs[:, hp, :], sg[:, LP:T])
        sm = self.cb[0:32, CB_SM:CB_SM + 256]
        neg32 = self.cb[0:32, CB_NEG1:CB_NEG1 + 128]
        latt32 = self.cbv(CB_LATT, 32, 32)
        Snew = Sl[0:32, 0:256]
        Slp = Sl[:, 256:512]
        for j in range(NS):
            s = cnt[0] % 2
            cnt[0] += 1
            for hp in range(4):
                self.mm(zA[s][0:32, hp * 64:(hp + 1) * 64], kTs[:, hp, 32 * j:32 * j + 32], qzs[:, hp, j, :])
            self.act(e_sb[s][0:32, 0:256], zA[s][0:32, 0:256], AF.Exp)
            self.act(sp_[s][0:32, 0:256], e_sb[s][0:32, 0:256], AF.Ln, bias=1.0)
            self.tt("pool", sp_[s][0:32, 0:256], sp_[s][0:32, 0:256], sm, ALU.mult)
            self.mm(zB[s][0:32, 0:256], zt[:, 0:32], zt.v, start=True, stop=False)
            for hp in range(4):
                self.mm(zB[s][0:32, hp * 64:(hp + 1) * 64], kTs[:, hp, 32 * j:32 * j + 32], qzs[:, hp, j, :], start=False, stop=False)
            self.mm(zB[s][0:32, 0:256], latt32, sp_[s][0:32, 0:256], start=False, stop=True)
            self.act(Wt[s][0:32, 0:256], zB[s][0:32, 0:256], AF.Exp)
            self.tt("pool", Wt[s][0:32, 0:256], Wt[s][0:32, 0:256], sm, ALU.mult)
            for hh in range(2):
                self.mm(bO[64 * hh:64 * hh + 64, 0:128], zt[:, 0:64], zt[:, 0:128], start=True, stop=False)
            for head in range(8):
                hp, hh = head // 2, head % 2
                self.mm(bO[64 * hh:64 * hh + 64, hp * 32:(hp + 1) * 32], vs[0:32, j, head * 64:(head + 1) * 64],
                        Wt[s][0:32, head * 32:(head + 1) * 32], start=False, stop=False)
            self.cp("dve", Snew, sp_[s][0:32, 0:256])
            blks = [(kq, a_) for kq in range(7, -1, -1) for a_ in range(3, -1, -1)]

            def s_stage1(n, s, j=j):
                kq, a_ = blks[n]
                ks_, vs_ = kst[kq % 2], vst[kq % 2]
                if a_ == 3:
                    self.dma("pool", ks_.v, I["ck"][l, j, kq * 512:(kq + 1) * 512, :].re("(a p) c -> p a c", a=4, p=128))
                    self.dma("pool", vs_.v, I["cv"][l, j, kq * 512:(kq + 1) * 512, :].re("(a p) c -> p a c", a=4, p=128))
                kp = kp4[s]
                for hp in range(4):
                    self.tr(bT[:, hp * 128:(hp + 1) * 128], ks_[:, a_, hp * 128:(hp + 1) * 128], identb)
                self.cp("dve", kp.v, bT[:, 0:512])
                for hp in range(4):
                    self.mm(zA[s][:, hp * 64:(hp + 1) * 64], kp[:, hp * 128:(hp + 1) * 128], qzs[:, hp, j, :])
                self.act(e_sb[s][:, 0:256], zA[s][:, 0:256], AF.Exp)
                self.act(sp_[s][:, 0:256], e_sb[s][:, 0:256], AF.Ln, bias=1.0)

            def s_stage2(n, s, j=j):
                kq, a_ = blks[n]
                vs_ = vst[kq % 2]
                kp = kp4[s]
                self.mm(zB[s][:, 0:256], latt, sp_[s][:, 0:256], start=True, stop=False)
                for hp in range(4):
                    self.mm(zB[s][:, hp * 64:(hp + 1) * 64], kp[:, hp * 128:(hp + 1) * 128], qzs[:, hp, j, :], start=False, stop=False)
                self.mm(zB[s][:, 0:256], neg32, Snew, start=False, stop=(n == 0))
                if n > 0:
                    self.mm(zB[s][:, 0:256], neg1, Slp, start=False, stop=True)
                self.act(Wt[s][:, 0:256], zB[s][:, 0:256], AF.Exp)
                lastb = (n == len(blks) - 1)
                for head in range(8):
                    hp, hh = head // 2, head % 2
                    self.mm(bO[64 * hh:64 * hh + 64, hp * 32:(hp + 1) * 32], vs_[:, a_, head * 64:(head + 1) * 64],
                            Wt[s][:, head * 32:(head + 1) * 32], start=False, stop=lastb)
                if not lastb:
                    if n == 0:
                        self.cp("dve", Slp, sp_[s][:, 0:256])
                    else:
                        self.tt("dve", Slp, Slp, sp_[s][:, 0:256], ALU.add)
            base = cnt[0]
            for idx in range(len(blks) + 1):
                if idx < len(blks):
                    s_stage1(idx, (base + idx) % 2)
                if idx >= 1:
                    s_stage2(idx - 1, (base + idx - 1) % 2)
            cnt[0] += len(blks)
            ys_ = yTt[j % 2]
            self.tt("dve", ys_[:, 0:128].re("p (a b) -> p a b", a=4, b=32), bO[:, 0:128].re("p (a b) -> p a b", a=4, b=32),
                    sgs[:, :, 32 * j:32 * j + 32], ALU.mult)
            self.P.dma("sp", self.ytd[4:8, :, LP + 32 * j:LP + 32 * j + 32].rearrange("c p t -> p c t"),
                       ys_.ap[:, 0:128].rearrange("p (a b) -> p a b", a=4, b=32), reads=[ys_], writes=[self.ytt[1][16]])
            self.chk("B_S0")
        self.P.barrier()
        mem.pop()
        self.phase_end()

    def mixer_C(self, l):
        I, O = self.I, self.O
        mem = self.mem
        self.phase_begin()
        if "A" not in self.mixers:
            self.small_proj(l)
        xc = mem.alloc([128, 4, T], BF16, "xc")
        xcv = mem.alloc([128, 4, T], BF16, "xcv")
        szT = mem.alloc([128, 4, T], BF16, "szT")
        identb = self.cbv(CB_ID)
        mem.push()
        xins = [mem.alloc([128, 3 + LP], F32, f"xin{i}") for i in range(2)]
        xin_ss = [mem.alloc([128, NS, 3 + LS], F32, f"xin_s{i}") for i in range(2)]
        accs = [mem.alloc([128, LP], F32, f"acc{i}") for i in range(2)]
        acc_ss = [mem.alloc([128, NS, LS], F32, f"acc_s{i}") for i in range(2)]
        cst = mem.alloc([12, 512], F32, "cst")
        self.dma("sp", cst.v, I["scc"][l])
        for x_ in xins:
            self.memset("dve", x_[:, 0:3], 0.0)
        get = self.fm_stream(l, list(range(FB_XC, FB_XC + 8)))
        pb = [self.bankt(i) for i in range(4)]
        sb_ = self.bankt(6)
        for b in range(8):
            w = get(b)
            xin, xin_s, acc, acc_s = xins[b % 2], xin_ss[b % 2], accs[b % 2], acc_ss[b % 2]
            if b < 4:
                self.tr(sb_[:, 0:12], cst[0:12, b * 128:(b + 1) * 128], self.cfv(CF_ID, 12, 12))
                self.cp("dve", xin_s[:, :, 0:3], sb_[:, 0:12].re("p (a b) -> p a b", a=NS, b=3))

                def evac(g, c0, n, ps, b=b):
                    if g < 4:
                        self.cp("act", xin[:, 3 + c0:3 + c0 + n], ps)
                        self.cp("dve", xc[:, b, c0:c0 + n], xin[:, 3 + c0:3 + c0 + n])
                    else:
                        self.cp("act", xin_s[:, :, 3:3 + LS], ps.re("p (a b) -> p a b", a=NS, b=LS))
                        self.cp("dve", xc[:, b, c0:c0 + n].re("p (a b) -> p a b", a=NS, b=LS), xin_s[:, :, 3:3 + LS])
                self.fm_block(w, pb, evac)
                self.dma("sp", O["ccp"][l, :, b * 128:(b + 1) * 128].re("k f -> f k"), xin[:, LP:LP + 3], allow_slow_non_contiguous=True)
                for j in range(NS):
                    self.dma("sp", O["ccs"][l, j, :, b * 128:(b + 1) * 128].re("k f -> f k"), xin_s[:, j, LS:LS + 3], allow_slow_non_contiguous=True)
                self.conv4(xin, xin_s, acc, acc_s, PP_CCW + 4 * b, PP_CCB + b)
                self.act(xcv[:, b, 0:LP], acc.v, AF.Silu)
                self.act(xcv[:, b, LP:T].re("p (a b) -> p a b", a=NS, b=LS), acc_s.v, AF.Silu)
            else:
                def evac(g, c0, n, ps, b=b):
                    self.act(szT[:, b - 4, c0:c0 + n], ps, AF.Silu)
                self.fm_block(w, pb, evac)
        self.P.barrier()
        mem.pop()
        self.chk("C_a")
        mem.push()
        wqkv = mem.alloc([128, 3, 4, 128], BF16, "wqkv")
        for k in range(3):
            self.dma("pool", wqkv[:, k], I["wqkv"][l, k].re("p (h e) -> p h e", h=4, e=128))
        ip = mem.alloc([128, 20, 4], F32, "ip")
        lf = mem.alloc([128, 20, 4], F32, "lf")
        rs = self.rs
        self.tt("dve", ip.v, self.small[:, :, 8:12], rs[:, 24:28].un(1).bc([128, 20, 4]), ALU.add)
        self.tt("dve", lf.v, self.small[:, :, 12:16], rs[:, 28:32].un(1).bc([128, 20, 4]), ALU.add)
        self.act(lf.v, lf.v, AF.Exp, scale=-1.0)
        self.act(lf.v, lf.v, AF.Ln, bias=1.0)
        self.ts("dve", lf.v, lf.v, -1.0, ALU.mult)
        W = {}
        for nm, shp, dty in [("A", [128, 4, 128], F32), ("Bm", [128, 4, 128], F32), ("E", [128, 4, 128], F32),
                             ("w", [128, 4, 128], BF16), ("wT", [128, 4, 128], BF16), ("qT", [128, 4, 128], BF16),
                             ("kT", [128, 4, 128], BF16), ("v", [128, 4, 128], BF16), ("vw", [128, 4, 130], BF16),
                             ("ktok", [128, 4, 128], BF16), ("tmp", [128, 4, 130], F32), ("hh", [128, 4, 128], F32),
                             ("hn", [128, 4, 128], F32), ("hnT", [128, 4, 128], F32), ("y1", [128, 4, 128], F32),
                             ("yT", [128, 4, 128], BF16), ("Cn", [128, 4, 130], F32), ("Cnb", [128, 4, 130], BF16),
                             ("cin", [128, 4, 128], F32), ("cout", [128, 4, 128], F32),
                             ("bsb", [128, 4], F32), ("rowmax", [128, 4], F32), ("mx", [128, 4], F32),
                             ("negmx", [128, 4], F32), ("mmb", [128, 4], F32), ("rsum", [128, 4], F32),
                             ("g", [128, 4], F32), ("nq", [128, 4], F32), ("emt", [128, 4], F32), ("den", [128, 4], F32),
                             ("vals", [128, 8], F32), ("lastb", [128, 8], F32), ("ws", [128, 4], F32), ("gl", [128, 4], F32),
                             ("t4", [128, 4], F32), ("bst", [128, 4, 6], F32), ("mvv", [128, 4, 2], F32), ("rsd", [128, 4], F32),
                             ("t4b", [128, 4], F32)]:
            W[nm] = mem.alloc(shp, dty, nm)
        b1 = self.banks[1]
        bk = {"P2": self.bankt(0), "pq": self.bankt(2), "pk": self.bankt(3), "pv": self.bankt(4), "pkt": self.bankt(5),
              "S": self.bankt(6), "num": self.bankt(7)}
        bk["ms"] = Tile(b1[:, 0:8], "ms")
        bk["lb"] = Tile(b1[:, 8:16], "lb")
        bk["wTp"] = Tile(b1[:, 256:512].bitcast(BF16), "wTp")
        for (kind, j, Q, chunks, c00) in self.seqs():
            Cn, Cnb, mmb = W["Cn"], W["Cnb"], W["mmb"]
            if kind == "P":
                self.memset("dve", Cn.v, 0.0)
                self.memset("pool", Cnb.v, 0.0)
                self.memset("dve", mmb.v, 0.0)
            else:
                self.dma("sp", W["cin"].v, I["smc"][l, j].re("h v d -> v h d"))
                for h in range(4):
                    self.tr(bk["num"][:, h * 128:(h + 1) * 128], W["cin"][:, h, :], self.cfv(CF_ID))
                self.cp("dve", Cn[:, :, 0:128], bk["num"].v.re("p (a b) -> p a b", a=4, b=128))
                self.dma("sp", Cn[:, :, 128:129], I["smn"][l, j].re("h (d o) -> d h o", o=1), allow_slow_non_contiguous=True)
                self.dma("sp", mmb.v, I["smm"][l, j:j + 1, :].bc([128, 4]))
                self.cp("act", Cnb.v, Cn.v)
            if kind == "S":
                self.chk("C_S0in")
            for c in chunks:
                self.ml_chunk(l, c, Q, W, bk, xc, xcv, szT, wqkv, ip, lf)
                self.chk("C_c1")
            if kind == "S":
                self.chk("C_S0")
            self.chk("C_P")
            for h in range(4):
                self.tr(bk["num"][:, h * 128:(h + 1) * 128], Cn[:, h, 0:128], self.cfv(CF_ID))
            self.cp("dve", W["cout"].v, bk["num"].v.re("p (a b) -> p a b", a=4, b=128))
            if kind == "P":
                dc, dn, dm = O["mcp"][l], O["mnp"][l], O["mmp"][l:l + 1, :]
            else:
                dc, dn, dm = O["mcs"][l, j], O["mns"][l, j], O["mms"][l, j:j + 1, :]
            self.dma("sp", dc.re("h v d -> v h d"), W["cout"].v)
            self.dma("sp", dn.re("h (d o) -> d h o", o=1), Cn[:, :, 128:129], allow_slow_non_contiguous=True)
            self.dma("sp", dm, mmb[0:1, :])
            self.chk("C_Pout")
        self.P.barrier()
        mem.pop()
        self.phase_end()

    def ml_chunk(self, l, c, Q, W, bk, xc, xcv, szT, wqkv, ip, lf):
        c0 = self.ccol(c)
        ident = self.cfv(CF_ID, Q, Q)
        tri = self.cfv(CF_TRI, Q, Q)
        ones = self.cfv(CF_ONE, Q, Q)
        neg1 = self.cfv(CF_NEG1, Q, Q)
        identb = self.cbv(CB_ID, Q, Q)
        nmT4 = self.cb[0:Q, CB_NMT4:CB_NMT4 + 512].re("p (a b) -> p a b", a=4, b=128)[:, :, 0:Q]
        sel = self.cf[0:Q, (CF_SEL128 if Q == 128 else CF_SEL32):(CF_SEL128 if Q == 128 else CF_SEL32) + 128]
        A, Bm, E, w, wT = W["A"], W["Bm"], W["E"], W["w"], W["wT"]
        DHS = 128 ** -0.5
        self.tt("dve", A[0:Q, :, 0:Q], ident.un(1).bc([Q, 4, Q]), ip[0:Q, c, :].un(2).bc([Q, 4, Q]), ALU.mult)
        self.tt("pool", Bm[0:Q, :, 0:Q], tri.un(1).bc([Q, 4, Q]), lf[0:Q, c, :].un(2).bc([Q, 4, Q]), ALU.mult)
        P2 = bk["P2"][0:Q, 0:4 * Q].re("p (a b) -> p a b", a=4, b=Q)
        self.mm(P2, ones, A[0:Q, :, 0:Q], start=True, stop=False)
        self.mm(P2, neg1, Bm[0:Q, :, 0:Q], start=False, stop=False)
        self.mm(P2, identb, nmT4, start=False, stop=True)
        self.mm(bk["ms"][0:Q, 0:4], tri, lf[0:Q, c, :])
        self.cp("dve", W["bsb"][0:Q, :], bk["ms"][0:Q, 0:4])
        self.red(W["rowmax"][0:Q, :], P2, ALU.max)
        self.tt("dve", W["mx"][0:Q, :], W["rowmax"][0:Q, :], W["mmb"][0:Q, :], ALU.max)
        self.ts("dve", W["negmx"][0:Q, :], W["mx"][0:Q, :], -1.0, ALU.mult)
        for h in range(4):
            self.act(E[0:Q, h, 0:Q], bk["P2"][0:Q, h * Q:(h + 1) * Q], AF.Exp, bias=W["negmx"][0:Q, h:h + 1])
        self.chk("C_m1")
        for h in range(4):
            self.mm(bk["pq"][:, h * Q:(h + 1) * Q], wqkv[:, 0, h, :], xcv[:, h, c0:c0 + Q])
            self.mm(bk["pk"][:, h * Q:(h + 1) * Q], wqkv[:, 1, h, :], xcv[:, h, c0:c0 + Q])
            self.mm(bk["pv"][0:Q, h * 128:(h + 1) * 128], xc[:, h, c0:c0 + Q], wqkv[:, 2, h, :])
            self.mm(bk["pkt"][0:Q, h * 128:(h + 1) * 128], xcv[:, h, c0:c0 + Q], wqkv[:, 1, h, :])
        self.cp("act", W["qT"][:, :, 0:Q], bk["pq"][:, 0:4 * Q].re("p (a b) -> p a b", a=4, b=Q))
        self.act(W["kT"][:, :, 0:Q], bk["pk"][:, 0:4 * Q].re("p (a b) -> p a b", a=4, b=Q), AF.Copy, scale=DHS)
        self.cp("dve", W["v"][0:Q], bk["pv"][0:Q, :].re("p (a b) -> p a b", a=4, b=128))
        self.act(W["ktok"][0:Q], bk["pkt"][0:Q, :].re("p (a b) -> p a b", a=4, b=128), AF.Copy, scale=DHS)
        for h in range(4):
            self.mm(bk["S"][0:Q, h * Q:(h + 1) * Q], W["qT"][:, h, 0:Q], W["kT"][:, h, 0:Q])
        for h in range(4):
            self.stt(w[0:Q, h, 0:Q], E[0:Q, h, 0:Q], 1.0, bk["S"][0:Q, h * Q:(h + 1) * Q], ALU.mult, ALU.mult,
                     accum=W["rsum"][0:Q, h:h + 1])
        for h in range(4):
            self.tr(bk["wTp"][0:Q, h * Q:(h + 1) * Q], w[0:Q, h, 0:Q], identb)
        self.cp("act", wT[0:Q, :, 0:Q], bk["wTp"][0:Q, 0:4 * Q].re("p (a b) -> p a b", a=4, b=Q))
        for h in range(4):
            self.mm(bk["num"][0:Q, h * 128:(h + 1) * 128], wT[0:Q, h, 0:Q], W["v"][0:Q, h, :])
        for h in range(4):
            ib = bk["pq"] if h < 2 else bk["pk"]
            self.mm(ib[0:Q, (h % 2) * 130:(h % 2) * 130 + 129], W["qT"][:, h, 0:Q], W["Cnb"][:, h, 0:129])
        self.chk("C_m2")
        self.tt("dve", W["t4"][0:Q, :], W["mmb"][0:Q, :], W["mx"][0:Q, :], ALU.subtract)
        self.act(W["g"][0:Q, :], W["t4"][0:Q, :], AF.Exp)
        tmp = W["tmp"]
        for h in range(4):
            ib = bk["pq"] if h < 2 else bk["pk"]
            self.ts("dve", tmp[0:Q, h, 0:129], ib[0:Q, (h % 2) * 130:(h % 2) * 130 + 129], W["g"][0:Q, h:h + 1], ALU.mult)
        self.tt("dve", W["hh"][0:Q], bk["num"][0:Q, :].re("p (a b) -> p a b", a=4, b=128), tmp[0:Q, :, 0:128], ALU.add)
        self.tt("dve", W["nq"][0:Q, :], W["rsum"][0:Q, :], tmp[0:Q, :, 128], ALU.add)
        self.tt("dve", W["t4"][0:Q, :], W["bsb"][0:Q, :], W["mx"][0:Q, :], ALU.add)
        self.act(W["emt"][0:Q, :], W["t4"][0:Q, :], AF.Exp, scale=-1.0)
        self.stt(W["nq"][0:Q, :], W["nq"][0:Q, :], -1.0, W["nq"][0:Q, :], ALU.mult, ALU.max)
        self.tt("dve", W["den"][0:Q, :], W["nq"][0:Q, :], W["emt"][0:Q, :], ALU.max)
        self.recip(W["den"][0:Q, :], W["den"][0:Q, :])
        self.tt("dve", W["hh"][0:Q], W["hh"][0:Q], W["den"][0:Q, :].un(2).bc([Q, 4, 128]), ALU.mult)
        for h in range(4):
            self.bnstats(W["bst"][0:Q, h, :], W["hh"][0:Q, h, :])
            self.bnaggr(W["mvv"][0:Q, h, :], W["bst"][0:Q, h, :])
        self.act(W["t4b"][0:Q, :], W["mvv"][0:Q, :, 1], AF.Ln, bias=EPS)
        self.act(W["rsd"][0:Q, :], W["t4b"][0:Q, :], AF.Exp, scale=-0.5)
        self.tt("dve", W["hn"][0:Q], W["hh"][0:Q], W["mvv"][0:Q, :, 0:1].bc([Q, 4, 128]), ALU.subtract)
        self.tt("dve", W["hn"][0:Q], W["hn"][0:Q], W["rsd"][0:Q, :].un(2).bc([Q, 4, 128]), ALU.mult)
        for h in range(4):
            self.tr(bk["S"][:, h * Q:(h + 1) * Q], W["hn"][0:Q, h, :], ident)
        for h in range(4):
            self.act(W["hnT"][:, h, 0:Q], bk["S"][:, h * Q:(h + 1) * Q], AF.Identity, scale=self.pp[:, PP_NCW + h:PP_NCW + h + 1])
        for h in range(4):
            self.stt(W["y1"][:, h, 0:Q], xcv[:, h, c0:c0 + Q], self.pp[:, PP_SKC + h:PP_SKC + h + 1], W["hnT"][:, h, 0:Q], ALU.mult, ALU.add)
        self.tt("dve", W["yT"][:, :, 0:Q], W["y1"][:, :, 0:Q], szT[:, :, c0:c0 + Q], ALU.mult)
        ti = c if c < 16 else 16
        self.P.dma("sp", self.ytd[8:12, :, c0:c0 + Q].rearrange("c p t -> p c t"), W["yT"].ap[:, :, 0:Q],
                   reads=[W["yT"]], writes=[self.ytt[2][ti]])
        self.chk("C_m3")
        self.cp("dve", W["vals"][0:Q, 0:4], W["t4"][0:Q, :])
        self.cp("dve", W["vals"][0:Q, 4:8], W["bsb"][0:Q, :])
        self.mm(bk["lb"].v, sel, W["vals"][0:Q, :])
        self.cp("dve", W["lastb"].v, bk["lb"].v)
        lb = W["lastb"]
        self.tt("dve", W["t4b"][0:Q, :], lb[0:Q, 4:8], W["bsb"][0:Q, :], ALU.subtract)
        self.tt("dve", W["t4b"][0:Q, :], W["t4b"][0:Q, :], ip[0:Q, c, :], ALU.add)
        self.tt("dve", W["t4b"][0:Q, :], W["t4b"][0:Q, :], lb[0:Q, 0:4], ALU.subtract)
        self.act(W["ws"][0:Q, :], W["t4b"][0:Q, :], AF.Exp)
        self.tt("dve", W["gl"].v, lb[:, 4:8], W["mmb"].v, ALU.add)
        self.tt("dve", W["gl"].v, W["gl"].v, lb[:, 0:4], ALU.subtract)
        self.act(W["gl"].v, W["gl"].v, AF.Exp)
        vw = W["vw"]
        self.tt("dve", vw[0:Q, :, 0:128], W["v"][0:Q], W["ws"][0:Q, :].un(2).bc([Q, 4, 128]), ALU.mult)
        self.cp("dve", vw[0:Q, :, 128], W["ws"][0:Q, :])
        for h in range(4):
            ib = bk["pv"] if h < 2 else bk["pkt"]
            self.mm(ib[:, (h % 2) * 130:(h % 2) * 130 + 129], W["ktok"][0:Q, h, :], vw[0:Q, h, 0:129])
        Cn = W["Cn"]
        for h in range(4):
            ib = bk["pv"] if h < 2 else bk["pkt"]
            self.stt(Cn[:, h, 0:129], Cn[:, h, 0:129], W["gl"][:, h:h + 1], ib[:, (h % 2) * 130:(h % 2) * 130 + 129], ALU.mult, ALU.add)
        self.cp("act", W["Cnb"].v, Cn.v)
        self.cp("dve", W["mmb"].v, lb[:, 0:4])

    def mixer_D(self, l):
        I, O = self.I, self.O
        mem = self.mem
        self.phase_begin()
        pp = self.pp
        sgT = mem.alloc([128, 4, T], BF16, "sgT")
        yd = mem.alloc([128, 4, T], F32, "yd")
        glus = [mem.alloc([128, 30 + LP], F32, f"glu{i}") for i in range(2)]
        glu_ss = [mem.alloc([128, NS, 30 + LS], F32, f"glu_s{i}") for i in range(2)]
        sig = [mem.alloc([128, 512], F32, f"sig{i}") for i in range(2)]
        cst = mem.alloc([30, NS, 512], F32, "cst")
        cdo = mem.alloc([32, 5, 512], F32, "cdo")
        self.dma("sp", cst.v, I["scd"][l].re("j k f -> k j f"))
        for g_ in glus:
            self.memset("dve", g_[:, 0:30], 0.0)
        get = self.fm_stream(l, [[FB_AD, FB_BD, FB_GD][k % 3] + k // 3 for k in range(12)])
        pb = [self.bankt(i) for i in range(4)]
        sb_ = self.bankt(6)
        tb_ = self.bankt(7)
        ident = self.cfv(CF_ID)
        for b in range(4):
            glu, glu_s = glus[b % 2], glu_ss[b % 2]
            for j in range(NS):
                self.tr(sb_[:, j * 32:j * 32 + 30], cst[0:30, j, b * 128:(b + 1) * 128], self.cfv(CF_ID, 30, 30))
            self.cp("dve", glu_s[:, :, 0:30], sb_[:, 0:128].re("p (a b) -> p a b", a=NS, b=32)[:, :, 0:30])

            def ev_a(g, c0, n, ps):
                if g < 4:
                    self.cp("act", glu[:, 30 + c0:30 + c0 + n], ps)
                else:
                    self.cp("act", glu_s[:, :, 30:30 + LS], ps.re("p (a b) -> p a b", a=NS, b=LS))

            def ev_b(g, c0, n, ps):
                s_ = sig[g % 2]
                self.act(s_[:, 0:n], ps, AF.Sigmoid)
                if g < 4:
                    self.tt("dve", glu[:, 30 + c0:30 + c0 + n], glu[:, 30 + c0:30 + c0 + n], s_[:, 0:n], ALU.mult)
                else:
                    self.tt("dve", glu_s[:, :, 30:30 + LS], glu_s[:, :, 30:30 + LS],
                            s_[:, 0:n].re("p (a b) -> p a b", a=NS, b=LS), ALU.mult)

            def ev_g(g, c0, n, ps, b=b):
                self.act(sgT[:, b, c0:c0 + n], ps, AF.Silu)
            self.fm_block(get(3 * b), pb, ev_a)
            self.fm_block(get(3 * b + 1), pb, ev_b)
            self.fm_block(get(3 * b + 2), pb, ev_g)
            self.tr(tb_[0:32, 0:128], glu[:, 30 + LP - 32:30 + LP], ident)
            for j in range(3):
                self.tr(tb_[0:32, (j + 1) * 128:(j + 2) * 128], glu_s[:, j, 30:30 + LS], ident)
            self.tr(sb_[0:32, 256:384], glu_s[:, 3, 30:30 + LS], ident)
            self.cp("dve", cdo[:, 0:4, b * 128:(b + 1) * 128], tb_[0:32, 0:512].re("p (a b) -> p a b", a=4, b=128))
            self.cp("dve", cdo[:, 4, b * 128:(b + 1) * 128], sb_[0:32, 256:384])
            w0 = PP_CDW + 31 * b
            ydp = yd[:, b, 0:LP]
            yds = yd[:, b, LP:T].re("p (a b) -> p a b", a=NS, b=LS)
            self.ts("dve", ydp, glu[:, 0:LP], pp[:, w0:w0 + 1], ALU.mult, pp[:, PP_CDB + b:PP_CDB + b + 1], ALU.add)
            self.ts("dve", yds, glu_s[:, :, 0:LS], pp[:, w0:w0 + 1], ALU.mult, pp[:, PP_CDB + b:PP_CDB + b + 1], ALU.add)
            for k in range(1, 31):
                self.stt(ydp, glu[:, k:k + LP], pp[:, w0 + k:w0 + k + 1], ydp, ALU.mult, ALU.add)
                self.stt(yds, glu_s[:, :, k:k + LS], pp[:, w0 + k:w0 + k + 1], yds, ALU.mult, ALU.add)
        self.dma("sp", O["cdp"][l], cdo[2:32, 0, :])
        for j in range(NS):
            self.dma("sp", O["cds"][l, j], cdo[2:32, 1 + j, :])
        self.P.barrier()
        if self.mixers.endswith("D"):
            off = self.uT_off // 2
            wpre = Tile(mem.t[0:128, off:off + KC * D].rearrange("p (k c) -> p k c", k=KC, c=D), "wout_pre")
            for q in range(4):
                self.dma("pool", wpre[:, 4 * q:4 * q + 4, :], I["wout"][l, :, 4 * q * D:(4 * q + 4) * D].re("p (k c) -> p k c", k=4, c=D))
            self.wout_pre = wpre
        mem.push()
        sq = [mem.alloc([128, 512], F32, f"sq{i}") for i in range(2)]
        mean = mem.alloc([128, 512], F32, "mean")
        rstd = mem.alloc([128, 512], F32, "rstd")
        t1 = [mem.alloc([128, 512], F32, f"t1{i}") for i in range(2)]
        yT = [mem.alloc([128, 4, 512], BF16, f"yTd{i}") for i in range(2)]
        ones = self.cfv(CF_ONE)
        bm_ = [self.bankt(0), self.bankt(1)]
        bq_ = [self.bankt(2), self.bankt(3)]
        for g, (c0, n) in enumerate(GROUPS):
            s = g % 2
            for b in range(4):
                self.mm(bm_[s][:, 0:n], ones, yd[:, b, c0:c0 + n], start=(b == 0), stop=(b == 3))
            for b in range(4):
                self.act(sq[b % 2][:, 0:n], yd[:, b, c0:c0 + n], AF.Square)
                self.mm(bq_[s][:, 0:n], ones, sq[b % 2][:, 0:n], start=(b == 0), stop=(b == 3))
            self.ts("dve", mean[:, 0:n], bm_[s][:, 0:n], 1.0 / 512, ALU.mult)
            self.tt("dve", rstd[:, 0:n], mean[:, 0:n], mean[:, 0:n], ALU.mult)
            self.stt(rstd[:, 0:n], bq_[s][:, 0:n], 1.0 / 512, rstd[:, 0:n], ALU.mult, ALU.subtract)
            self.act(rstd[:, 0:n], rstd[:, 0:n], AF.Ln, bias=EPS)
            self.act(rstd[:, 0:n], rstd[:, 0:n], AF.Exp, scale=-0.5)
            for b in range(4):
                t_ = t1[b % 2]
                self.tt("dve", t_[:, 0:n], yd[:, b, c0:c0 + n], mean[:, 0:n], ALU.subtract)
                self.tt("dve", t_[:, 0:n], t_[:, 0:n], rstd[:, 0:n], ALU.mult)
                self.act(t_[:, 0:n], t_[:, 0:n], AF.Silu, scale=pp[:, PP_LDG + b:PP_LDG + b + 1], bias=pp[:, PP_LDB + b:PP_LDB + b + 1])
                self.tt("pool", yT[s][:, b, 0:n], t_[:, 0:n], sgT[:, b, c0:c0 + n], ALU.mult)
            for ii in range(n // 128):
                i = c0 // 128 + ii
                self.P.dma("sp", self.ytd[12:16, :, i * 128:(i + 1) * 128].rearrange("c p t -> p c t"),
                           yT[s].ap[:, :, ii * 128:(ii + 1) * 128], reads=[yT[s]], writes=[self.ytt[3][i]])
        self.P.barrier()
        mem.pop()
        self.phase_end()


_CACHE = {}


def _prep_weights(inp):
    L = DEPTH
    f = np.float32
    w_mod = inp["w_mod"]
    wmod = np.ascontiguousarray(w_mod.reshape(L, KC, 128, 12, 512).transpose(0, 3, 2, 1, 4)).reshape(L, 12, 128, KC * 512)
    bmod = np.ascontiguousarray(np.broadcast_to(inp["b_mod"][:, None, :], (L, 5, 6144))).astype(f)
    w_in = inp["w_in"].reshape(L, KC, 128, 6160)
    wfm = np.empty((L, 40, 128, KC * 128), f)
    for b, c0 in enumerate(FM_COLS):
        wfm[:, b] = w_in[:, :, :, c0:c0 + 128].transpose(0, 2, 1, 3).reshape(L, 128, KC * 128)
    wtm = np.empty((L, 3, 128, KC * 512), f)
    for b, c0 in enumerate(TM_COLS):
        wtm[:, b] = w_in[:, :, :, c0:c0 + 512].transpose(0, 2, 1, 3).reshape(L, 128, KC * 512)
    sm = np.concatenate([w_in[..., 1536:1544], w_in[..., 4616:4624]], axis=-1)
    wsm = np.ascontiguousarray(sm.transpose(0, 2, 1, 3)).reshape(L, 128, KC * 16)
    wout = np.ascontiguousarray(inp["w_out"].reshape(L, KC, 128, D).transpose(0, 2, 1, 3)).reshape(L, 128, KC * D)
    wqkv = np.stack([inp["wq_c"], inp["wk_c"], inp["wv_c"]], axis=1)
    wqkv = np.ascontiguousarray(wqkv.transpose(0, 1, 3, 2, 4)).reshape(L, 3, 128, 512)
    pp = np.zeros((L, 128, NPP), f)
    caw = inp["conv_a_w"].reshape(L, 4, 8, 128)
    pp[:, :, PP_CAW:PP_CAW + 32] = caw.transpose(0, 3, 2, 1).reshape(L, 128, 32)
    pp[:, :, PP_CAB:PP_CAB + 8] = inp["conv_a_b"].reshape(L, 8, 128).transpose(0, 2, 1)
    ccw = inp["conv_c_w"].reshape(L, 4, 4, 128)
    pp[:, :, PP_CCW:PP_CCW + 16] = ccw.transpose(0, 3, 2, 1).reshape(L, 128, 16)
    pp[:, :, PP_CCB:PP_CCB + 4] = inp["conv_c_b"].reshape(L, 4, 128).transpose(0, 2, 1)
    cdw = inp["conv_d_w"].reshape(L, 31, 4, 128)
    pp[:, :, PP_CDW:PP_CDW + 124] = cdw.transpose(0, 3, 2, 1).reshape(L, 128, 124)
    pp[:, :, PP_CDB:PP_CDB + 4] = inp["conv_d_b"].reshape(L, 4, 128).transpose(0, 2, 1)
    pp[:, :, PP_LDG:PP_LDG + 4] = inp["ln_d_g"].reshape(L, 4, 128).transpose(0, 2, 1)
    pp[:, :, PP_LDB:PP_LDB + 4] = inp["ln_d_b"].reshape(L, 4, 128).transpose(0, 2, 1)
    pp[:, :, PP_NAW:PP_NAW + 4] = inp["norm_a_w"].reshape(L, 4, 128).transpose(0, 2, 1)
    pp[:, :, PP_NCW:PP_NCW + 4] = inp["norm_c_w"].reshape(L, 4, 128).transpose(0, 2, 1)
    pp[:, :, PP_SKC:PP_SKC + 4] = inp["skip_c"].reshape(L, 4, 128).transpose(0, 2, 1)
    rp = np.zeros((L, 128, NRP), f)
    rp[:, :, RP_LNG:RP_LNG + 2048] = inp["ln_g"][:, None, :]
    rp[:, :, RP_LNB:RP_LNB + 2048] = inp["ln_b"][:, None, :]
    rp[:, :, RP_DTB:RP_DTB + 8] = inp["dt_bias"][:, None, :]
    rp[:, :, RP_ALOG:RP_ALOG + 8] = inp["a_log"][:, None, :]
    rp[:, :, RP_DSK:RP_DSK + 8] = inp["d_skip"][:, None, :]
    rp[:, :, RP_IGB:RP_IGB + 4] = inp["ig_bias"][:, None, :]
    rp[:, :, RP_FGB:RP_FGB + 4] = inp["fg_bias"][:, None, :]
    cf, cb = make_consts()
    return dict(wmod=wmod, bmod=bmod, wfm=wfm, wtm=wtm, wsm=wsm, wout=wout, wqkv=wqkv, ppack=pp, rpack=rp, cf=cf, cb=cb)


def _core_inputs(inp, shared, c):
    f = np.float32
    p = c % 4
    s0 = 4 * c
    m = dict(shared)
    m["xp"] = np.ascontiguousarray(inp["x_prompt"][p])
    m["xs"] = np.ascontiguousarray(inp["x_sample"][s0:s0 + 4]).reshape(128, D)
    m["ck"] = np.ascontiguousarray(inp["cache_k"][:, s0:s0 + 4]).reshape(DEPTH, NS, PAST, 512)
    m["cv"] = np.ascontiguousarray(inp["cache_v"][:, s0:s0 + 4]).reshape(DEPTH, NS, PAST, 512)
    m["sca"] = np.ascontiguousarray(inp["state_conv_a"][:, s0:s0 + 4]).reshape(DEPTH, NS * 3, 1024)
    m["sssm"] = np.ascontiguousarray(inp["state_ssm"][:, s0:s0 + 4]).reshape(DEPTH, NS, 512, 128)
    m["scc"] = np.ascontiguousarray(inp["state_conv_c"][:, s0:s0 + 4]).reshape(DEPTH, NS * 3, 512)
    m["smc"] = np.ascontiguousarray(inp["state_mlstm_c"][:, s0:s0 + 4])
    m["smn"] = np.ascontiguousarray(inp["state_mlstm_n"][:, s0:s0 + 4])
    m["smm"] = np.ascontiguousarray(inp["state_mlstm_m"][:, s0:s0 + 4])
    m["scd"] = np.ascontiguousarray(inp["state_conv_d"][:, s0:s0 + 4])
    call = np.concatenate([inp["c_prompt"][p:p + 1], inp["c_sample"][s0:s0 + 4]], axis=0)
    m["cT"] = np.ascontiguousarray(call.reshape(5, KC, 128).transpose(2, 1, 0)).reshape(128, KC * 5).astype(f)
    return m


def build_nc(nlayers=DEPTH, mixers="ABCD", stop=None):
    key = (nlayers, mixers)
    if key not in _CACHE:
        b = Builder(nlayers, mixers, stop)
        with b.stack:
            nc = b.build()
        _CACHE[key] = (nc, b.P.stats, b.mem.peak)
    return _CACHE[key]


def run(inp, nlayers=DEPTH, mixers="ABCD", ncores=8, stop=None, core0=0, trace=False):
    nc, stats, peak = build_nc(nlayers, mixers, stop)
    L = nlayers
    shared = _prep_weights(inp)
    for k in list(shared.keys()):
        if k not in ("cf", "cb"):
            shared[k] = np.ascontiguousarray(shared[k][:L])
    in_maps = []
    for c in range(ncores):
        m = _core_inputs(inp, shared, c)
        for k in ("ck", "cv", "sca", "sssm", "scc", "smc", "smn", "smm", "scd"):
            m[k] = np.ascontiguousarray(m[k][:L])
        in_maps.append(m)
    if trace:
        res = run_bass_kernel_spmd(nc, in_maps, core_ids=[core0 + i for i in range(ncores)], trace=True)
        print("EXEC_NS", getattr(res, "exec_time_ns", None))
    else:
        res = run_bass_kernel_spmd(nc, in_maps, core_ids=[core0 + i for i in range(ncores)])
    R = res.results
    f = np.float32
    npc = min(4, ncores)

    y_prompt = np.stack([R[c]["yp"] for c in range(npc)], axis=0)
    y_sample = np.concatenate([R[c]["ys"].reshape(NS, LS, D) for c in range(ncores)], axis=0)
    nkp = np.stack([R[c]["nkp"] for c in range(npc)], axis=1).reshape(L, npc, LP, 8, 64)
    nvp = np.stack([R[c]["nvp"] for c in range(npc)], axis=1).reshape(L, npc, LP, 8, 64)
    cap = np.stack([R[c]["cap"] for c in range(npc)], axis=1)
    ssmp = np.stack([R[c]["ssmp"] for c in range(npc)], axis=1).reshape(L, npc, 8, 64, 128)
    ccp = np.stack([R[c]["ccp"] for c in range(npc)], axis=1)
    mcp = np.stack([R[c]["mcp"] for c in range(npc)], axis=1)
    mnp = np.stack([R[c]["mnp"] for c in range(npc)], axis=1)
    mmp = np.stack([R[c]["mmp"] for c in range(npc)], axis=1)
    cdp = np.stack([R[c]["cdp"] for c in range(npc)], axis=1)
    nks = np.concatenate([R[c]["nks"].reshape(L, NS, LS, 8, 64) for c in range(ncores)], axis=1)
    nvs = np.concatenate([R[c]["nvs"].reshape(L, NS, LS, 8, 64) for c in range(ncores)], axis=1)
    cas = np.concatenate([R[c]["cas"] for c in range(ncores)], axis=1)
    ssms = np.concatenate([R[c]["ssms"].reshape(L, NS, 8, 64, 128) for c in range(ncores)], axis=1)
    ccs = np.concatenate([R[c]["ccs"] for c in range(ncores)], axis=1)
    mcs = np.concatenate([R[c]["mcs"] for c in range(ncores)], axis=1)
    mns = np.concatenate([R[c]["mns"] for c in range(ncores)], axis=1)
    mms = np.concatenate([R[c]["mms"] for c in range(ncores)], axis=1)
    cds = np.concatenate([R[c]["cds"] for c in range(ncores)], axis=1)
    outs = (y_prompt, y_sample, nkp, nvp, cap, ssmp, ccp, mcp, mnp, mmp, cdp,
            nks, nvs, cas, ssms, ccs, mcs, mns, mms, cds)
    return tuple(np.ascontiguousarray(o, dtype=f) for o in outs)


def kernel(**inputs):
    inp = {k: np.asarray(v) for k, v in inputs.items()}
    return run(inp)
```

```python
import numpy as np
import os
from contextlib import ExitStack
import concourse.bass as bass
import concourse.mybir as mybir
from concourse.bass_utils import run_bass_kernel_spmd

F32 = mybir.dt.float32
BF16 = mybir.dt.bfloat16
AF = mybir.ActivationFunctionType
ALU = mybir.AluOpType
AX = mybir.AxisListType

ENGS = ("pe", "act", "dve", "pool", "sp")
SEM_LIMIT = 30000
N_DMA_SEMS = 6
SB_BYTES = 212800


class V:
    __slots__ = ("t", "ap")

    def __init__(self, t, ap):
        self.t = t
        self.ap = ap

    def __getitem__(self, k):
        return V(self.t, self.ap[k])

    def re(self, pattern_, **kw):
        return V(self.t, self.ap.rearrange(pattern_, **kw))

    def un(self, axis):
        return V(self.t, self.ap.unsqueeze(axis))

    def bc(self, shape):
        return V(self.t, self.ap.to_broadcast(list(shape)))

    def bitcast(self, dt):
        return V(self.t, self.ap.bitcast(dt))


class Tile:
    __slots__ = ("ap", "ws", "r", "rd", "name")

    def __init__(self, ap, name=""):
        self.ap = ap
        self.ws = []
        self.r = {}
        self.rd = []
        self.name = name

    def __getitem__(self, k):
        return V(self, self.ap[k])

    @property
    def v(self):
        return V(self, self.ap)


class Ins:
    __slots__ = ("eng", "fn", "deps", "inc", "seq", "dma", "sem", "val")

    def __init__(self, eng, fn, dma):
        self.eng = eng
        self.fn = fn
        self.dma = dma
        self.deps = []
        self.inc = False
        self.seq = 0
        self.sem = None
        self.val = 0


class Prog:
    def __init__(self, nc):
        self.nc = nc
        self.q = {e: [] for e in ENGS}
        self.extra = {e: [] for e in ENGS}
        self.dmas_since_barrier = []

    def _add(self, eng, fn, reads, writes, dma=False):
        ins = Ins(eng, fn, dma)
        ins.seq = len(self.q[eng])
        raw = set()
        deps = []
        for t in reads:
            for w in t.ws:
                deps.append(w)
                raw.add(id(w))
        for t in writes:
            deps.extend(t.ws)
            deps.extend(t.r.values())
            deps.extend(t.rd)
        deps.extend(self.extra[eng])
        self.extra[eng] = []
        for t in writes:
            if t.r or t.rd:
                t.ws = [ins]
            else:
                t.ws = [w for w in t.ws if w.dma or w.eng != eng or dma] + [ins]
            t.r = {}
            t.rd = []
        for t in reads:
            if dma:
                t.rd.append(ins)
            else:
                t.r[eng] = ins
        best = {}
        seen = set()
        out = []
        for d in deps:
            if d is ins or id(d) in seen:
                continue
            seen.add(id(d))
            if d.dma:
                out.append(d)
                continue
            if d.eng == eng and not dma:
                if eng == "pe" or id(d) not in raw:
                    continue
            b = best.get(d.eng)
            if b is None or d.seq > b.seq:
                best[d.eng] = d
        out.extend(best.values())
        for d in out:
            d.inc = True
        ins.deps = out
        self.q[eng].append(ins)
        if dma:
            ins.inc = True
            self.dmas_since_barrier.append(ins)
        return ins

    def op(self, eng, fn, reads=(), writes=()):
        return self._add(eng, fn, reads, writes, False)

    def dma(self, eng, out_ap, in_ap, reads=(), writes=(), **kw):
        return self._add(eng, lambda e: e.dma_start(out=out_ap, in_=in_ap, **kw), reads, writes, True)

    def barrier(self):
        lasts = []
        for e in ENGS:
            for ins in reversed(self.q[e]):
                if not ins.dma:
                    lasts.append(ins)
                    break
        alld = list(self.dmas_since_barrier)
        self.dmas_since_barrier = []
        for e in ENGS:
            self.extra[e] = self.extra[e] + lasts + alld

    def emit(self, stack):
        nc = self.nc
        self.barrier()
        fin = {}
        for e in ENGS:
            deps = self.extra[e]
            self.extra[e] = []
            seen = set()
            out = []
            for d in deps:
                if id(d) in seen:
                    continue
                seen.add(id(d))
                if (not d.dma) and d.eng == e:
                    continue
                d.inc = True
                out.append(d)
            fin[e] = out

        def newsem(name):
            return stack.enter_context(nc.semaphore(name))
        nsem = 0
        for e in ENGS:
            cur = None
            cnt = 0
            k = 0
            dsems = []
            dcnt = []
            di = 0
            for ins in self.q[e]:
                if ins.dma:
                    if not dsems:
                        dsems = [newsem(f"d_{e}_{k}_{j}") for j in range(N_DMA_SEMS)]
                        nsem += N_DMA_SEMS
                        dcnt = [0] * N_DMA_SEMS
                        k += 1
                    j = di % N_DMA_SEMS
                    di += 1
                    dcnt[j] += 16
                    ins.sem = dsems[j]
                    ins.val = dcnt[j]
                    if dcnt[j] >= SEM_LIMIT and j == N_DMA_SEMS - 1:
                        dsems = []
                elif ins.inc:
                    if cur is None or cnt >= SEM_LIMIT:
                        cur = newsem(f"c_{e}_{k}")
                        nsem += 1
                        k += 1
                        cnt = 0
                    cnt += 1
                    ins.sem = cur
                    ins.val = cnt
        waited = {}
        nwaits = [0]

        def do_waits(eobj, e, deps):
            for d in deps:
                key = (e, id(d.sem))
                if waited.get(key, 0) >= d.val:
                    continue
                waited[key] = d.val
                eobj.wait_ge(d.sem, d.val)
                nwaits[0] += 1

        def run(e, eobj):
            for ins in self.q[e]:
                do_waits(eobj, e, ins.deps)
                r = ins.fn(eobj)
                if ins.inc:
                    r.then_inc(ins.sem, 16 if ins.dma else 1)
            do_waits(eobj, e, fin[e])

        block = stack.enter_context(nc.Block())

        @block.tensor
        def _(x):
            run("pe", x)

        @block.scalar
        def _(x):
            run("act", x)

        @block.vector
        def _(x):
            run("dve", x)

        @block.gpsimd
        def _(x):
            run("pool", x)

        @block.sync
        def _(x):
            run("sp", x)
        self.stats = {e: len(self.q[e]) for e in ENGS}
        self.stats["waits"] = nwaits[0]
        self.stats["sems"] = nsem


DSIZE = {F32: 4, BF16: 2}


class Mem:
    def __init__(self, nc, stack, nbytes):
        self.t = stack.enter_context(nc.sbuf_tensor("sbpool", [128, nbytes // 2], BF16))
        self.nbytes = nbytes
        self.top = 0
        self.marks = []
        self.peak = 0

    def push(self):
        self.marks.append(self.top)

    def pop(self):
        self.top = self.marks.pop()

    def alloc(self, shape, dtype, name=""):
        P = shape[0]
        free = list(shape[1:])
        cnt = int(np.prod(free))
        n = cnt * DSIZE[dtype]
        n = (n + 63) // 64 * 64
        off = self.top
        self.top += n
        self.peak = max(self.peak, self.top)
        assert self.top <= self.nbytes, f"SBUF overflow {self.top} > {self.nbytes} at {name}"
        ap = self.t[0:P, off // 2:(off + n) // 2]
        if dtype != BF16:
            ap = ap.bitcast(dtype)
        ap = ap[:, 0:cnt]
        if len(free) == 2:
            ap = ap.rearrange("p (a b) -> p a b", a=free[0], b=free[1])
        elif len(free) == 3:
            ap = ap.rearrange("p (a b c) -> p a b c", a=free[0], b=free[1], c=free[2])
        return Tile(ap, name)


DEPTH = 4
D = 2048
KC = 16
LP = 2048
NS = 4
LS = 32
PAST = 4096
T = LP + NS * LS
ALPHA = (2 * DEPTH) ** 0.25
EPS = 1e-5
GROUPS = [(0, 512), (512, 512), (1024, 512), (1536, 512), (2048, 128)]
NEG = -30000.0

CF_ID, CF_ONE, CF_NEG1, CF_TRI, CF_SEL128, CF_SEL32, CF_SELP5, CF_SELS5 = [i * 128 for i in range(8)]
NCF = 1024
CB_ID, CB_NEG1, CB_LATT = 0, 128, 256
CB_NM4 = 384
CB_NMT4 = 896
CB_AM = 1408
CB_SM = 3456
NCB = 3712
PP_CAW, PP_CAB, PP_CCW, PP_CCB, PP_CDW, PP_CDB, PP_LDG, PP_LDB, PP_NAW, PP_NCW, PP_SKC = 0, 32, 40, 56, 60, 184, 188, 192, 196, 200, 204
NPP = 208
RP_LNG, RP_LNB, RP_DTB, RP_ALOG, RP_DSK, RP_IGB, RP_FGB = 0, 2048, 4096, 4104, 4112, 4120, 4124
NRP = 4128
FB_XBC, FB_Q, FB_K, FB_G, FB_XC, FB_ZC, FB_AD, FB_BD, FB_GD = 0, 8, 12, 16, 20, 24, 28, 32, 36
FM_COLS = [512 + 128 * b for b in range(8)] + [1544 + 128 * b for b in range(4)] + [2056 + 128 * b for b in range(4)] + \
    [3080 + 128 * b for b in range(4)] + [3592 + 128 * b for b in range(4)] + [4104 + 128 * b for b in range(4)] + \
    [4624 + 128 * b for b in range(4)] + [5136 + 128 * b for b in range(4)] + [5648 + 128 * b for b in range(4)]
TM_COLS = [0, 2568, 2056]


def make_consts():
    cf = np.zeros((128, NCF), np.float32)
    k = np.arange(128)[:, None]
    t = np.arange(128)[None, :]
    cf[:, CF_ID:CF_ID + 128] = (k == t)
    cf[:, CF_ONE:CF_ONE + 128] = 1.0
    cf[:, CF_NEG1:CF_NEG1 + 128] = -1.0
    cf[:, CF_TRI:CF_TRI + 128] = (k <= t)
    cf[127, CF_SEL128:CF_SEL128 + 128] = 1.0
    cf[31, CF_SEL32:CF_SEL32 + 128] = 1.0
    cf[0, CF_SELP5:CF_SELP5 + 128] = 1.0
    for j in range(4):
        cf[1 + j, CF_SELS5 + 32 * j:CF_SELS5 + 32 * j + 32] = 1.0
    cb = np.zeros((128, NCB), np.float32)
    cb[:, CB_ID:CB_ID + 128] = (k == t)
    cb[:, CB_NEG1:CB_NEG1 + 128] = -1.0
    cb[:, CB_LATT:CB_LATT + 128] = np.where(k >= t, -1.0, 0.0)
    nm = np.where(t < k, NEG, 0.0)
    nmT = np.where(t > k, NEG, 0.0)
    for h in range(4):
        cb[:, CB_NM4 + 128 * h:CB_NM4 + 128 * h + 128] = nm
        cb[:, CB_NMT4 + 128 * h:CB_NMT4 + 128 * h + 128] = nmT
    tl = np.arange(512)[None, :]
    for dl in range(4):
        cb[:, CB_AM + 512 * dl:CB_AM + 512 * dl + 512] = ((128 * dl + k) < tl)
    q = np.arange(32)[None, :]
    for h in range(8):
        cb[:, CB_SM + 32 * h:CB_SM + 32 * h + 32] = (k < q)
    return cf, cb


class StopBuild(Exception):
    pass


class Builder:
    def chk(self, name):
        if self.stop == name:
            raise StopBuild()

    def __init__(self, nlayers=DEPTH, mixers="ABCD", stop=None):
        self.nlayers = nlayers
        self.mixers = mixers
        self.stop = stop
        self.nc = bass.Bass("TRN2", target_bir_lowering=False)
        self.stack = ExitStack()

    def din(self, name, shape):
        return Tile(self.nc.dram_tensor(name, list(shape), F32, kind="ExternalInput").ap(), name)

    def dout(self, name, shape):
        return self.nc.dram_tensor(name, list(shape), F32, kind="ExternalOutput").ap()

    def mm(self, out, lhsT, rhs, start=True, stop=True):
        self.P.op("pe", lambda e: e.matmul(out.ap, lhsT=lhsT.ap, rhs=rhs.ap, start=start, stop=stop),
                  reads=[lhsT.t, rhs.t], writes=[out.t])

    def tr(self, out, in_, ident):
        self.P.op("pe", lambda e: e.transpose(out=out.ap, in_=in_.ap, identity=ident.ap),
                  reads=[in_.t, ident.t], writes=[out.t])

    def act(self, out, in_, func, scale=None, bias=None, accum=None):
        kw = {}
        reads = [in_.t]
        writes = [out.t]
        if scale is not None:
            if isinstance(scale, V):
                kw["scale"] = scale.ap
                reads.append(scale.t)
            else:
                kw["scale"] = float(scale)
        if bias is not None:
            if isinstance(bias, V):
                kw["bias"] = bias.ap
                reads.append(bias.t)
            else:
                kw["bias"] = self.cbias(float(bias), in_)
                reads.append(self.cbias_t)
        if accum is not None:
            kw["accum_out"] = accum.ap
            writes.append(accum.t)
        self.P.op("act", lambda e: e.activation(out=out.ap, in_=in_.ap, func=func, **kw), reads=reads, writes=writes)

    def cbias(self, val, in_):
        idx = self.cbias_vals.index(val)
        p = in_.ap.shape[0]
        return self.cbias_t.ap[0:p, idx:idx + 1]

    def tt(self, eng, out, a, b, op):
        self.P.op(eng, lambda e: e.tensor_tensor(out=out.ap, in0=a.ap, in1=b.ap, op=op),
                  reads=[a.t, b.t], writes=[out.t])

    def ts(self, eng, out, a, s1, op0, s2=None, op1=None, accum=None):
        reads = [a.t]
        writes = [out.t]
        s1v = s1.ap if isinstance(s1, V) else float(s1)
        if isinstance(s1, V):
            reads.append(s1.t)
        s2v = None
        if s2 is not None:
            s2v = s2.ap if isinstance(s2, V) else float(s2)
            if isinstance(s2, V):
                reads.append(s2.t)
        kw = {}
        if op1 is not None:
            kw["op1"] = op1
        if accum is not None:
            kw["accum_out"] = accum.ap
            writes.append(accum.t)
        self.P.op(eng, lambda e: e.tensor_scalar(out=out.ap, in0=a.ap, scalar1=s1v, scalar2=s2v, op0=op0, **kw),
                  reads=reads, writes=writes)

    def stt(self, out, a, s, b, op0, op1, accum=None):
        reads = [a.t, b.t]
        writes = [out.t]
        sv = s.ap if isinstance(s, V) else float(s)
        if isinstance(s, V):
            reads.append(s.t)
        kw = {}
        if accum is not None:
            kw["accum_out"] = accum.ap
            writes.append(accum.t)
        self.P.op("dve", lambda e: e.scalar_tensor_tensor(out=out.ap, in0=a.ap, scalar=sv, in1=b.ap, op0=op0, op1=op1, **kw),
                  reads=reads, writes=writes)

    def cp(self, eng, out, in_):
        if eng == "act":
            self.P.op("act", lambda e: e.activation(out=out.ap, in_=in_.ap, func=AF.Copy), reads=[in_.t], writes=[out.t])
        else:
            self.P.op(eng, lambda e: e.tensor_copy(out=out.ap, in_=in_.ap), reads=[in_.t], writes=[out.t])

    def red(self, out, in_, op):
        self.P.op("dve", lambda e: e.tensor_reduce(out=out.ap, in_=in_.ap, axis=AX.X, op=op), reads=[in_.t], writes=[out.t])

    def memset(self, eng, out, val):
        self.P.op(eng, lambda e: e.memset(out.ap, val), writes=[out.t])

    def dma(self, q, out, in_, **kw):
        self.P.dma(q, out.ap, in_.ap, reads=[in_.t], writes=[out.t], **kw)

    def bnstats(self, out, in_):
        self.P.op("dve", lambda e: e.bn_stats(out=out.ap, in_=in_.ap), reads=[in_.t], writes=[out.t])

    def bnaggr(self, out, in_):
        self.P.op("dve", lambda e: e.bn_aggr(out=out.ap, in_=in_.ap), reads=[in_.t], writes=[out.t])

    def recip(self, out, in_):
        self.P.op("dve", lambda e: e.reciprocal(out=out.ap, in_=in_.ap), reads=[in_.t], writes=[out.t])

    def bankt(self, i, bf=False):
        ap = self.banks[i][:, :]
        if bf:
            ap = ap.bitcast(BF16)
        return Tile(ap, f"bank{i}")

    def phase_begin(self):
        self.P.barrier()
        self.mem.push()

    def phase_end(self):
        self.P.barrier()
        self.mem.pop()

    def rstd(self, out, var, tmp, scale=1.0):
        self.act(tmp, var, AF.Ln, scale=scale, bias=EPS)
        self.act(out, tmp, AF.Exp, scale=-0.5)

    def build(self):
        nc = self.nc
        st = self.stack
        L = self.nlayers
        I = {}
        I["xp"] = self.din("xp", [LP, D])
        I["xs"] = self.din("xs", [NS * LS, D])
        I["ck"] = self.din("ck", [L, NS, PAST, 512])
        I["cv"] = self.din("cv", [L, NS, PAST, 512])
        I["sca"] = self.din("sca", [L, NS * 3, 1024])
        I["sssm"] = self.din("sssm", [L, NS, 512, 128])
        I["scc"] = self.din("scc", [L, NS * 3, 512])
        I["smc"] = self.din("smc", [L, NS, 4, 128, 128])
        I["smn"] = self.din("smn", [L, NS, 4, 128])
        I["smm"] = self.din("smm", [L, NS, 4])
        I["scd"] = self.din("scd", [L, NS, 30, 512])
        I["cT"] = self.din("cT", [128, KC * 5])
        I["wmod"] = self.din("wmod", [L, 12, 128, KC * 512])
        I["bmod"] = self.din("bmod", [L, 5, 6144])
        I["wfm"] = self.din("wfm", [L, 40, 128, KC * 128])
        I["wtm"] = self.din("wtm", [L, 3, 128, KC * 512])
        I["wsm"] = self.din("wsm", [L, 128, KC * 16])
        I["wout"] = self.din("wout", [L, 128, KC * D])
        I["wqkv"] = self.din("wqkv", [L, 3, 128, 4 * 128])
        I["ppack"] = self.din("ppack", [L, 128, NPP])
        I["rpack"] = self.din("rpack", [L, 128, NRP])
        I["cf"] = self.din("cf", [128, NCF])
        I["cb"] = self.din("cb", [128, NCB])
        self.I = I
        O = {}

        def o(name, shape):
            O[name] = Tile(self.dout(name, shape), name)
        o("yp", [LP, D]); o("ys", [NS * LS, D])
        o("nkp", [L, LP, 512]); o("nvp", [L, LP, 512])
        o("cap", [L, 3, 1024]); o("ssmp", [L, 512, 128]); o("ccp", [L, 3, 512])
        o("mcp", [L, 4, 128, 128]); o("mnp", [L, 4, 128]); o("mmp", [L, 4]); o("cdp", [L, 30, 512])
        o("nks", [L, NS * LS, 512]); o("nvs", [L, NS * LS, 512])
        o("cas", [L, NS, 3, 1024]); o("ssms", [L, NS, 512, 128]); o("ccs", [L, NS, 3, 512])
        o("mcs", [L, NS, 4, 128, 128]); o("mns", [L, NS, 4, 128]); o("mms", [L, NS, 4]); o("cds", [L, NS, 30, 512])
        self.O = O
        self.xres = [Tile(nc.dram_tensor(f"xres{i}", [128, D], F32, kind="Internal").ap(), f"xres{i}") for i in range(17)]
        ytd = nc.dram_tensor("ytd", [KC, 128, T], BF16, kind="Internal").ap()
        self.ytd = ytd
        self.ytt = [[Tile(ytd[4 * m:4 * m + 4, :, (i * 128):(i * 128 + 128)], f"yt{m}_{i}") for i in range(17)] for m in range(4)]
        self.modg = Tile(nc.dram_tensor("modg", [5, D], F32, kind="Internal").ap(), "modg")

        self.mem = Mem(nc, st, SB_BYTES)
        self.banks = [st.enter_context(nc.psum_tensor(f"bank{i}", [128, 512], F32)) for i in range(8)]
        self.P = Prog(nc)
        mem = self.mem

        self.cf = mem.alloc([128, NCF], F32, "cf")
        self.cb = mem.alloc([128, NCB], BF16, "cb")
        self.dma("sp", self.cf.v, I["cf"].v)
        self.dma("pool", self.cb.v, I["cb"].v)
        self.cbias_vals = [EPS, 1.0]
        self.cbias_t = mem.alloc([128, 2], F32, "cbias")
        self.memset("dve", self.cbias_t[:, 0:1], EPS)
        self.memset("dve", self.cbias_t[:, 1:2], 1.0)
        cT = mem.alloc([128, KC, 5], BF16, "cT")
        self.dma("pool", cT.v, I["cT"].v.re("p (k s) -> p k s", k=KC, s=5))
        self.cT = cT
        self.pp = mem.alloc([128, NPP], F32, "pp")
        self.rs = mem.alloc([128, 32], F32, "rs")
        self.modT = mem.alloc([128, 32, 5], F32, "modT")
        self.small = mem.alloc([128, 20, 16], F32, "small")

        try:
            for l in range(self.nlayers):
                self.layer(l)
        except StopBuild:
            pass
        self.P.emit(st)
        return nc

    def cfv(self, off, Q=128, M=128):
        return self.cf[0:Q, off:off + M]

    def cbv(self, off, Q=128, M=128):
        return self.cb[0:Q, off:off + M]

    def layer(self, l):
        I = self.I
        self.P.barrier()
        self.dma("sp", self.pp.v, I["ppack"][l])
        self.dma("sp", self.rs.v, I["rpack"][l, :, 4096:4128])
        if self.stop == "const":
            return
        self.phase_mod(l)
        if self.stop == "mod":
            return
        self.P.barrier()
        self.mem.push()
        self.uT_off = self.mem.top
        self.uT = self.mem.alloc([128, KC, T], BF16, "uT")
        self.phase_ln(l)
        if self.stop == "ln":
            return
        if "A" in self.mixers:
            self.mixer_A(l)
        if "B" in self.mixers:
            self.mixer_B(l)
        if "C" in self.mixers:
            self.mixer_C(l)
        if "D" in self.mixers:
            self.mixer_D(l)
        self.P.barrier()
        self.mem.pop()
        self.phase_out(l)

    def phase_mod(self, l):
        I = self.I
        mem = self.mem
        self.phase_begin()
        modsb = mem.alloc([5, 6144], F32, "modsb")
        bm = mem.alloc([5, 6144], F32, "bm")
        self.dma("sp", bm.v, I["bmod"][l])
        wm = [mem.alloc([128, KC, 512], BF16, f"wm{i}") for i in range(2)]
        wf = [mem.alloc([128, KC, 512], F32, f"wf{i}") for i in range(2)]
        bk = [self.bankt(0), self.bankt(1)]
        bT = self.bankt(2)
        for blk in range(12):
            w = wm[blk % 2]
            f = wf[blk % 2]
            self.dma("sp", f.v, I["wmod"][l, blk].re("p (k c) -> p k c", k=KC, c=512))
            self.cp("dve", w[:, 0:8, :], f[:, 0:8, :])
            self.cp("act", w[:, 8:16, :], f[:, 8:16, :])
            b = bk[blk % 2]
            for kc in range(KC):
                self.mm(b[0:5, :], self.cT[:, kc, :], w[:, kc, :], start=(kc == 0), stop=(kc == KC - 1))
            self.tt("dve", modsb[:, blk * 512:(blk + 1) * 512], b[0:5, :], bm[:, blk * 512:(blk + 1) * 512], ALU.add)
        for j in range(32):
            self.tr(bT[:, j * 8:j * 8 + 5], modsb[0:5, j * 128:(j + 1) * 128], self.cfv(CF_ID, 5, 5))
        self.cp("dve", self.modT[:, 0:16, :], bT[:, 0:128].re("p (a b) -> p a b", a=16, b=8)[:, :, 0:5])
        self.ts("dve", self.modT[:, 16:32, :], bT[:, 128:256].re("p (a b) -> p a b", a=16, b=8)[:, :, 0:5], 1.0, ALU.add)
        self.dma("sp", self.modg.v, modsb[0:5, 4096:6144])
        self.phase_end()

    def xsrc(self, l, i):
        if l == 0:
            if i < 16:
                return self.I["xp"][i * 128:(i + 1) * 128, :]
            return self.I["xs"].v
        return self.xres[i].v

    def phase_ln(self, l):
        mem = self.mem
        self.phase_begin()
        xt = [mem.alloc([128, D], F32, f"xt{i}") for i in range(2)]
        xn = [mem.alloc([128, D], F32, f"xn{i}") for i in range(2)]
        bs = [mem.alloc([128, 24], F32, f"bs{i}") for i in range(2)]
        mv = [mem.alloc([128, 4], F32, f"mv{i}") for i in range(2)]
        bk = [[self.bankt(4 * s + g) for g in range(4)] for s in range(2)]
        ident = self.cfv(CF_ID)
        for i in range(17):
            s = i % 2
            x, n, b, m = xt[s], xn[s], bs[s], mv[s]
            self.dma("sp", x.v, self.xsrc(l, i))
            for j in range(4):
                self.bnstats(b[:, j * 6:(j + 1) * 6], x[:, j * 512:(j + 1) * 512])
            self.bnaggr(m[:, 0:2], b[:, 0:24])
            self.rstd(m[:, 2:3], m[:, 1:2], m[:, 3:4])
            self.ts("dve", n.v, x.v, m[:, 0:1], ALU.subtract, m[:, 2:3], ALU.mult)
            for kc in range(KC):
                self.tr(bk[s][kc // 4][:, (kc % 4) * 128:(kc % 4) * 128 + 128], n[:, kc * 128:(kc + 1) * 128], ident)
            for kc in range(KC):
                src = bk[s][kc // 4][:, (kc % 4) * 128:(kc % 4) * 128 + 128]
                if i < 16:
                    dst = self.uT[:, kc, i * 128:(i + 1) * 128]
                    if kc % 2 == 0:
                        self.act(dst, src, AF.Identity, scale=self.modT[:, 16 + kc, 0:1], bias=self.modT[:, kc, 0:1])
                    else:
                        self.ts("dve", dst, src, self.modT[:, 16 + kc, 0:1], ALU.mult, self.modT[:, kc, 0:1], ALU.add)
                else:
                    for j in range(NS):
                        dst = self.uT[:, kc, LP + 32 * j:LP + 32 * j + 32]
                        sj = src[:, 32 * j:32 * j + 32]
                        if (kc + j) % 2 == 0:
                            self.act(dst, sj, AF.Identity, scale=self.modT[:, 16 + kc, 1 + j:2 + j], bias=self.modT[:, kc, 1 + j:2 + j])
                        else:
                            self.ts("dve", dst, sj, self.modT[:, 16 + kc, 1 + j:2 + j], ALU.mult, self.modT[:, kc, 1 + j:2 + j], ALU.add)
        self.phase_end()

    def phase_out(self, l):
        I, O = self.I, self.O
        mem = self.mem
        self.phase_begin()
        last = (l == self.nlayers - 1)
        pre = getattr(self, "wout_pre", None)
        if pre is not None:
            assert mem.top == self.uT_off
        wout = mem.alloc([128, KC, D], BF16, "wout")
        if pre is not None:
            wout = pre
            self.wout_pre = None
        else:
            for q in range(4):
                self.dma("pool", wout[:, 4 * q:4 * q + 4, :], I["wout"][l, :, 4 * q * D:(4 * q + 4) * D].re("p (k c) -> p k c", k=4, c=D))
        lng = mem.alloc([128, D], F32, "lng")
        lnb = mem.alloc([128, D], F32, "lnb")
        self.dma("sp", lng.v, I["rpack"][l, :, 0:2048])
        self.dma("sp", lnb.v, I["rpack"][l, :, 2048:4096])
        mg = mem.alloc([5, D], F32, "mg")
        self.dma("sp", mg.v, self.modg.v)
        gP = mem.alloc([128, D], F32, "gP")
        gS = mem.alloc([128, D], F32, "gS")
        bk = [self.bankt(i) for i in range(8)]
        for c in range(4):
            self.mm(bk[c].v, self.cf[0:5, CF_SELP5:CF_SELP5 + 128], mg[0:5, c * 512:(c + 1) * 512])
            self.ts("dve", gP[:, c * 512:(c + 1) * 512], bk[c].v, 1.0, ALU.add)
            self.mm(bk[4 + c].v, self.cf[0:5, CF_SELS5:CF_SELS5 + 128], mg[0:5, c * 512:(c + 1) * 512])
            self.ts("dve", gS[:, c * 512:(c + 1) * 512], bk[4 + c].v, 1.0, ALU.add)
        yt = [mem.alloc([128, KC, 128], BF16, f"yt{i}") for i in range(2)]
        xt = [mem.alloc([128, D], F32, f"xt{i}") for i in range(2)]
        vt = [mem.alloc([128, D], F32, f"vt{i}") for i in range(2)]
        bs = [mem.alloc([128, 24], F32, f"bs{i}") for i in range(2)]
        mv = [mem.alloc([128, 4], F32, f"mv{i}") for i in range(2)]
        for i in range(17):
            s = i % 2
            y, x, v, xx, b, m = yt[s], xt[s], vt[s], xt[s], bs[s], mv[s]
            for mx in range(4):
                self.P.dma("sp", y.ap[:, 4 * mx:4 * mx + 4, :],
                           self.ytd[4 * mx:4 * mx + 4, :, i * 128:(i + 1) * 128].rearrange("c p t -> p c t"),
                           reads=[self.ytt[mx][i]], writes=[y])
            self.dma("sp", x.v, self.xsrc(l, i))
            if os.environ.get("DBGY") and i == 16 and l == 0:
                self.cp("dve", gP.v, y.v.re("p a b -> p (a b)"))
                self.dma("sp", O["yp"][0:128, :], gP.v)
            g = gP if i < 16 else gS
            for c in range(4):
                b4 = bk[4 * s + c]
                for kc in range(KC):
                    self.mm(b4.v, y[:, kc, :], wout[:, kc, c * 512:(c + 1) * 512], start=(kc == 0), stop=(kc == KC - 1))
                self.tt("dve", v[:, c * 512:(c + 1) * 512], b4.v, g[:, c * 512:(c + 1) * 512], ALU.mult)
            self.stt(v.v, x.v, ALPHA, v.v, ALU.mult, ALU.add)
            for j in range(4):
                self.bnstats(b[:, j * 6:(j + 1) * 6], v[:, j * 512:(j + 1) * 512])
            self.bnaggr(m[:, 0:2], b[:, 0:24])
            self.rstd(m[:, 2:3], m[:, 1:2], m[:, 3:4])
            self.ts("dve", m[:, 3:4], m[:, 0:1], m[:, 2:3], ALU.mult, -1.0, ALU.mult)
            self.act(xx.v, v.v, AF.Identity, scale=m[:, 2:3], bias=m[:, 3:4])
            self.tt("pool", xx.v, xx.v, lng.v, ALU.mult)
            self.tt("pool", xx.v, xx.v, lnb.v, ALU.add)
            if last:
                if i < 16:
                    dst = O["yp"][i * 128:(i + 1) * 128, :]
                else:
                    dst = O["ys"].v
            else:
                dst = self.xres[i].v
            self.dma("pool", dst, xx.v)
        self.phase_end()

    def fm_stream(self, l, blocks, depth=3):
        mem = self.mem
        ring = [mem.alloc([128, KC, 128], BF16, f"wfm{i}") for i in range(depth)]
        state = {"next": 0}
        I = self.I

        def ensure(upto):
            while state["next"] <= upto and state["next"] < len(blocks):
                k = state["next"]
                self.dma("pool", ring[k % depth].v, I["wfm"][l, blocks[k]].re("p (k c) -> p k c", k=KC, c=128))
                state["next"] += 1

        def get(k):
            ensure(k + depth - 1)
            return ring[k % depth]
        return get

    def fm_block(self, w, banks, evac, groups=GROUPS):
        for g, (c0, n) in enumerate(groups):
            b = banks[g % len(banks)]
            for kc in range(KC):
                self.mm(b[:, 0:n], w[:, kc, :], self.uT[:, kc, c0:c0 + n], start=(kc == 0), stop=(kc == KC - 1))
            evac(g, c0, n, b[:, 0:n])

    def seqs(self):
        out = [("P", 0, 128, list(range(16)), 0)]
        for j in range(NS):
            out.append(("S", j, 32, [16 + j], LP + 32 * j))
        return out

    @staticmethod
    def ccol(c):
        return c * 128 if c < 16 else LP + 32 * (c - 16)

    def small_proj(self, l):
        mem = self.mem
        I = self.I
        mem.push()
        wsm = mem.alloc([128, KC, 16], BF16, "wsm")
        self.dma("pool", wsm.v, I["wsm"][l].re("p (k c) -> p k c", k=KC, c=16))
        b = self.bankt(7)
        for c in range(20):
            Q = 128 if c < 16 else 32
            c0 = self.ccol(c)
            for kc in range(KC):
                self.mm(b[0:Q, c * 16:(c + 1) * 16], self.uT[:, kc, c0:c0 + Q], wsm[:, kc, :], start=(kc == 0), stop=(kc == KC - 1))
        self.cp("dve", self.small[:, 0:16, :], b[:, 0:256].re("p (a b) -> p a b", a=16, b=16))
        self.cp("dve", self.small[0:32, 16:20, :], b[0:32, 256:320].re("p (a b) -> p a b", a=4, b=16))
        self.P.barrier()
        mem.pop()

    def conv4(self, xin, xin_s, acc, acc_s, wcol, bcol):
        pp = self.pp
        self.ts("dve", acc.v, xin[:, 0:LP], pp[:, wcol:wcol + 1], ALU.mult, pp[:, bcol:bcol + 1], ALU.add)
        self.ts("pool", acc_s.v, xin_s[:, :, 0:LS], pp[:, wcol:wcol + 1], ALU.mult, pp[:, bcol:bcol + 1], ALU.add)
        for k in range(1, 4):
            self.stt(acc.v, xin[:, k:k + LP], pp[:, wcol + k:wcol + k + 1], acc.v, ALU.mult, ALU.add)
            self.stt(acc_s.v, xin_s[:, :, k:k + LS], pp[:, wcol + k:wcol + k + 1], acc_s.v, ALU.mult, ALU.add)

    def mixer_A(self, l):
        I, O = self.I, self.O
        mem = self.mem
        self.phase_begin()
        self.small_proj(l)
        self.chk("A_small")
        xtok = mem.alloc([128, 20, 512], BF16, "xtok")
        btok = mem.alloc([128, 20, 256], BF16, "btok")
        BT = mem.alloc([128, 2, T], BF16, "BT")
        CT = mem.alloc([128, 2, T], BF16, "CT")
        identb = self.cbv(CB_ID)
        mem.push()
        xins = [mem.alloc([128, 3 + LP], F32, f"xin{i}") for i in range(2)]
        xin_ss = [mem.alloc([128, NS, 3 + LS], F32, f"xin_s{i}") for i in range(2)]
        accs = [mem.alloc([128, LP], F32, f"acc{i}") for i in range(2)]
        acc_ss = [mem.alloc([128, NS, LS], F32, f"acc_s{i}") for i in range(2)]
        so = mem.alloc([128, T], BF16, "so")
        cst = mem.alloc([12, 1024], F32, "cst")
        self.dma("sp", cst.v, I["sca"][l])
        for x_ in xins:
            self.memset("dve", x_[:, 0:3], 0.0)
        get = self.fm_stream(l, list(range(FB_XBC, FB_XBC + 8)))
        pb = [self.bankt(i) for i in range(4)]
        tb = [self.bankt(4, bf=True), self.bankt(5, bf=True)]
        sb_ = self.bankt(6)
        for b in range(8):
            w = get(b)
            xin, xin_s, acc, acc_s = xins[b % 2], xin_ss[b % 2], accs[b % 2], acc_ss[b % 2]
            self.tr(sb_[:, 0:12], cst[0:12, b * 128:(b + 1) * 128], self.cfv(CF_ID, 12, 12))
            self.cp("dve", xin_s[:, :, 0:3], sb_[:, 0:12].re("p (a b) -> p a b", a=NS, b=3))

            def evac(g, c0, n, ps, b=b):
                if g < 4:
                    self.cp("act", xin[:, 3 + c0:3 + c0 + n], ps)
                else:
                    self.cp("act", xin_s[:, :, 3:3 + LS], ps.re("p (a b) -> p a b", a=NS, b=LS))
            self.fm_block(w, pb, evac)
            self.dma("sp", O["cap"][l, :, b * 128:(b + 1) * 128].re("k f -> f k"), xin[:, LP:LP + 3], allow_slow_non_contiguous=True)
            for j in range(NS):
                self.dma("sp", O["cas"][l, j, :, b * 128:(b + 1) * 128].re("k f -> f k"), xin_s[:, j, LS:LS + 3], allow_slow_non_contiguous=True)
            self.conv4(xin, xin_s, acc, acc_s, PP_CAW + 4 * b, PP_CAB + b)
            if b < 4:
                dst = so.v
            elif b < 6:
                dst = BT[:, b - 4, :]
            else:
                dst = CT[:, b - 6, :]
            self.act(dst[:, 0:LP], acc.v, AF.Silu)
            self.act(dst[:, LP:T].re("p (a b) -> p a b", a=NS, b=LS), acc_s.v, AF.Silu)
            if b < 6:
                tok = xtok if b < 4 else btok
                co = b * 128 if b < 4 else (b - 4) * 128
                for q in range(4):
                    t_ = tb[q % 2]
                    for ii in range(4):
                        i = 4 * q + ii
                        self.tr(t_[:, ii * 128:(ii + 1) * 128], dst[:, i * 128:(i + 1) * 128], identb)
                    self.cp("act" if q % 2 else "dve", tok[:, 4 * q:4 * q + 4, co:co + 128], t_[:, 0:512].re("p (a b) -> p a b", a=4, b=128))
                t_ = tb[0]
                for j in range(NS):
                    self.tr(t_[0:32, j * 128:(j + 1) * 128], dst[:, LP + 32 * j:LP + 32 * j + 32], identb)
                self.cp("dve", tok[0:32, 16:20, co:co + 128], t_[0:32, 0:512].re("p (a b) -> p a b", a=4, b=128))
        self.P.barrier()
        mem.pop()
        self.chk("A_a")
        mem.push()
        wz = mem.alloc([128, KC, 512], BF16, "wz")
        self.dma("pool", wz.v, I["wtm"][l, 0].re("p (k c) -> p k c", k=KC, c=512))
        dt = mem.alloc([128, 20, 8], F32, "dt")
        dta = mem.alloc([128, 20, 8], F32, "dta")
        ab = mem.alloc([128, 8], F32, "ab")
        rs = self.rs
        self.tt("dve", dt.v, self.small[:, :, 0:8], rs[:, 0:8].un(1).bc([128, 20, 8]), ALU.add)
        self.act(dt.v, dt.v, AF.Exp)
        self.act(dt.v, dt.v, AF.Ln, bias=1.0)
        self.act(ab.v, rs[:, 8:16], AF.Exp)
        self.ts("dve", ab.v, ab.v, -1.0, ALU.mult)
        self.tt("dve", dta.v, dt.v, ab.v.un(1).bc([128, 20, 8]), ALU.mult)
        self.chk("A_dt")
        W = {}
        for nm, shp, dty in [("X", [128, 8, 128], F32), ("dec", [128, 8, 128], F32), ("MT", [128, 8, 128], BF16),
                             ("xdt", [128, 8, 64], BF16), ("xdtw", [128, 8, 64], BF16), ("y1", [128, 8, 64], F32),
                             ("y2", [128, 8, 64], F32), ("y3", [128, 8, 64], F32), ("sz", [128, 512], F32),
                             ("yg", [128, 512], F32), ("junk", [128, 512], F32), ("yn", [128, 512], F32),
                             ("yT", [128, 4, 128], BF16), ("acum", [128, 8], F32), ("nacum", [128, 8], F32),
                             ("eacum", [128, 8], F32), ("tmp8", [128, 8], F32), ("wS", [128, 8], F32),
                             ("elast", [128, 8], F32), ("ss", [128, 4], F32), ("hT", [128, 8, 64], F32),
                             ("hTb", [128, 8, 64], BF16), ("hin", [128, 4, 128], F32), ("hout", [128, 4, 128], F32)]:
            W[nm] = mem.alloc(shp, dty, nm)
        bk = {"P1a": self.bankt(0), "P1b": self.bankt(1), "intra": self.bankt(3), "inter": self.bankt(4),
              "z": self.bankt(5), "yT": self.bankt(6), "su": self.bankt(7)}
        b2 = self.banks[2]
        bk["GT"] = Tile(b2[:, 0:256], "GT")
        bk["ac"] = Tile(b2[:, 256:264], "ac")
        bk["al"] = Tile(b2[:, 264:272], "al")
        for (kind, j, Q, chunks, c00) in self.seqs():
            hT, hTb = W["hT"], W["hTb"]
            hTf = hT.v.re("p a b -> p (a b)")
            if kind == "P":
                self.memset("dve", hT.v, 0.0)
                self.memset("pool", hTb.v, 0.0)
            else:
                self.dma("sp", W["hin"].v, I["sssm"][l, j].re("(a p) n -> p a n", a=4, p=128))
                for a in range(4):
                    self.tr(bk["su"][:, a * 128:(a + 1) * 128], W["hin"][:, a, :], self.cfv(CF_ID))
                self.cp("dve", hTf, bk["su"].v)
                self.cp("act", hTb.v, hT.v)
            if kind == "S":
                self.chk("A_S0in")
            for c in chunks:
                self.ssd_chunk(l, c, Q, W, bk, xtok, btok, BT, CT, dt, dta, wz)
            for a in range(4):
                self.tr(bk["su"][:, a * 128:(a + 1) * 128], hTf[:, a * 128:(a + 1) * 128], self.cfv(CF_ID))
            self.cp("dve", W["hout"].v.re("p a b -> p (a b)"), bk["su"].v)
            dst = O["ssmp"][l] if kind == "P" else O["ssms"][l, j]
            self.dma("sp", dst.re("(a p) n -> p a n", a=4, p=128), W["hout"].v)
            self.chk("A_Pout")
        self.P.barrier()
        mem.pop()
        self.phase_end()

    def ssd_chunk(self, l, c, Q, W, bk, xtok, btok, BT, CT, dt, dta, wz):
        c0 = self.ccol(c)
        X, dec, MT, xdt, xdtw = W["X"], W["dec"], W["MT"], W["xdt"], W["xdtw"]
        tri = self.cfv(CF_TRI, Q, Q)
        ones = self.cfv(CF_ONE, Q, Q)
        identb = self.cbv(CB_ID, Q, Q)
        nm4 = self.cb[0:Q, CB_NM4:CB_NM4 + 512].re("p (a b) -> p a b", a=4, b=128)[:, :, 0:Q]
        sel = self.cf[0:Q, (CF_SEL128 if Q == 128 else CF_SEL32):(CF_SEL128 if Q == 128 else CF_SEL32) + 128]
        self.tt("dve", X[0:Q, :, 0:Q], tri.un(1).bc([Q, 8, Q]), dta[0:Q, c, :].un(2).bc([Q, 8, Q]), ALU.mult)
        for hf in range(2):
            p1 = bk["P1a" if hf == 0 else "P1b"]
            o_ = p1[0:Q, 0:4 * Q].re("p (a b) -> p a b", a=4, b=Q)
            self.mm(o_, ones, X[0:Q, 4 * hf:4 * hf + 4, 0:Q], start=True, stop=False)
            self.mm(o_, identb, nm4, start=False, stop=True)
        self.chk("A_c1")
        self.mm(bk["ac"][0:Q, :], tri, dta[0:Q, c, :])
        self.cp("dve", W["acum"][0:Q, :], bk["ac"][0:Q, :])
        self.ts("dve", W["nacum"][0:Q, :], bk["ac"][0:Q, :], -1.0, ALU.mult)
        self.mm(bk["al"].v, sel, W["acum"][0:Q, :])
        self.act(W["eacum"][0:Q, :], W["acum"][0:Q, :], AF.Exp)
        self.tt("dve", W["tmp8"][0:Q, :], bk["al"][0:Q, :], W["acum"][0:Q, :], ALU.subtract)
        self.act(W["wS"][0:Q, :], W["tmp8"][0:Q, :], AF.Exp)
        self.act(W["elast"].v, bk["al"].v, AF.Exp)
        for h in range(8):
            p1 = bk["P1a" if h < 4 else "P1b"]
            self.act(dec[0:Q, h, 0:Q], p1[0:Q, (h % 4) * Q:(h % 4) * Q + Q], AF.Exp, bias=W["nacum"][0:Q, h:h + 1])
        self.chk("A_c2")
        for g in range(2):
            self.mm(bk["GT"][0:Q, g * Q:(g + 1) * Q], BT[:, g, c0:c0 + Q], CT[:, g, c0:c0 + Q])
        for g in range(2):
            self.tt("dve", MT[0:Q, 4 * g:4 * g + 4, 0:Q], dec[0:Q, 4 * g:4 * g + 4, 0:Q],
                    bk["GT"][0:Q, g * Q:(g + 1) * Q].un(1).bc([Q, 4, Q]), ALU.mult)
        xt_ = xtok[0:Q, c, :].re("p (a b) -> p a b", a=8, b=64)
        self.tt("pool", xdt[0:Q], xt_, dt[0:Q, c, :].un(2).bc([Q, 8, 64]), ALU.mult)
        for h in range(8):
            self.mm(bk["intra"][0:Q, h * 64:(h + 1) * 64], MT[0:Q, h, 0:Q], xdt[0:Q, h, :])
        hTbf = W["hTb"].v.re("p a b -> p (a b)")
        for g in range(2):
            self.mm(bk["inter"][0:Q, g * 256:(g + 1) * 256], CT[:, g, c0:c0 + Q], hTbf[:, g * 256:(g + 1) * 256])
        self.tt("dve", W["y1"][0:Q], bk["inter"][0:Q, :].re("p (a b) -> p a b", a=8, b=64),
                W["eacum"][0:Q, :].un(2).bc([Q, 8, 64]), ALU.mult)
        self.tt("dve", W["y2"][0:Q], bk["intra"][0:Q, :].re("p (a b) -> p a b", a=8, b=64), W["y1"][0:Q], ALU.add)
        self.tt("pool", W["y3"][0:Q], xt_, self.rs[0:Q, 16:24].un(2).bc([Q, 8, 64]), ALU.mult)
        self.tt("pool", W["y2"][0:Q], W["y2"][0:Q], W["y3"][0:Q], ALU.add)
        self.chk("A_c3")
        for kc in range(KC):
            self.mm(bk["z"][0:Q, :], self.uT[:, kc, c0:c0 + Q], wz[:, kc, :], start=(kc == 0), stop=(kc == KC - 1))
        self.act(W["sz"][0:Q, :], bk["z"][0:Q, :], AF.Silu)
        self.tt("dve", W["yg"][0:Q, :], W["y2"][0:Q].re("p a b -> p (a b)"), W["sz"][0:Q, :], ALU.mult)
        self.act(W["junk"][0:Q, :], W["yg"][0:Q, :], AF.Square, accum=W["ss"][0:Q, 0:1])
        self.rstd(W["ss"][0:Q, 1:2], W["ss"][0:Q, 0:1], W["ss"][0:Q, 2:3], scale=1.0 / 512)
        self.ts("dve", W["yn"][0:Q, :], W["yg"][0:Q, :], W["ss"][0:Q, 1:2], ALU.mult)
        for a in range(4):
            self.tr(bk["yT"][:, a * Q:(a + 1) * Q], W["yn"][0:Q, a * 128:(a + 1) * 128], self.cfv(CF_ID, Q, Q))
        for a in range(4):
            self.act(W["yT"][:, a, 0:Q], bk["yT"][:, a * Q:(a + 1) * Q], AF.Identity, scale=self.pp[:, PP_NAW + a:PP_NAW + a + 1])
        ti = c if c < 16 else 16
        self.P.dma("sp", self.ytd[0:4, :, c0:c0 + Q].rearrange("c p t -> p c t"), W["yT"].ap[:, :, 0:Q],
                   reads=[W["yT"]], writes=[self.ytt[0][ti]])
        self.chk("A_c4")
        self.tt("pool", xdtw[0:Q], xdt[0:Q], W["wS"][0:Q, :].un(2).bc([Q, 8, 64]), ALU.mult)
        for g in range(2):
            self.mm(bk["su"][:, g * 256:(g + 1) * 256], btok[0:Q, c, g * 128:(g + 1) * 128],
                    xdtw[0:Q, 4 * g:4 * g + 4, :])
        self.tt("dve", W["hT"].v, W["hT"].v, W["elast"].v.un(2).bc([128, 8, 64]), ALU.mult)
        self.tt("dve", W["hT"].v, W["hT"].v, bk["su"].v.re("p (a b) -> p a b", a=8, b=64), ALU.add)
        self.cp("act", W["hTb"].v, W["hT"].v)
        self.chk("A_c5")
        if c == 15:
            self.chk("A_P")

    def mixer_B(self, l):
        I, O = self.I, self.O
        mem = self.mem
        self.phase_begin()
        identb = self.cbv(CB_ID)
        vtok = mem.alloc([128, 16, 512], BF16, "vtok")
        vs = mem.alloc([32, NS, 512], BF16, "vs")
        mem.push()
        wv = mem.alloc([128, KC, 512], BF16, "wv")
        wk = mem.alloc([128, KC, 512], BF16, "wk")
        self.dma("pool", wv.v, I["wtm"][l, 1].re("p (k c) -> p k c", k=KC, c=512))
        self.dma("pool", wk.v, I["wtm"][l, 2].re("p (k c) -> p k c", k=KC, c=512))
        ko = [mem.alloc([128, 512], F32, f"ko{i}") for i in range(2)]
        vo = [mem.alloc([128, 512], F32, f"vo{i}") for i in range(2)]
        bkk = [self.bankt(0), self.bankt(1)]
        bkv = [self.bankt(2), self.bankt(3)]
        bvs = self.bankt(4)
        for i in range(17):
            s = i % 2
            c0 = i * 128
            for kc in range(KC):
                self.mm(bkv[s].v, self.uT[:, kc, c0:c0 + 128], wv[:, kc, :], start=(kc == 0), stop=(kc == KC - 1))
            for kc in range(KC):
                self.mm(bkk[s].v, self.uT[:, kc, c0:c0 + 128], wk[:, kc, :], start=(kc == 0), stop=(kc == KC - 1))
            self.cp("act", vo[s].v, bkv[s].v)
            self.cp("dve", ko[s].v, bkk[s].v)
            if i < 16:
                self.cp("pool", vtok[:, i, :], vo[s].v)
                self.dma("sp", O["nvp"][l, c0:c0 + 128, :], vo[s].v)
                self.dma("sp", O["nkp"][l, c0:c0 + 128, :], ko[s].v)
            else:
                self.dma("sp", O["nvs"][l], vo[s].v)
                self.dma("sp", O["nks"][l], ko[s].v)
        for j in range(NS):
            c0 = LP + 32 * j
            for kc in range(KC):
                self.mm(bvs[0:32, :], self.uT[:, kc, c0:c0 + 32], wv[:, kc, :], start=(kc == 0), stop=(kc == KC - 1))
            self.cp("act", vs[:, j, :], bvs[0:32, :])
        self.P.barrier()
        mem.pop()
        self.chk("B_tm")
        mem.push()
        qT = mem.alloc([128, T], BF16, "qT")
        kT = mem.alloc([128, T], BF16, "kT")
        sg = mem.alloc([128, T], BF16, "sg")
        e_sb = [mem.alloc([128, 512], F32, f"e{i}") for i in range(2)]
        sp_ = [mem.alloc([128, 512], BF16, f"sp{i}") for i in range(2)]
        Wt = [mem.alloc([128, 512], BF16, f"W{i}") for i in range(2)]
        Sl = mem.alloc([128, 512], BF16, "Sl")
        yTt = [mem.alloc([128, 512], BF16, f"yTb{i}") for i in range(2)]
        kst = [mem.alloc([128, 4, 512], BF16, f"kst{i}") for i in range(2)]
        vst = [mem.alloc([128, 4, 512], BF16, f"vst{i}") for i in range(2)]
        kTp = [mem.alloc([128, 128], BF16, f"kTp{i}") for i in range(2)]
        qzs = mem.alloc([128, 4, NS, 64], BF16, "qzs")
        self.memset("dve", qzs.v.re("p a b c -> p (a b c)"), 0.0)
        kTs = mem.alloc([128, 4, 128], BF16, "kTs")
        sgs = mem.alloc([128, 4, 128], BF16, "sgs")
        kp4 = [mem.alloc([128, 512], BF16, f"kp4{i}") for i in range(2)]
        zt = mem.alloc([128, 256], BF16, "zt")
        self.memset("dve", zt.v, 0.0)
        get = self.fm_stream(l, [[FB_Q, FB_K, FB_G][k % 3] + k // 3 for k in range(12)], depth=3)
        pb = [self.bankt(0), self.bankt(1)]
        zA = [self.bankt(2), self.bankt(3)]
        zB = [self.bankt(4), self.bankt(5)]
        bO = self.bankt(6)
        bT = self.bankt(7, bf=True)
        am = self.cb[:, CB_AM:CB_AM + 2048].re("p (a b) -> p a b", a=4, b=512)
        sm = self.cb[0:32, CB_SM:CB_SM + 256]
        latt = self.cbv(CB_LATT)
        neg1 = self.cbv(CB_NEG1)
        cnt = [0]
        for hp in range(4):

            def ev_q(g, c0, n, ps):
                self.act(qT[:, c0:c0 + n], ps, AF.Copy, scale=0.125)

            def ev_k(g, c0, n, ps):
                self.cp("dve", kT[:, c0:c0 + n], ps)

            def ev_g(g, c0, n, ps):
                self.act(sg[:, c0:c0 + n], ps, AF.Silu)
            self.fm_block(get(3 * hp), pb, ev_q)
            self.fm_block(get(3 * hp + 1), pb, ev_k)
            self.fm_block(get(3 * hp + 2), pb, ev_g)
            self.chk("B_fm")
            its = []
            for QS in range(4):
                for hh in range(2):
                    kbs = list(range(4 * QS + 3, -1, -1))
                    for n_, kb in enumerate(kbs):
                        its.append((QS, hh, kb, n_ == 0, hh == 1 and kb == 0))

            def p_stage1(it, s):
                QS, hh, kb, first, lastq = it
                q0, po = QS * 512, 64 * hh
                lk = kT[po:po + 64, kb * 128:(kb + 1) * 128]
                rq = qT[po:po + 64, q0:q0 + 512]
                self.mm(zA[s].v, lk, rq)
                self.act(e_sb[s].v, zA[s].v, AF.Exp)
                self.act(sp_[s].v, e_sb[s].v, AF.Ln, bias=1.0)
                dl = kb - 4 * QS
                if dl >= 0:
                    self.tt("pool", sp_[s].v, sp_[s].v, am[:, dl, :], ALU.mult)

            def p_stage2(it, s, hp=hp):
                QS, hh, kb, first, lastq = it
                q0, po = QS * 512, 64 * hh
                head = 2 * hp + hh
                lk = kT[po:po + 64, kb * 128:(kb + 1) * 128]
                rq = qT[po:po + 64, q0:q0 + 512]
                dl = kb - 4 * QS
                self.mm(zB[s].v, lk, rq, start=True, stop=False)
                self.mm(zB[s].v, latt, sp_[s].v, start=False, stop=first)
                if not first:
                    self.mm(zB[s].v, neg1, Sl.v, start=False, stop=True)
                self.act(Wt[s].v, zB[s].v, AF.Exp)
                if dl >= 0:
                    self.tt("pool", Wt[s].v, Wt[s].v, am[:, dl, :], ALU.mult)
                self.mm(bO[po:po + 64, :], vtok[:, kb, head * 64:(head + 1) * 64], Wt[s].v, start=first, stop=(kb == 0))
                if kb > 0:
                    if first:
                        self.cp("dve", Sl.v, sp_[s].v)
                    else:
                        self.tt("dve", Sl.v, Sl.v, sp_[s].v, ALU.add)
                if lastq:
                    yb = yTt[QS % 2]
                    self.tt("dve", yb.v, bO.v, sg[:, q0:q0 + 512], ALU.mult)
                    for ii in range(4):
                        i = 4 * QS + ii
                        self.P.dma("sp", self.ytd[4 + hp, :, i * 128:(i + 1) * 128], yb.ap[:, ii * 128:(ii + 1) * 128],
                                   reads=[yb], writes=[self.ytt[1][i]])
            base = cnt[0]
            for idx in range(len(its) + 1):
                if idx < len(its):
                    p_stage1(its[idx], (base + idx) % 2)
                if idx >= 1:
                    p_stage2(its[idx - 1], (base + idx - 1) % 2)
            cnt[0] += len(its)
            self.chk("B_P")
            for j in range(NS):
                cq = LP + 32 * j
                self.cp("dve", qzs[0:64, hp, j, 0:32], qT[0:64, cq:cq + 32])
                self.cp("dve", qzs[64:128, hp, j, 32:64], qT[64:128, cq:cq + 32])
            self.cp("dve", kTs[:, hp, :], kT[:, LP:T])
            self.cp("dve", sgs[:, hp, :], sg[:, LP:T])
        sm = self.cb[0:32, CB_SM:CB_SM + 256]
        neg32 = self.cb[0:32, CB_NEG1:CB_NEG1 + 128]
        latt32 = self.cbv(CB_LATT, 32, 32)
        Snew = Sl[0:32, 0:256]
        Slp = Sl[:, 256:512]
        for j in range(NS):
            s = cnt[0] % 2
            cnt[0] += 1
            for hp in range(4):
                self.mm(zA[s][0:32, hp * 64:(hp + 1) * 64], kTs[:, hp, 32 * j:32 * j + 32], qzs[:, hp, j, :])
            self.act(e_sb[s][0:32, 0:256], zA[s][0:32, 0:256], AF.Exp)
            self.act(sp_[s][0:32, 0:256], e_sb[s][0:32, 0:256], AF.Ln, bias=1.0)
            self.tt("pool", sp_[s][0:32, 0:256], sp_[s][0:32, 0:256], sm, ALU.mult)
            self.mm(zB[s][0:32, 0:256], zt[:, 0:32], zt.v, start=True, stop=False)
            for hp in range(4):
                self.mm(zB[s][0:32, hp * 64:(hp + 1) * 64], kTs[:, hp, 32 * j:32 * j + 32], qzs[:, hp, j, :], start=False, stop=False)
            self.mm(zB[s][0:32, 0:256], latt32, sp_[s][0:32, 0:256], start=False, stop=True)
            self.act(Wt[s][0:32, 0:256], zB[s][0:32, 0:256], AF.Exp)
            self.tt("pool", Wt[s][0:32, 0:256], Wt[s][0:32, 0:256], sm, ALU.mult)
            for hh in range(2):
                self.mm(bO[64 * hh:64 * hh + 64, 0:128], zt[:, 0:64], zt[:, 0:128], start=True, stop=False)
            for head in range(8):
                hp, hh = head // 2, head % 2
                self.mm(bO[64 * hh:64 * hh + 64, hp * 32:(hp + 1) * 32], vs[0:32, j, head * 64:(head + 1) * 64],
                        Wt[s][0:32, head * 32:(head + 1) * 32], start=False, stop=False)
            self.cp("dve", Snew, sp_[s][0:32, 0:256])
            blks = [(kq, a_) for kq in range(7, -1, -1) for a_ in range(3, -1, -1)]

            def s_stage1(n, s, j=j):
                kq, a_ = blks[n]
                ks_, vs_ = kst[kq % 2], vst[kq % 2]
                if a_ == 3:
                    self.dma("pool", ks_.v, I["ck"][l, j, kq * 512:(kq + 1) * 512, :].re("(a p) c -> p a c", a=4, p=128))
                    self.dma("pool", vs_.v, I["cv"][l, j, kq * 512:(kq + 1) * 512, :].re("(a p) c -> p a c", a=4, p=128))
                kp = kp4[s]
                for hp in range(4):
                    self.tr(bT[:, hp * 128:(hp + 1) * 128], ks_[:, a_, hp * 128:(hp + 1) * 128], identb)
                self.cp("dve", kp.v, bT[:, 0:512])
                for hp in range(4):
                    self.mm(zA[s][:, hp * 64:(hp + 1) * 64], kp[:, hp * 128:(hp + 1) * 128], qzs[:, hp, j, :])
                self.act(e_sb[s][:, 0:256], zA[s][:, 0:256], AF.Exp)
                self.act(sp_[s][:, 0:256], e_sb[s][:, 0:256], AF.Ln, bias=1.0)

            def s_stage2(n, s, j=j):
                kq, a_ = blks[n]
                vs_ = vst[kq % 2]
                kp = kp4[s]
                self.mm(zB[s][:, 0:256], latt, sp_[s][:, 0:256], start=True, stop=False)
                for hp in range(4):
                    self.mm(zB[s][:, hp * 64:(hp + 1) * 64], kp[:, hp * 128:(hp + 1) * 128], qzs[:, hp, j, :], start=False, stop=False)
                self.mm(zB[s][:, 0:256], neg32, Snew, start=False, stop=(n == 0))
                if n > 0:
                    self.mm(zB[s][:, 0:256], neg1, Slp, start=False, stop=True)
                self.act(Wt[s][:, 0:256], zB[s][:, 0:256], AF.Exp)
                lastb = (n == len(blks) - 1)
                for head in range(8):
                    hp, hh = head // 2, head % 2
                    self.mm(bO[64 * hh:64 * hh + 64, hp * 32:(hp + 1) * 32], vs_[:, a_, head * 64:(head + 1) * 64],
                            Wt[s][:, head * 32:(head + 1) * 32], start=False, stop=lastb)
                if not lastb:
                    if n == 0:
                        self.cp("dve", Slp, sp_[s][:, 0:256])
                    else:
                        self.tt("dve", Slp, Slp, sp_[s][:, 0:256], ALU.add)
            base = cnt[0]
            for idx in range(len(blks) + 1):
                if idx < len(blks):
                    s_stage1(idx, (base + idx) % 2)
                if idx >= 1:
                    s_stage2(idx - 1, (base + idx - 1) % 2)
            cnt[0] += len(blks)
            ys_ = yTt[j % 2]
            self.tt("dve", ys_[:, 0:128].re("p (a b) -> p a b", a=4, b=32), bO[:, 0:128].re("p (a b) -> p a b", a=4, b=32),
                    sgs[:, :, 32 * j:32 * j + 32], ALU.mult)
            self.P.dma("sp", self.ytd[4:8, :, LP + 32 * j:LP + 32 * j + 32].rearrange("c p t -> p c t"),
                       ys_.ap[:, 0:128].rearrange("p (a b) -> p a b", a=4, b=32), reads=[ys_], writes=[self.ytt[1][16]])
            self.chk("B_S0")
        self.P.barrier()
        mem.pop()
        self.phase_end()

    def mixer_C(self, l):
        I, O = self.I, self.O
        mem = self.mem
        self.phase_begin()
        if "A" not in self.mixers:
            self.small_proj(l)
        xc = mem.alloc([128, 4, T], BF16, "xc")
        xcv = mem.alloc([128, 4, T], BF16, "xcv")
        szT = mem.alloc([128, 4, T], BF16, "szT")
        identb = self.cbv(CB_ID)
        mem.push()
        xins = [mem.alloc([128, 3 + LP], F32, f"xin{i}") for i in range(2)]
        xin_ss = [mem.alloc([128, NS, 3 + LS], F32, f"xin_s{i}") for i in range(2)]
        accs = [mem.alloc([128, LP], F32, f"acc{i}") for i in range(2)]
        acc_ss = [mem.alloc([128, NS, LS], F32, f"acc_s{i}") for i in range(2)]
        cst = mem.alloc([12, 512], F32, "cst")
        self.dma("sp", cst.v, I["scc"][l])
        for x_ in xins:
            self.memset("dve", x_[:, 0:3], 0.0)
        get = self.fm_stream(l, list(range(FB_XC, FB_XC + 8)))
        pb = [self.bankt(i) for i in range(4)]
        sb_ = self.bankt(6)
        for b in range(8):
            w = get(b)
            xin, xin_s, acc, acc_s = xins[b % 2], xin_ss[b % 2], accs[b % 2], acc_ss[b % 2]
            if b < 4:
                self.tr(sb_[:, 0:12], cst[0:12, b * 128:(b + 1) * 128], self.cfv(CF_ID, 12, 12))
                self.cp("dve", xin_s[:, :, 0:3], sb_[:, 0:12].re("p (a b) -> p a b", a=NS, b=3))

                def evac(g, c0, n, ps, b=b):
                    if g < 4:
                        self.cp("act", xin[:, 3 + c0:3 + c0 + n], ps)
                        self.cp("dve", xc[:, b, c0:c0 + n], xin[:, 3 + c0:3 + c0 + n])
                    else:
                        self.cp("act", xin_s[:, :, 3:3 + LS], ps.re("p (a b) -> p a b", a=NS, b=LS))
                        self.cp("dve", xc[:, b, c0:c0 + n].re("p (a b) -> p a b", a=NS, b=LS), xin_s[:, :, 3:3 + LS])
                self.fm_block(w, pb, evac)
                self.dma("sp", O["ccp"][l, :, b * 128:(b + 1) * 128].re("k f -> f k"), xin[:, LP:LP + 3], allow_slow_non_contiguous=True)
                for j in range(NS):
                    self.dma("sp", O["ccs"][l, j, :, b * 128:(b + 1) * 128].re("k f -> f k"), xin_s[:, j, LS:LS + 3], allow_slow_non_contiguous=True)
                self.conv4(xin, xin_s, acc, acc_s, PP_CCW + 4 * b, PP_CCB + b)
                self.act(xcv[:, b, 0:LP], acc.v, AF.Silu)
                self.act(xcv[:, b, LP:T].re("p (a b) -> p a b", a=NS, b=LS), acc_s.v, AF.Silu)
            else:
                def evac(g, c0, n, ps, b=b):
                    self.act(szT[:, b - 4, c0:c0 + n], ps, AF.Silu)
                self.fm_block(w, pb, evac)
        self.P.barrier()
        mem.pop()
        self.chk("C_a")
        mem.push()
        wqkv = mem.alloc([128, 3, 4, 128], BF16, "wqkv")
        for k in range(3):
            self.dma("pool", wqkv[:, k], I["wqkv"][l, k].re("p (h e) -> p h e", h=4, e=128))
        ip = mem.alloc([128, 20, 4], F32, "ip")
        lf = mem.alloc([128, 20, 4], F32, "lf")
        rs = self.rs
        self.tt("dve", ip.v, self.small[:, :, 8:12], rs[:, 24:28].un(1).bc([128, 20, 4]), ALU.add)
        self.tt("dve", lf.v, self.small[:, :, 12:16], rs[:, 28:32].un(1).bc([128, 20, 4]), ALU.add)
        self.act(lf.v, lf.v, AF.Exp, scale=-1.0)
        self.act(lf.v, lf.v, AF.Ln, bias=1.0)
        self.ts("dve", lf.v, lf.v, -1.0, ALU.mult)
        W = {}
        for nm, shp, dty in [("A", [128, 4, 128], F32), ("Bm", [128, 4, 128], F32), ("E", [128, 4, 128], F32),
                             ("w", [128, 4, 128], BF16), ("wT", [128, 4, 128], BF16), ("qT", [128, 4, 128], BF16),
                             ("kT", [128, 4, 128], BF16), ("v", [128, 4, 128], BF16), ("vw", [128, 4, 130], BF16),
                             ("ktok", [128, 4, 128], BF16), ("tmp", [128, 4, 130], F32), ("hh", [128, 4, 128], F32),
                             ("hn", [128, 4, 128], F32), ("hnT", [128, 4, 128], F32), ("y1", [128, 4, 128], F32),
                             ("yT", [128, 4, 128], BF16), ("Cn", [128, 4, 130], F32), ("Cnb", [128, 4, 130], BF16),
                             ("cin", [128, 4, 128], F32), ("cout", [128, 4, 128], F32),
                             ("bsb", [128, 4], F32), ("rowmax", [128, 4], F32), ("mx", [128, 4], F32),
                             ("negmx", [128, 4], F32), ("mmb", [128, 4], F32), ("rsum", [128, 4], F32),
                             ("g", [128, 4], F32), ("nq", [128, 4], F32), ("emt", [128, 4], F32), ("den", [128, 4], F32),
                             ("vals", [128, 8], F32), ("lastb", [128, 8], F32), ("ws", [128, 4], F32), ("gl", [128, 4], F32),
                             ("t4", [128, 4], F32), ("bst", [128, 4, 6], F32), ("mvv", [128, 4, 2], F32), ("rsd", [128, 4], F32),
                             ("t4b", [128, 4], F32)]:
            W[nm] = mem.alloc(shp, dty, nm)
        b1 = self.banks[1]
        bk = {"P2": self.bankt(0), "pq": self.bankt(2), "pk": self.bankt(3), "pv": self.bankt(4), "pkt": self.bankt(5),
              "S": self.bankt(6), "num": self.bankt(7)}
        bk["ms"] = Tile(b1[:, 0:8], "ms")
        bk["lb"] = Tile(b1[:, 8:16], "lb")
        bk["wTp"] = Tile(b1[:, 256:512].bitcast(BF16), "wTp")
        for (kind, j, Q, chunks, c00) in self.seqs():
            Cn, Cnb, mmb = W["Cn"], W["Cnb"], W["mmb"]
            if kind == "P":
                self.memset("dve", Cn.v, 0.0)
                self.memset("pool", Cnb.v, 0.0)
                self.memset("dve", mmb.v, 0.0)
            else:
                self.dma("sp", W["cin"].v, I["smc"][l, j].re("h v d -> v h d"))
                for h in range(4):
                    self.tr(bk["num"][:, h * 128:(h + 1) * 128], W["cin"][:, h, :], self.cfv(CF_ID))
                self.cp("dve", Cn[:, :, 0:128], bk["num"].v.re("p (a b) -> p a b", a=4, b=128))
                self.dma("sp", Cn[:, :, 128:129], I["smn"][l, j].re("h (d o) -> d h o", o=1), allow_slow_non_contiguous=True)
                self.dma("sp", mmb.v, I["smm"][l, j:j + 1, :].bc([128, 4]))
                self.cp("act", Cnb.v, Cn.v)
            if kind == "S":
                self.chk("C_S0in")
            for c in chunks:
                self.ml_chunk(l, c, Q, W, bk, xc, xcv, szT, wqkv, ip, lf)
                self.chk("C_c1")
            if kind == "S":
                self.chk("C_S0")
            self.chk("C_P")
            for h in range(4):
                self.tr(bk["num"][:, h * 128:(h + 1) * 128], Cn[:, h, 0:128], self.cfv(CF_ID))
            self.cp("dve", W["cout"].v, bk["num"].v.re("p (a b) -> p a b", a=4, b=128))
            if kind == "P":
                dc, dn, dm = O["mcp"][l], O["mnp"][l], O["mmp"][l:l + 1, :]
            else:
                dc, dn, dm = O["mcs"][l, j], O["mns"][l, j], O["mms"][l, j:j + 1, :]
            self.dma("sp", dc.re("h v d -> v h d"), W["cout"].v)
            self.dma("sp", dn.re("h (d o) -> d h o", o=1), Cn[:, :, 128:129], allow_slow_non_contiguous=True)
            self.dma("sp", dm, mmb[0:1, :])
            self.chk("C_Pout")
        self.P.barrier()
        mem.pop()
        self.phase_end()

    def ml_chunk(self, l, c, Q, W, bk, xc, xcv, szT, wqkv, ip, lf):
        c0 = self.ccol(c)
        ident = self.cfv(CF_ID, Q, Q)
        tri = self.cfv(CF_TRI, Q, Q)
        ones = self.cfv(CF_ONE, Q, Q)
        neg1 = self.cfv(CF_NEG1, Q, Q)
        identb = self.cbv(CB_ID, Q, Q)
        nmT4 = self.cb[0:Q, CB_NMT4:CB_NMT4 + 512].re("p (a b) -> p a b", a=4, b=128)[:, :, 0:Q]
        sel = self.cf[0:Q, (CF_SEL128 if Q == 128 else CF_SEL32):(CF_SEL128 if Q == 128 else CF_SEL32) + 128]
        A, Bm, E, w, wT = W["A"], W["Bm"], W["E"], W["w"], W["wT"]
        DHS = 128 ** -0.5
        self.tt("dve", A[0:Q, :, 0:Q], ident.un(1).bc([Q, 4, Q]), ip[0:Q, c, :].un(2).bc([Q, 4, Q]), ALU.mult)
        self.tt("pool", Bm[0:Q, :, 0:Q], tri.un(1).bc([Q, 4, Q]), lf[0:Q, c, :].un(2).bc([Q, 4, Q]), ALU.mult)
        P2 = bk["P2"][0:Q, 0:4 * Q].re("p (a b) -> p a b", a=4, b=Q)
        self.mm(P2, ones, A[0:Q, :, 0:Q], start=True, stop=False)
        self.mm(P2, neg1, Bm[0:Q, :, 0:Q], start=False, stop=False)
        self.mm(P2, identb, nmT4, start=False, stop=True)
        self.mm(bk["ms"][0:Q, 0:4], tri, lf[0:Q, c, :])
        self.cp("dve", W["bsb"][0:Q, :], bk["ms"][0:Q, 0:4])
        self.red(W["rowmax"][0:Q, :], P2, ALU.max)
        self.tt("dve", W["mx"][0:Q, :], W["rowmax"][0:Q, :], W["mmb"][0:Q, :], ALU.max)
        self.ts("dve", W["negmx"][0:Q, :], W["mx"][0:Q, :], -1.0, ALU.mult)
        for h in range(4):
            self.act(E[0:Q, h, 0:Q], bk["P2"][0:Q, h * Q:(h + 1) * Q], AF.Exp, bias=W["negmx"][0:Q, h:h + 1])
        self.chk("C_m1")
        for h in range(4):
            self.mm(bk["pq"][:, h * Q:(h + 1) * Q], wqkv[:, 0, h, :], xcv[:, h, c0:c0 + Q])
            self.mm(bk["pk"][:, h * Q:(h + 1) * Q], wqkv[:, 1, h, :], xcv[:, h, c0:c0 + Q])
            self.mm(bk["pv"][0:Q, h * 128:(h + 1) * 128], xc[:, h, c0:c0 + Q], wqkv[:, 2, h, :])
            self.mm(bk["pkt"][0:Q, h * 128:(h + 1) * 128], xcv[:, h, c0:c0 + Q], wqkv[:, 1, h, :])
        self.cp("act", W["qT"][:, :, 0:Q], bk["pq"][:, 0:4 * Q].re("p (a b) -> p a b", a=4, b=Q))
        self.act(W["kT"][:, :, 0:Q], bk["pk"][:, 0:4 * Q].re("p (a b) -> p a b", a=4, b=Q), AF.Copy, scale=DHS)
        self.cp("dve", W["v"][0:Q], bk["pv"][0:Q, :].re("p (a b) -> p a b", a=4, b=128))
        self.act(W["ktok"][0:Q], bk["pkt"][0:Q, :].re("p (a b) -> p a b", a=4, b=128), AF.Copy, scale=DHS)
        for h in range(4):
            self.mm(bk["S"][0:Q, h * Q:(h + 1) * Q], W["qT"][:, h, 0:Q], W["kT"][:, h, 0:Q])
        for h in range(4):
            self.stt(w[0:Q, h, 0:Q], E[0:Q, h, 0:Q], 1.0, bk["S"][0:Q, h * Q:(h + 1) * Q], ALU.mult, ALU.mult,
                     accum=W["rsum"][0:Q, h:h + 1])
        for h in range(4):
            self.tr(bk["wTp"][0:Q, h * Q:(h + 1) * Q], w[0:Q, h, 0:Q], identb)
        self.cp("act", wT[0:Q, :, 0:Q], bk["wTp"][0:Q, 0:4 * Q].re("p (a b) -> p a b", a=4, b=Q))
        for h in range(4):
            self.mm(bk["num"][0:Q, h * 128:(h + 1) * 128], wT[0:Q, h, 0:Q], W["v"][0:Q, h, :])
        for h in range(4):
            ib = bk["pq"] if h < 2 else bk["pk"]
            self.mm(ib[0:Q, (h % 2) * 130:(h % 2) * 130 + 129], W["qT"][:, h, 0:Q], W["Cnb"][:, h, 0:129])
        self.chk("C_m2")
        self.tt("dve", W["t4"][0:Q, :], W["mmb"][0:Q, :], W["mx"][0:Q, :], ALU.subtract)
        self.act(W["g"][0:Q, :], W["t4"][0:Q, :], AF.Exp)
        tmp = W["tmp"]
        for h in range(4):
            ib = bk["pq"] if h < 2 else bk["pk"]
            self.ts("dve", tmp[0:Q, h, 0:129], ib[0:Q, (h % 2) * 130:(h % 2) * 130 + 129], W["g"][0:Q, h:h + 1], ALU.mult)
        self.tt("dve", W["hh"][0:Q], bk["num"][0:Q, :].re("p (a b) -> p a b", a=4, b=128), tmp[0:Q, :, 0:128], ALU.add)
        self.tt("dve", W["nq"][0:Q, :], W["rsum"][0:Q, :], tmp[0:Q, :, 128], ALU.add)
        self.tt("dve", W["t4"][0:Q, :], W["bsb"][0:Q, :], W["mx"][0:Q, :], ALU.add)
        self.act(W["emt"][0:Q, :], W["t4"][0:Q, :], AF.Exp, scale=-1.0)
        self.stt(W["nq"][0:Q, :], W["nq"][0:Q, :], -1.0, W["nq"][0:Q, :], ALU.mult, ALU.max)
        self.tt("dve", W["den"][0:Q, :], W["nq"][0:Q, :], W["emt"][0:Q, :], ALU.max)
        self.recip(W["den"][0:Q, :], W["den"][0:Q, :])
        self.tt("dve", W["hh"][0:Q], W["hh"][0:Q], W["den"][0:Q, :].un(2).bc([Q, 4, 128]), ALU.mult)
        for h in range(4):
            self.bnstats(W["bst"][0:Q, h, :], W["hh"][0:Q, h, :])
            self.bnaggr(W["mvv"][0:Q, h, :], W["bst"][0:Q, h, :])
        self.act(W["t4b"][0:Q, :], W["mvv"][0:Q, :, 1], AF.Ln, bias=EPS)
        self.act(W["rsd"][0:Q, :], W["t4b"][0:Q, :], AF.Exp, scale=-0.5)
        self.tt("dve", W["hn"][0:Q], W["hh"][0:Q], W["mvv"][0:Q, :, 0:1].bc([Q, 4, 128]), ALU.subtract)
        self.tt("dve", W["hn"][0:Q], W["hn"][0:Q], W["rsd"][0:Q, :].un(2).bc([Q, 4, 128]), ALU.mult)
        for h in range(4):
            self.tr(bk["S"][:, h * Q:(h + 1) * Q], W["hn"][0:Q, h, :], ident)
        for h in range(4):
            self.act(W["hnT"][:, h, 0:Q], bk["S"][:, h * Q:(h + 1) * Q], AF.Identity, scale=self.pp[:, PP_NCW + h:PP_NCW + h + 1])
        for h in range(4):
            self.stt(W["y1"][:, h, 0:Q], xcv[:, h, c0:c0 + Q], self.pp[:, PP_SKC + h:PP_SKC + h + 1], W["hnT"][:, h, 0:Q], ALU.mult, ALU.add)
        self.tt("dve", W["yT"][:, :, 0:Q], W["y1"][:, :, 0:Q], szT[:, :, c0:c0 + Q], ALU.mult)
        ti = c if c < 16 else 16
        self.P.dma("sp", self.ytd[8:12, :, c0:c0 + Q].rearrange("c p t -> p c t"), W["yT"].ap[:, :, 0:Q],
                   reads=[W["yT"]], writes=[self.ytt[2][ti]])
        self.chk("C_m3")
        self.cp("dve", W["vals"][0:Q, 0:4], W["t4"][0:Q, :])
        self.cp("dve", W["vals"][0:Q, 4:8], W["bsb"][0:Q, :])
        self.mm(bk["lb"].v, sel, W["vals"][0:Q, :])
        self.cp("dve", W["lastb"].v, bk["lb"].v)
        lb = W["lastb"]
        self.tt("dve", W["t4b"][0:Q, :], lb[0:Q, 4:8], W["bsb"][0:Q, :], ALU.subtract)
        self.tt("dve", W["t4b"][0:Q, :], W["t4b"][0:Q, :], ip[0:Q, c, :], ALU.add)
        self.tt("dve", W["t4b"][0:Q, :], W["t4b"][0:Q, :], lb[0:Q, 0:4], ALU.subtract)
        self.act(W["ws"][0:Q, :], W["t4b"][0:Q, :], AF.Exp)
        self.tt("dve", W["gl"].v, lb[:, 4:8], W["mmb"].v, ALU.add)
        self.tt("dve", W["gl"].v, W["gl"].v, lb[:, 0:4], ALU.subtract)
        self.act(W["gl"].v, W["gl"].v, AF.Exp)
        vw = W["vw"]
        self.tt("dve", vw[0:Q, :, 0:128], W["v"][0:Q], W["ws"][0:Q, :].un(2).bc([Q, 4, 128]), ALU.mult)
        self.cp("dve", vw[0:Q, :, 128], W["ws"][0:Q, :])
        for h in range(4):
            ib = bk["pv"] if h < 2 else bk["pkt"]
            self.mm(ib[:, (h % 2) * 130:(h % 2) * 130 + 129], W["ktok"][0:Q, h, :], vw[0:Q, h, 0:129])
        Cn = W["Cn"]
        for h in range(4):
            ib = bk["pv"] if h < 2 else bk["pkt"]
            self.stt(Cn[:, h, 0:129], Cn[:, h, 0:129], W["gl"][:, h:h + 1], ib[:, (h % 2) * 130:(h % 2) * 130 + 129], ALU.mult, ALU.add)
        self.cp("act", W["Cnb"].v, Cn.v)
        self.cp("dve", W["mmb"].v, lb[:, 0:4])

    def mixer_D(self, l):
        I, O = self.I, self.O
        mem = self.mem
        self.phase_begin()
        pp = self.pp
        sgT = mem.alloc([128, 4, T], BF16, "sgT")
        yd = mem.alloc([128, 4, T], F32, "yd")
        glus = [mem.alloc([128, 30 + LP], F32, f"glu{i}") for i in range(2)]
        glu_ss = [mem.alloc([128, NS, 30 + LS], F32, f"glu_s{i}") for i in range(2)]
        sig = [mem.alloc([128, 512], F32, f"sig{i}") for i in range(2)]
        cst = mem.alloc([30, NS, 512], F32, "cst")
        cdo = mem.alloc([32, 5, 512], F32, "cdo")
        self.dma("sp", cst.v, I["scd"][l].re("j k f -> k j f"))
        for g_ in glus:
            self.memset("dve", g_[:, 0:30], 0.0)
        get = self.fm_stream(l, [[FB_AD, FB_BD, FB_GD][k % 3] + k // 3 for k in range(12)])
        pb = [self.bankt(i) for i in range(4)]
        sb_ = self.bankt(6)
        tb_ = self.bankt(7)
        ident = self.cfv(CF_ID)
        for b in range(4):
            glu, glu_s = glus[b % 2], glu_ss[b % 2]
            for j in range(NS):
                self.tr(sb_[:, j * 32:j * 32 + 30], cst[0:30, j, b * 128:(b + 1) * 128], self.cfv(CF_ID, 30, 30))
            self.cp("dve", glu_s[:, :, 0:30], sb_[:, 0:128].re("p (a b) -> p a b", a=NS, b=32)[:, :, 0:30])

            def ev_a(g, c0, n, ps):
                if g < 4:
                    self.cp("act", glu[:, 30 + c0:30 + c0 + n], ps)
                else:
                    self.cp("act", glu_s[:, :, 30:30 + LS], ps.re("p (a b) -> p a b", a=NS, b=LS))

            def ev_b(g, c0, n, ps):
                s_ = sig[g % 2]
                self.act(s_[:, 0:n], ps, AF.Sigmoid)
                if g < 4:
                    self.tt("dve", glu[:, 30 + c0:30 + c0 + n], glu[:, 30 + c0:30 + c0 + n], s_[:, 0:n], ALU.mult)
                else:
                    self.tt("dve", glu_s[:, :, 30:30 + LS], glu_s[:, :, 30:30 + LS],
                            s_[:, 0:n].re("p (a b) -> p a b", a=NS, b=LS), ALU.mult)

            def ev_g(g, c0, n, ps, b=b):
                self.act(sgT[:, b, c0:c0 + n], ps, AF.Silu)
            self.fm_block(get(3 * b), pb, ev_a)
            self.fm_block(get(3 * b + 1), pb, ev_b)
            self.fm_block(get(3 * b + 2), pb, ev_g)
            self.tr(tb_[0:32, 0:128], glu[:, 30 + LP - 32:30 + LP], ident)
            for j in range(3):
                self.tr(tb_[0:32, (j + 1) * 128:(j + 2) * 128], glu_s[:, j, 30:30 + LS], ident)
            self.tr(sb_[0:32, 256:384], glu_s[:, 3, 30:30 + LS], ident)
            self.cp("dve", cdo[:, 0:4, b * 128:(b + 1) * 128], tb_[0:32, 0:512].re("p (a b) -> p a b", a=4, b=128))
            self.cp("dve", cdo[:, 4, b * 128:(b + 1) * 128], sb_[0:32, 256:384])
            w0 = PP_CDW + 31 * b
            ydp = yd[:, b, 0:LP]
            yds = yd[:, b, LP:T].re("p (a b) -> p a b", a=NS, b=LS)
            self.ts("dve", ydp, glu[:, 0:LP], pp[:, w0:w0 + 1], ALU.mult, pp[:, PP_CDB + b:PP_CDB + b + 1], ALU.add)
            self.ts("dve", yds, glu_s[:, :, 0:LS], pp[:, w0:w0 + 1], ALU.mult, pp[:, PP_CDB + b:PP_CDB + b + 1], ALU.add)
            for k in range(1, 31):
                self.stt(ydp, glu[:, k:k + LP], pp[:, w0 + k:w0 + k + 1], ydp, ALU.mult, ALU.add)
                self.stt(yds, glu_s[:, :, k:k + LS], pp[:, w0 + k:w0 + k + 1], yds, ALU.mult, ALU.add)
        self.dma("sp", O["cdp"][l], cdo[2:32, 0, :])
        for j in range(NS):
            self.dma("sp", O["cds"][l, j], cdo[2:32, 1 + j, :])
        self.P.barrier()
        if self.mixers.endswith("D"):
            off = self.uT_off // 2
            wpre = Tile(mem.t[0:128, off:off + KC * D].rearrange("p (k c) -> p k c", k=KC, c=D), "wout_pre")
            for q in range(4):
                self.dma("pool", wpre[:, 4 * q:4 * q + 4, :], I["wout"][l, :, 4 * q * D:(4 * q + 4) * D].re("p (k c) -> p k c", k=4, c=D))
            self.wout_pre = wpre
        mem.push()
        sq = [mem.alloc([128, 512], F32, f"sq{i}") for i in range(2)]
        mean = mem.alloc([128, 512], F32, "mean")
        rstd = mem.alloc([128, 512], F32, "rstd")
        t1 = [mem.alloc([128, 512], F32, f"t1{i}") for i in range(2)]
        yT = [mem.alloc([128, 4, 512], BF16, f"yTd{i}") for i in range(2)]
        ones = self.cfv(CF_ONE)
        bm_ = [self.bankt(0), self.bankt(1)]
        bq_ = [self.bankt(2), self.bankt(3)]
        for g, (c0, n) in enumerate(GROUPS):
            s = g % 2
            for b in range(4):
                self.mm(bm_[s][:, 0:n], ones, yd[:, b, c0:c0 + n], start=(b == 0), stop=(b == 3))
            for b in range(4):
                self.act(sq[b % 2][:, 0:n], yd[:, b, c0:c0 + n], AF.Square)
                self.mm(bq_[s][:, 0:n], ones, sq[b % 2][:, 0:n], start=(b == 0), stop=(b == 3))
            self.ts("dve", mean[:, 0:n], bm_[s][:, 0:n], 1.0 / 512, ALU.mult)
            self.tt("dve", rstd[:, 0:n], mean[:, 0:n], mean[:, 0:n], ALU.mult)
            self.stt(rstd[:, 0:n], bq_[s][:, 0:n], 1.0 / 512, rstd[:, 0:n], ALU.mult, ALU.subtract)
            self.act(rstd[:, 0:n], rstd[:, 0:n], AF.Ln, bias=EPS)
            self.act(rstd[:, 0:n], rstd[:, 0:n], AF.Exp, scale=-0.5)
            for b in range(4):
                t_ = t1[b % 2]
                self.tt("dve", t_[:, 0:n], yd[:, b, c0:c0 + n], mean[:, 0:n], ALU.subtract)
                self.tt("dve", t_[:, 0:n], t_[:, 0:n], rstd[:, 0:n], ALU.mult)
                self.act(t_[:, 0:n], t_[:, 0:n], AF.Silu, scale=pp[:, PP_LDG + b:PP_LDG + b + 1], bias=pp[:, PP_LDB + b:PP_LDB + b + 1])
                self.tt("pool", yT[s][:, b, 0:n], t_[:, 0:n], sgT[:, b, c0:c0 + n], ALU.mult)
            for ii in range(n // 128):
                i = c0 // 128 + ii
                self.P.dma("sp", self.ytd[12:16, :, i * 128:(i + 1) * 128].rearrange("c p t -> p c t"),
                           yT[s].ap[:, :, ii * 128:(ii + 1) * 128], reads=[yT[s]], writes=[self.ytt[3][i]])
        self.P.barrier()
        mem.pop()
        self.phase_end()


_CACHE = {}


def _prep_weights(inp):
    L = DEPTH
    f = np.float32
    w_mod = inp["w_mod"]
    wmod = np.ascontiguousarray(w_mod.reshape(L, KC, 128, 12, 512).transpose(0, 3, 2, 1, 4)).reshape(L, 12, 128, KC * 512)
    bmod = np.ascontiguousarray(np.broadcast_to(inp["b_mod"][:, None, :], (L, 5, 6144))).astype(f)
    w_in = inp["w_in"].reshape(L, KC, 128, 6160)
    wfm = np.empty((L, 40, 128, KC * 128), f)
    for b, c0 in enumerate(FM_COLS):
        wfm[:, b] = w_in[:, :, :, c0:c0 + 128].transpose(0, 2, 1, 3).reshape(L, 128, KC * 128)
    wtm = np.empty((L, 3, 128, KC * 512), f)
    for b, c0 in enumerate(TM_COLS):
        wtm[:, b] = w_in[:, :, :, c0:c0 + 512].transpose(0, 2, 1, 3).reshape(L, 128, KC * 512)
    sm = np.concatenate([w_in[..., 1536:1544], w_in[..., 4616:4624]], axis=-1)
    wsm = np.ascontiguousarray(sm.transpose(0, 2, 1, 3)).reshape(L, 128, KC * 16)
    wout = np.ascontiguousarray(inp["w_out"].reshape(L, KC, 128, D).transpose(0, 2, 1, 3)).reshape(L, 128, KC * D)
    wqkv = np.stack([inp["wq_c"], inp["wk_c"], inp["wv_c"]], axis=1)
    wqkv = np.ascontiguousarray(wqkv.transpose(0, 1, 3, 2, 4)).reshape(L, 3, 128, 512)
    pp = np.zeros((L, 128, NPP), f)
    caw = inp["conv_a_w"].reshape(L, 4, 8, 128)
    pp[:, :, PP_CAW:PP_CAW + 32] = caw.transpose(0, 3, 2, 1).reshape(L, 128, 32)
    pp[:, :, PP_CAB:PP_CAB + 8] = inp["conv_a_b"].reshape(L, 8, 128).transpose(0, 2, 1)
    ccw = inp["conv_c_w"].reshape(L, 4, 4, 128)
    pp[:, :, PP_CCW:PP_CCW + 16] = ccw.transpose(0, 3, 2, 1).reshape(L, 128, 16)
    pp[:, :, PP_CCB:PP_CCB + 4] = inp["conv_c_b"].reshape(L, 4, 128).transpose(0, 2, 1)
    cdw = inp["conv_d_w"].reshape(L, 31, 4, 128)
    pp[:, :, PP_CDW:PP_CDW + 124] = cdw.transpose(0, 3, 2, 1).reshape(L, 128, 124)
    pp[:, :, PP_CDB:PP_CDB + 4] = inp["conv_d_b"].reshape(L, 4, 128).transpose(0, 2, 1)
    pp[:, :, PP_LDG:PP_LDG + 4] = inp["ln_d_g"].reshape(L, 4, 128).transpose(0, 2, 1)
    pp[:, :, PP_LDB:PP_LDB + 4] = inp["ln_d_b"].reshape(L, 4, 128).transpose(0, 2, 1)
    pp[:, :, PP_NAW:PP_NAW + 4] = inp["norm_a_w"].reshape(L, 4, 128).transpose(0, 2, 1)
    pp[:, :, PP_NCW:PP_NCW + 4] = inp["norm_c_w"].reshape(L, 4, 128).transpose(0, 2, 1)
    pp[:, :, PP_SKC:PP_SKC + 4] = inp["skip_c"].reshape(L, 4, 128).transpose(0, 2, 1)
    rp = np.zeros((L, 128, NRP), f)
    rp[:, :, RP_LNG:RP_LNG + 2048] = inp["ln_g"][:, None, :]
    rp[:, :, RP_LNB:RP_LNB + 2048] = inp["ln_b"][:, None, :]
    rp[:, :, RP_DTB:RP_DTB + 8] = inp["dt_bias"][:, None, :]
    rp[:, :, RP_ALOG:RP_ALOG + 8] = inp["a_log"][:, None, :]
    rp[:, :, RP_DSK:RP_DSK + 8] = inp["d_skip"][:, None, :]
    rp[:, :, RP_IGB:RP_IGB + 4] = inp["ig_bias"][:, None, :]
    rp[:, :, RP_FGB:RP_FGB + 4] = inp["fg_bias"][:, None, :]
    cf, cb = make_consts()
    return dict(wmod=wmod, bmod=bmod, wfm=wfm, wtm=wtm, wsm=wsm, wout=wout, wqkv=wqkv, ppack=pp, rpack=rp, cf=cf, cb=cb)


def _core_inputs(inp, shared, c):
    f = np.float32
    p = c % 4
    s0 = 4 * c
    m = dict(shared)
    m["xp"] = np.ascontiguousarray(inp["x_prompt"][p])
    m["xs"] = np.ascontiguousarray(inp["x_sample"][s0:s0 + 4]).reshape(128, D)
    m["ck"] = np.ascontiguousarray(inp["cache_k"][:, s0:s0 + 4]).reshape(DEPTH, NS, PAST, 512)
    m["cv"] = np.ascontiguousarray(inp["cache_v"][:, s0:s0 + 4]).reshape(DEPTH, NS, PAST, 512)
    m["sca"] = np.ascontiguousarray(inp["state_conv_a"][:, s0:s0 + 4]).reshape(DEPTH, NS * 3, 1024)
    m["sssm"] = np.ascontiguousarray(inp["state_ssm"][:, s0:s0 + 4]).reshape(DEPTH, NS, 512, 128)
    m["scc"] = np.ascontiguousarray(inp["state_conv_c"][:, s0:s0 + 4]).reshape(DEPTH, NS * 3, 512)
    m["smc"] = np.ascontiguousarray(inp["state_mlstm_c"][:, s0:s0 + 4])
    m["smn"] = np.ascontiguousarray(inp["state_mlstm_n"][:, s0:s0 + 4])
    m["smm"] = np.ascontiguousarray(inp["state_mlstm_m"][:, s0:s0 + 4])
    m["scd"] = np.ascontiguousarray(inp["state_conv_d"][:, s0:s0 + 4])
    call = np.concatenate([inp["c_prompt"][p:p + 1], inp["c_sample"][s0:s0 + 4]], axis=0)
    m["cT"] = np.ascontiguousarray(call.reshape(5, KC, 128).transpose(2, 1, 0)).reshape(128, KC * 5).astype(f)
    return m


def build_nc(nlayers=DEPTH, mixers="ABCD", stop=None):
    key = (nlayers, mixers)
    if key not in _CACHE:
        b = Builder(nlayers, mixers, stop)
        with b.stack:
            nc = b.build()
        _CACHE[key] = (nc, b.P.stats, b.mem.peak)
    return _CACHE[key]


def run(inp, nlayers=DEPTH, mixers="ABCD", ncores=8, stop=None, core0=0, trace=False):
    nc, stats, peak = build_nc(nlayers, mixers, stop)
    L = nlayers
    shared = _prep_weights(inp)
    for k in list(shared.keys()):
        if k not in ("cf", "cb"):
            shared[k] = np.ascontiguousarray(shared[k][:L])
    in_maps = []
    for c in range(ncores):
        m = _core_inputs(inp, shared, c)
        for k in ("ck", "cv", "sca", "sssm", "scc", "smc", "smn", "smm", "scd"):
            m[k] = np.ascontiguousarray(m[k][:L])
        in_maps.append(m)
    if trace:
        res = run_bass_kernel_spmd(nc, in_maps, core_ids=[core0 + i for i in range(ncores)], trace=True)
        print("EXEC_NS", getattr(res, "exec_time_ns", None))
    else:
        res = run_bass_kernel_spmd(nc, in_maps, core_ids=[core0 + i for i in range(ncores)])
    R = res.results
    f = np.float32
    npc = min(4, ncores)

    y_prompt = np.stack([R[c]["yp"] for c in range(npc)], axis=0)
    y_sample = np.concatenate([R[c]["ys"].reshape(NS, LS, D) for c in range(ncores)], axis=0)
    nkp = np.stack([R[c]["nkp"] for c in range(npc)], axis=1).reshape(L, npc, LP, 8, 64)
    nvp = np.stack([R[c]["nvp"] for c in range(npc)], axis=1).reshape(L, npc, LP, 8, 64)
    cap = np.stack([R[c]["cap"] for c in range(npc)], axis=1)
    ssmp = np.stack([R[c]["ssmp"] for c in range(npc)], axis=1).reshape(L, npc, 8, 64, 128)
    ccp = np.stack([R[c]["ccp"] for c in range(npc)], axis=1)
    mcp = np.stack([R[c]["mcp"] for c in range(npc)], axis=1)
    mnp = np.stack([R[c]["mnp"] for c in range(npc)], axis=1)
    mmp = np.stack([R[c]["mmp"] for c in range(npc)], axis=1)
    cdp = np.stack([R[c]["cdp"] for c in range(npc)], axis=1)
    nks = np.concatenate([R[c]["nks"].reshape(L, NS, LS, 8, 64) for c in range(ncores)], axis=1)
    nvs = np.concatenate([R[c]["nvs"].reshape(L, NS, LS, 8, 64) for c in range(ncores)], axis=1)
    cas = np.concatenate([R[c]["cas"] for c in range(ncores)], axis=1)
    ssms = np.concatenate([R[c]["ssms"].reshape(L, NS, 8, 64, 128) for c in range(ncores)], axis=1)
    ccs = np.concatenate([R[c]["ccs"] for c in range(ncores)], axis=1)
    mcs = np.concatenate([R[c]["mcs"] for c in range(ncores)], axis=1)
    mns = np.concatenate([R[c]["mns"] for c in range(ncores)], axis=1)
    mms = np.concatenate([R[c]["mms"] for c in range(ncores)], axis=1)
    cds = np.concatenate([R[c]["cds"] for c in range(ncores)], axis=1)
    outs = (y_prompt, y_sample, nkp, nvp, cap, ssmp, ccp, mcp, mnp, mmp, cdp,
            nks, nvs, cas, ssms, ccs, mcs, mns, mms, cds)
    return tuple(np.ascontiguousarray(o, dtype=f) for o in outs)


def kernel(**inputs):
    inp = {k: np.asarray(v) for k, v in inputs.items()}
    return run(inp)
```
